# Optimizing a Trainium2 kernel written in Bass

```python
import jax, jax.numpy as jnp
from jax import lax
import numpy as np

D_MODEL = 1024
BATCH = 2
SEQ = 8192
DEPTH = 2

HEAD_DIM = 64
RET_HEADS = 8
DIL_HEADS = 8
SB_HEADS = D_MODEL // HEAD_DIM
RET_WIDTH = RET_HEADS * HEAD_DIM
DIL_WIDTH = DIL_HEADS * HEAD_DIM
SB_WIDTH = SB_HEADS * HEAD_DIM
HYB_IN = 4 * RET_WIDTH + 3 * DIL_WIDTH
D_FF = 2816
BLOCK = 128
RET_CHUNK = 128
RET_ROPE_THETA = 10000.0
ROPE_THETA = 500000.0
ROPE_DIM = HEAD_DIM // 4
DILATED_PATTERNS = ((128, 1), (512, 4), (2048, 16))
NORM_EPS = 1e-6
GN_EPS = 1e-5
N_EVEN = (DEPTH + 1) // 2
N_ODD = DEPTH // 2

kernel_name = "hybrid_retention_dilated_stickbreaking_macaron"


def rms_norm(x, g):
    xf = x.astype(jnp.float32)
    y = xf * lax.rsqrt(jnp.mean(xf * xf, axis=-1, keepdims=True) + NORM_EPS)
    return (y * g.astype(jnp.float32)).astype(x.dtype)


def swiglu_ffn(x, g, w_in, w_out):
    h = rms_norm(x, g)
    gate, up = jnp.split(h @ w_in, 2, axis=-1)
    return (jax.nn.silu(gate) * up) @ w_out


def split_heads(t, n_heads):
    b, s, _ = t.shape
    return t.reshape(b, s, n_heads, HEAD_DIM).transpose(0, 2, 1, 3)


def merge_heads(t):
    b, h, s, dh = t.shape
    return t.transpose(0, 2, 1, 3).reshape(b, s, h * dh)


def rotary(x, rot_dim, theta):
    s = x.shape[-2]
    half = rot_dim // 2
    inv_freq = 1.0 / (theta ** (jnp.arange(half, dtype=jnp.float32) / half))
    ang = jnp.arange(s, dtype=jnp.float32)[:, None] * inv_freq[None, :]
    cos, sin = jnp.cos(ang), jnp.sin(ang)
    xr = x[..., :rot_dim].astype(jnp.float32)
    x1, x2 = xr[..., :half], xr[..., half:]
    rot = jnp.concatenate([x1 * cos - x2 * sin, x2 * cos + x1 * sin], axis=-1).astype(x.dtype)
    return jnp.concatenate([rot, x[..., rot_dim:]], axis=-1)


def retention_chunkwise(q, k, v):
    b, h, s, dk = q.shape
    c = RET_CHUNK
    nc = s // c
    out_dtype = v.dtype
    q, k, v = (t.astype(jnp.float32) for t in (q, k, v))
    k = k * (dk ** -0.5)
    log_g = jnp.log(1.0 - 2.0 ** (-5.0 - jnp.arange(h, dtype=jnp.float32)))
    i = jnp.arange(c, dtype=jnp.float32)
    diff = i[:, None] - i[None, :]
    decay_in = jnp.where(diff >= 0, jnp.exp(jnp.maximum(diff, 0.0)[None] * log_g[:, None, None]), 0.0)
    qc = q.reshape(b, h, nc, c, dk)
    kc = k.reshape(b, h, nc, c, dk)
    vc = v.reshape(b, h, nc, c, -1)
    scores = jnp.einsum('bhnid,bhnjd->bhnij', qc, kc) * decay_in[None, :, None]
    inner = jnp.einsum('bhnij,bhnje->bhnie', scores, vc)
    k_decay = jnp.exp((c - 1 - i)[None, :] * log_g[:, None])
    kv = jnp.einsum('bhnjd,bhnje->nbhde', kc * k_decay[None, :, None, :, None], vc)
    chunk_decay = jnp.exp(c * log_g)[None, :, None, None]

    def step(state, kv_n):
        return state * chunk_decay + kv_n, state

    _, state_prev = lax.scan(step, jnp.zeros_like(kv[0]), kv)
    q_decay = jnp.exp((i + 1)[None, :] * log_g[:, None])
    cross = jnp.einsum('bhnid,nbhde->bhnie', qc * q_decay[None, :, None, :, None], state_prev)
    return (inner + cross).reshape(b, h, s, -1).astype(out_dtype)


def head_group_norm(o, g):
    of = o.astype(jnp.float32)
    mu = jnp.mean(of, axis=-1, keepdims=True)
    var = jnp.mean(jnp.square(of - mu), axis=-1, keepdims=True)
    y = (of - mu) * lax.rsqrt(var + GN_EPS)
    y = y * g.astype(jnp.float32).reshape(o.shape[1], 1, o.shape[3])
    return y.astype(o.dtype)


def band_attention(q, k, v, n_back):
    *lead, l_len, dh = q.shape
    nb = l_len // BLOCK
    qb = q.reshape(*lead, nb, BLOCK, dh)

    def with_prev(t):
        tb = t.reshape(*lead, nb, BLOCK, dh)
        prev = jnp.concatenate([jnp.zeros_like(tb[..., :1, :, :]), tb[..., :-1, :, :]], axis=-3)
        return jnp.concatenate([prev, tb], axis=-2)

    kb, vb = with_prev(k), with_prev(v)
    s = jnp.einsum('...nqd,...nkd->...nqk', qb, kb).astype(jnp.float32) * (dh ** -0.5)
    dist = (jnp.arange(BLOCK)[:, None] + BLOCK) - jnp.arange(2 * BLOCK)[None, :]
    in_band = (dist >= 0) & (dist <= n_back)
    key_pos = jnp.arange(nb)[:, None] * BLOCK - BLOCK + jnp.arange(2 * BLOCK)[None, :]
    mask = in_band[None] & (key_pos >= 0)[:, None, :]
    s = jnp.where(mask, s, -jnp.inf)
    m = jnp.max(s, axis=-1, keepdims=True)
    p = jnp.exp(s - m)
    den = jnp.sum(p, axis=-1, keepdims=True)
    o = jnp.einsum('...nqk,...nkd->...nqd', p, vb.astype(jnp.float32)) / den
    lse = (m + jnp.log(den))[..., 0]
    return o.reshape(*lead, l_len, dh), lse.reshape(*lead, l_len)


def dilated_attention(q, k, v):
    b, h, s, dh = q.shape
    outs, lses = [], []
    for window, dil in DILATED_PATTERNS:
        mult = dil * BLOCK
        s_pad = -(-s // mult) * mult
        pad = ((0, 0), (0, 0), (0, s_pad - s), (0, 0))

        def by_residue(t):
            return jnp.pad(t, pad).reshape(b, h, s_pad // dil, dil, dh).transpose(0, 1, 3, 2, 4)

        o, lse = band_attention(by_residue(q), by_residue(k), by_residue(v), window // dil)
        outs.append(o.transpose(0, 1, 3, 2, 4).reshape(b, h, s_pad, dh)[:, :, :s])
        lses.append(lse.transpose(0, 1, 3, 2).reshape(b, h, s_pad)[:, :, :s])
    w = jax.nn.softmax(jnp.stack(lses, axis=0), axis=0)
    o = jnp.sum(w[..., None] * jnp.stack(outs, axis=0), axis=0)
    return o.astype(q.dtype)


def stick_breaking_attention(q, k, v):
    b, h, s, dh = q.shape
    nb = s // BLOCK
    kf, vf = k.astype(jnp.float32), v.astype(jnp.float32)
    key_pos = jnp.arange(s)

    def one_block(args):
        q_blk, blk = args
        z = jnp.einsum('bhqd,bhkd->bhqk', q_blk.astype(jnp.float32), kf) * (dh ** -0.5)
        q_pos = blk * BLOCK + jnp.arange(BLOCK)
        causal = key_pos[None, :] < q_pos[:, None]
        log_stay = jnp.where(causal, jax.nn.log_sigmoid(-z), 0.0)
        later = lax.cumsum(log_stay, axis=3, reverse=True) - log_stay
        a = jnp.where(causal, jnp.exp(jax.nn.log_sigmoid(z) + later), 0.0)
        return jnp.einsum('bhqk,bhkd->bhqd', a, vf)

    q_blocks = q.reshape(b, h, nb, BLOCK, dh).transpose(2, 0, 1, 3, 4)
    o = lax.map(one_block, (q_blocks, jnp.arange(nb)))
    return o.transpose(1, 2, 0, 3, 4).reshape(b, h, s, dh).astype(q.dtype)


def retention_dilated_mixer(x, norm_g, w_in, ret_gn, w_out):
    h = rms_norm(x, norm_g)
    proj = h @ w_in
    cuts = [RET_WIDTH, 2 * RET_WIDTH, 3 * RET_WIDTH, 4 * RET_WIDTH,
            4 * RET_WIDTH + DIL_WIDTH, 4 * RET_WIDTH + 2 * DIL_WIDTH]
    rq, rk, rv, rg, dq, dk, dv = jnp.split(proj, cuts, axis=-1)
    rq = rotary(split_heads(rq, RET_HEADS), HEAD_DIM, RET_ROPE_THETA)
    rk = rotary(split_heads(rk, RET_HEADS), HEAD_DIM, RET_ROPE_THETA)
    ret = retention_chunkwise(rq, rk, split_heads(rv, RET_HEADS))
    ret = merge_heads(head_group_norm(ret, ret_gn)) * jax.nn.silu(rg)
    dq = rotary(split_heads(dq, DIL_HEADS), ROPE_DIM, ROPE_THETA)
    dk = rotary(split_heads(dk, DIL_HEADS), ROPE_DIM, ROPE_THETA)
    dil = merge_heads(dilated_attention(dq, dk, split_heads(dv, DIL_HEADS)))
    return jnp.concatenate([ret, dil], axis=-1) @ w_out


def stick_breaking_mixer(x, norm_g, w_in, w_out):
    h = rms_norm(x, norm_g)
    q, k, v = jnp.split(h @ w_in, 3, axis=-1)
    o = stick_breaking_attention(split_heads(q, SB_HEADS), split_heads(k, SB_HEADS), split_heads(v, SB_HEADS))
    return merge_heads(o) @ w_out


def setup_inputs(seed: int = 0) -> dict:
    key = jax.random.key(seed)
    ks = jax.random.split(key, 16)
    f32 = jnp.float32

    def gain(k, shape):
        return 1.0 + 0.02 * jax.random.normal(k, shape, f32)

    def dense(k, shape, fan_in):
        return jax.random.normal(k, shape, f32) * (fan_in ** -0.5)

    return {
        'x': jax.random.normal(ks[0], (BATCH, SEQ, D_MODEL), f32),
        'ffn1_norm': gain(ks[1], (DEPTH, D_MODEL)),
        'ffn1_w_in': dense(ks[2], (DEPTH, D_MODEL, 2 * D_FF), D_MODEL),
        'ffn1_w_out': dense(ks[3], (DEPTH, D_FF, D_MODEL), D_FF),
        'mix_norm': gain(ks[4], (DEPTH, D_MODEL)),
        'ffn2_norm': gain(ks[5], (DEPTH, D_MODEL)),
        'ffn2_w_in': dense(ks[6], (DEPTH, D_MODEL, 2 * D_FF), D_MODEL),
        'ffn2_w_out': dense(ks[7], (DEPTH, D_FF, D_MODEL), D_FF),
        'hyb_w_in': dense(ks[8], (N_EVEN, D_MODEL, HYB_IN), D_MODEL),
        'ret_gn': gain(ks[9], (N_EVEN, RET_WIDTH)),
        'hyb_w_out': dense(ks[10], (N_EVEN, RET_WIDTH + DIL_WIDTH, D_MODEL), RET_WIDTH + DIL_WIDTH),
        'sb_w_in': dense(ks[11], (N_ODD, D_MODEL, 3 * SB_WIDTH), D_MODEL),
        'sb_w_out': dense(ks[12], (N_ODD, SB_WIDTH, D_MODEL), SB_WIDTH),
        'final_norm': gain(ks[13], (D_MODEL,)),
    }


def reference(x, ffn1_norm, ffn1_w_in, ffn1_w_out, mix_norm, ffn2_norm, ffn2_w_in, ffn2_w_out,
              hyb_w_in, ret_gn, hyb_w_out, sb_w_in, sb_w_out, final_norm):
    for layer in range(DEPTH):
        x = x + 0.5 * swiglu_ffn(x, ffn1_norm[layer], ffn1_w_in[layer], ffn1_w_out[layer])
        if layer % 2 == 0:
            e = layer // 2
            x = x + retention_dilated_mixer(x, mix_norm[layer], hyb_w_in[e], ret_gn[e], hyb_w_out[e])
        else:
            o = layer // 2
            x = x + stick_breaking_mixer(x, mix_norm[layer], sb_w_in[o], sb_w_out[o])
        x = x + 0.5 * swiglu_ffn(x, ffn2_norm[layer], ffn2_w_in[layer], ffn2_w_out[layer])
    return rms_norm(x, final_norm)
```

```python
import contextlib
import numpy as np
import ml_dtypes
import concourse.bass as bass
import concourse.mybir as mybir
from concourse.bass_utils import run_bass_kernel_spmd

F32 = mybir.dt.float32
BF16 = mybir.dt.bfloat16
AF = mybir.ActivationFunctionType
ALU = mybir.AluOpType
AX = mybir.AxisListType
NPBF = ml_dtypes.bfloat16

D = 1024
S = 8192
B = 2
NCORE = 8
NT = 2048
DC = 8
FF = 2816
FC = 22
HYB_IN = 3584
EPS = 1e-6
GN_EPS = 1e-5
GT = 1024
DBG = {}


class Prog:
    ENGS = ("sync", "scalar", "vector", "gpsimd", "tensor")

    def __init__(self, nc, stack, name):
        self.nc = nc
        self.stack = stack
        self.name = name
        self.ops = {e: [] for e in self.ENGS}
        self.esem = {}
        self.ecount = {e: 0 for e in self.ENGS}
        self.waited = {e: {} for e in self.ENGS}
        self.dcount = {}
        self.nsem = 0

    def new_sem(self, tag):
        self.nsem += 1
        return self.stack.enter_context(self.nc.semaphore(f"{self.name}_{tag}_{self.nsem}"))

    def dma_sems(self, tag, n):
        sems = [self.new_sem(tag) for _ in range(n)]
        for s in sems:
            self.dcount[id(s)] = [s, 0]
        return sems

    def _filter_waits(self, eng, waits):
        ws = []
        best = {}
        for t in waits:
            if t is None:
                continue
            if id(t[0]) not in best or best[id(t[0])][1] < t[1]:
                best[id(t[0])] = t
        for t in best.values():
            sem, val = t
            if eng in self.esem and sem is self.esem[eng]:
                continue
            key = id(sem)
            if self.waited[eng].get(key, 0) >= val:
                continue
            self.waited[eng][key] = val
            ws.append((sem, val))
        return ws

    def op(self, eng, fn, waits=(), sig=False, free=False, hard=()):
        ws = self._filter_waits(eng, waits)
        tok = None
        strict = DBG.get("strict", True) and eng in ("scalar", "vector", "gpsimd")
        if DBG.get("strict_all") and eng == "tensor":
            strict = True
        if strict:
            sig = True
            if self.ecount[eng] > 0 and not free:
                ws.append((self.esem[eng], self.ecount[eng]))
            else:
                for t in hard:
                    if t is not None:
                        ws.append(t)
        if sig:
            if eng not in self.esem:
                self.esem[eng] = self.new_sem("e" + eng)
            self.ecount[eng] += 1
            tok = (self.esem[eng], self.ecount[eng])
        self.ops[eng].append((fn, ws, tok, 1))
        return tok

    def pe(self, fn, waits=(), sig=False, free=False, hard=()):
        return self.op("tensor", fn, waits, sig, free, hard)

    def act(self, fn, waits=(), sig=False, free=False, hard=()):
        return self.op("scalar", fn, waits, sig, free, hard)

    def dve(self, fn, waits=(), sig=False, free=False, hard=()):
        return self.op("vector", fn, waits, sig, free, hard)

    def pool(self, fn, waits=(), sig=False, free=False, hard=()):
        return self.op("gpsimd", fn, waits, sig, free, hard)

    def dma(self, queue, out, in_, sem, waits=()):
        ws = self._filter_waits(queue, waits)
        ent = self.dcount[id(sem)]
        ent[1] += 16
        tok = (sem, ent[1])
        self.ops[queue].append((lambda e, o=out, i=in_: e.dma_start(out=o, in_=i), ws, tok, 16))
        return tok

    def emit(self, block):
        for eng in self.ENGS:
            ops = self.ops[eng]
            if not ops:
                continue

            def body(e, ops=ops):
                for fn, ws, tok, inc in ops:
                    for sem, val in ws:
                        e.wait_ge(sem, val)
                    ins = fn(e)
                    if tok is not None:
                        ins.then_inc(tok[0], inc)

            getattr(block, eng)(body)


class Ring:
    def __init__(self, tiles):
        self.tiles = tiles
        self.n = len(tiles)
        self.i = 0
        self.free = [[] for _ in tiles]

    def next(self):
        k = self.i % self.n
        self.i += 1
        toks = self.free[k]
        self.free[k] = []
        return k, self.tiles[k], toks

    def release(self, k, *toks):
        self.free[k].extend(t for t in toks if t is not None)


def sb(stack, nc, name, shape, dt):
    return stack.enter_context(nc.sbuf_tensor(name, list(shape), dt))


def psb(stack, nc, name, shape=(128, 512), dt=F32):
    return stack.enter_context(nc.psum_tensor(name, list(shape), dt))


class Ctx:
    pass


class NormUnit:
    def __init__(self, P, st, nc, name, gT, ones_bf):
        self.P = P
        self.gT = gT
        self.ones = ones_bf
        self.sq = sb(st, nc, name + "_sq", [128, DC, 512], BF16)
        self.tmp = sb(st, nc, name + "_tmp", [128, 512], F32)
        self.rstd = sb(st, nc, name + "_rstd", [128, 512], F32)
        self.ss = psb(st, nc, name + "_ss")
        self.t_pe = None
        self.t_sqrt = None
        self.t_rec = None

    def run(self, xT, t0, gcol, out_fn, x_ready=(), out_free=(), final=False):
        P = self.P
        sq, tmp, rstd, ss = self.sq, self.tmp, self.rstd, self.ss
        t_sq = None
        for c in range(DC):
            t_sq = P.act(lambda e, c=c: e.activation(out=sq[:, c, :], in_=xT[:, c, t0:t0 + 512], func=AF.Square),
                         waits=list(x_ready) + [self.t_pe], sig=(c == DC - 1))
        t_mm = None
        for c in range(DC):
            t_mm = P.pe(lambda e, c=c: e.matmul(ss[:, :], self.ones[:, :], sq[:, c, :], start=(c == 0), stop=(c == DC - 1)),
                        waits=[t_sq, self.t_sqrt], sig=(c == DC - 1))
        self.t_pe = t_mm
        t_s = P.act(lambda e: e.activation(out=tmp[:, :], in_=ss[:, :], func=AF.Sqrt, bias=EPS_TILE[0][:, 0:1], scale=1.0 / D),
                    waits=[t_mm, self.t_rec], sig=True)
        self.t_sqrt = t_s
        t_r = P.dve(lambda e: e.reciprocal(out=rstd[:, :], in_=tmp[:, :]), waits=[t_s], sig=True)
        self.t_rec = t_r
        t_h = None
        for c in range(DC):
            t_h = P.dve(lambda e, c=c: e.scalar_tensor_tensor(out=out_fn(c), in0=xT[:, c, t0:t0 + 512],
                                                              scalar=self.gT[:, gcol + c:gcol + c + 1],
                                                              in1=rstd[:, :], op0=ALU.mult, op1=ALU.mult),
                        waits=list(x_ready) + list(out_free), sig=(c == DC - 1))
        return t_h


EPS_TILE = [None]


def block_consts(nc, st, name):
    ones = sb(st, nc, name + "_ones", [128, 128], BF16)
    eps = sb(st, nc, name + "_eps", [128, 2], F32)
    with nc.Block(name + "_c") as blk:
        @blk.vector
        def _(v):
            v.memset(ones[:, :], 1.0)
            v.memset(eps[:, 0:1], EPS)
            v.memset(eps[:, 1:2], GN_EPS)
    EPS_TILE[0] = eps
    return ones, eps


def block_load(nc, name, pairs, queue="sync"):
    with contextlib.ExitStack() as st:
        P = Prog(nc, st, name)
        sem = P.dma_sems("ld", 1)[0]
        blk = st.enter_context(nc.Block(name))
        tok = None
        for o, i in pairs:
            tok = P.dma(queue, o, i, sem)
        P.op(queue, lambda e: e.nop(), waits=[tok])
        P.emit(blk)


def block_ffn(nc, cx, name, gcol, w_in, w_out):
    xT = cx.xT
    with contextlib.ExitStack() as st:
        P = Prog(nc, st, name)
        nu = NormUnit(P, st, nc, name, cx.gT, cx.ones)
        hT = sb(st, nc, name + "_hT", [128, DC, GT], BF16)
        actT = sb(st, nc, name + "_actT", [128, FC, GT], BF16)
        wgu = Ring([sb(st, nc, f"{name}_wgu{k}", [128, 2, DC, 128], BF16) for k in range(3)])
        wo = Ring([sb(st, nc, f"{name}_wo{k}", [128, FC, 128], BF16) for k in range(2)])
        sg = Ring([sb(st, nc, f"{name}_sg{k}", [128, 512], F32) for k in range(2)])
        pg = Ring([psb(st, nc, f"{name}_pg{k}") for k in range(2)])
        pu = Ring([psb(st, nc, f"{name}_pu{k}") for k in range(2)])
        py = Ring([psb(st, nc, f"{name}_py{k}") for k in range(2)])
        wsem = P.dma_sems("w", 3)
        osem = P.dma_sems("o", 2)
        blk = st.enter_context(nc.Block(name))
        w_in_v = w_in.rearrange("(c p) f -> p c f", p=128)
        w_out_v = w_out.rearrange("(j p) d -> p j d", p=128)
        ntile = GT // 512
        h_free = []
        a_free = []
        for gi in range(NT // GT):
            g0 = gi * GT
            t_h = []
            for t in range(ntile):
                th = nu.run(xT, g0 + t * 512, gcol, lambda c, t=t: hT[:, c, t * 512:(t + 1) * 512], out_free=h_free)
                t_h.append(th)
            h_free = []
            t_act_last = None
            for j in range(FC):
                k, wt, fr = wgu.next()
                P.dma("gpsimd", wt[:, 0, :, :], w_in_v[:, :, j * 128:(j + 1) * 128], wsem[k], waits=fr)
                t_w = P.dma("gpsimd", wt[:, 1, :, :], w_in_v[:, :, FF + j * 128:FF + (j + 1) * 128], wsem[k])
                t_pe_last = None
                for t in range(ntile):
                    kg, pgt, frg = pg.next()
                    ku, put, fru = pu.next()
                    ks, sgt, frs = sg.next()
                    tg = None
                    for c in range(DC):
                        tg = P.pe(lambda e, c=c, wt=wt, pgt=pgt, t=t: e.matmul(pgt[:, :], wt[:, 0, c, :], hT[:, c, t * 512:(t + 1) * 512],
                                                                               start=(c == 0), stop=(c == DC - 1)),
                                  waits=[t_w, t_h[t]] + frg, sig=(c == DC - 1))
                    tu = None
                    for c in range(DC):
                        tu = P.pe(lambda e, c=c, wt=wt, put=put, t=t: e.matmul(put[:, :], wt[:, 1, c, :], hT[:, c, t * 512:(t + 1) * 512],
                                                                               start=(c == 0), stop=(c == DC - 1)),
                                  waits=fru, sig=(c == DC - 1))
                    t_pe_last = tu
                    ts = P.act(lambda e, sgt=sgt, pgt=pgt: e.activation(out=sgt[:, :], in_=pgt[:, :], func=AF.Silu),
                               waits=[tg] + frs, sig=True)
                    pg.release(kg, ts)
                    ta = P.dve(lambda e, sgt=sgt, put=put, j=j, t=t: e.tensor_tensor(out=actT[:, j, t * 512:(t + 1) * 512], in0=sgt[:, :], in1=put[:, :],
                                                                                   op=ALU.mult),
                               waits=[ts, tu] + a_free, sig=True)
                    pu.release(ku, ta)
                    sg.release(ks, ta)
                    t_act_last = ta
                wgu.release(k, t_pe_last)
                if j == FC - 1:
                    h_free = [t_pe_last]
            a_free = []
            for c in range(DC):
                k, wt, fr = wo.next()
                t_w = P.dma("gpsimd", wt[:, :, :], w_out_v[:, :, c * 128:(c + 1) * 128], osem[k], waits=fr)
                t_pe_last = None
                for t in range(ntile):
                    ky, pyt, fry = py.next()
                    tp = None
                    for j in range(FC):
                        tp = P.pe(lambda e, j=j, wt=wt, pyt=pyt, t=t: e.matmul(pyt[:, :], wt[:, j, :], actT[:, j, t * 512:(t + 1) * 512],
                                                                               start=(j == 0), stop=(j == FC - 1)),
                                  waits=[t_w, t_act_last] + fry, sig=(j == FC - 1))
                    t_pe_last = tp
                    tx = P.dve(lambda e, pyt=pyt, c=c, t=t, g0=g0: e.scalar_tensor_tensor(out=xT[:, c, g0 + t * 512:g0 + (t + 1) * 512], in0=pyt[:, :], scalar=0.5,
                                                                                  in1=xT[:, c, g0 + t * 512:g0 + (t + 1) * 512], op0=ALU.mult, op1=ALU.add),
                               waits=[tp], sig=True)
                    py.release(ky, tx)
                wo.release(k, t_pe_last)
                if c == DC - 1:
                    a_free = [t_pe_last]
        P.emit(blk)


def block_outproj(nc, cx, name, mix_dram, w_out):
    xT = cx.xT
    with contextlib.ExitStack() as st:
        P = Prog(nc, st, name)
        mT = sb(st, nc, name + "_mT", [128, DC, NT], BF16)
        wr = Ring([sb(st, nc, f"{name}_w{k}", [128, DC, 128], BF16) for k in range(2)])
        py = Ring([psb(st, nc, f"{name}_py{k}") for k in range(2)])
        msem = P.dma_sems("m", 1)[0]
        wsem = P.dma_sems("w", 2)
        blk = st.enter_context(nc.Block(name))
        t_m = None
        mv = mix_dram.rearrange("(c p) t -> p c t", p=128)
        for c in range(DC):
            t_m = P.dma("sync", mT[:, c, :], mv[:, c, :], msem)
        w_v = w_out.rearrange("(c p) d -> p c d", p=128)
        for oc in range(DC):
            k, wt, fr = wr.next()
            t_w = P.dma("gpsimd", wt[:, :, :], w_v[:, :, oc * 128:(oc + 1) * 128], wsem[k], waits=fr)
            tl = None
            for t in range(NT // 512):
                ky, pyt, fry = py.next()
                tp = None
                for c in range(DC):
                    tp = P.pe(lambda e, c=c, wt=wt, pyt=pyt, t=t: e.matmul(pyt[:, :], wt[:, c, :], mT[:, c, t * 512:(t + 1) * 512],
                                                                           start=(c == 0), stop=(c == DC - 1)),
                              waits=[t_w, t_m] + fry, sig=(c == DC - 1))
                tl = tp
                tx = P.dve(lambda e, pyt=pyt, oc=oc, t=t: e.tensor_tensor(out=xT[:, oc, t * 512:(t + 1) * 512], in0=pyt[:, :],
                                                                         in1=xT[:, oc, t * 512:(t + 1) * 512], op=ALU.add),
                           waits=[tp], sig=True)
                py.release(ky, tx)
            wr.release(k, tl)
        P.emit(blk)


def block_proj(nc, cx, name, gcol, fm_specs, tm_specs, tabs=None):
    xT = cx.xT
    with contextlib.ExitStack() as st:
        P = Prog(nc, st, name)
        nu = NormUnit(P, st, nc, name, cx.gT, cx.ones)
        hT = sb(st, nc, name + "_hT", [128, DC, NT], BF16)
        wr = Ring([sb(st, nc, f"{name}_w{k}", [128, 2, DC, 128], BF16) for k in range(2)])
        wsem = P.dma_sems("w", 2)
        p1 = Ring([psb(st, nc, f"{name}_p1{k}") for k in range(2)])
        p2 = Ring([psb(st, nc, f"{name}_p2{k}") for k in range(2)])
        tabt = []
        tsem = P.dma_sems("t", 1)[0]
        blk = st.enter_context(nc.Block(name))
        t_tab = None
        if tabs:
            for i, (cd, sd) in enumerate(tabs):
                ct = sb(st, nc, f"{name}_tc{i}", [128, NT], F32)
                stt = sb(st, nc, f"{name}_ts{i}", [128, NT], F32)
                P.dma("sync", ct[:, :], cd, tsem)
                t_tab = P.dma("sync", stt[:, :], sd, tsem)
                tabt.append((ct, stt))
        t_h = []
        for t in range(NT // 512):
            t_h.append(nu.run(xT, t * 512, gcol, lambda c, t=t: hT[:, c, t * 512:(t + 1) * 512]))
        f1 = Ring([sb(st, nc, f"{name}_f1{k}", [128, 512], F32) for k in range(2)])
        f2 = Ring([sb(st, nc, f"{name}_f2{k}", [128, 512], F32) for k in range(2)])
        og = Ring([sb(st, nc, f"{name}_og{k}", [128, 512], BF16) for k in range(3)])
        ssem = P.dma_sems("s", 3)
        last_store = []
        for (w_ap, wsw_ap, tab_idx, scale, out_dram) in fm_specs:
            k, wt, fr = wr.next()
            wv = w_ap.rearrange("(c p) f -> p c f", p=128)
            t_w = P.dma("gpsimd", wt[:, 0, :, :], wv, wsem[k], waits=fr)
            if wsw_ap is not None:
                t_w = P.dma("gpsimd", wt[:, 1, :, :], wsw_ap.rearrange("(c p) f -> p c f", p=128), wsem[k])
            tl = None
            for t in range(NT // 512):
                k1, p1t, fr1 = p1.next()
                ta = None
                for c in range(DC):
                    ta = P.pe(lambda e, c=c, wt=wt, p1t=p1t, t=t: e.matmul(p1t[:, :], wt[:, 0, c, :], hT[:, c, t * 512:(t + 1) * 512],
                                                                           start=(c == 0), stop=(c == DC - 1)),
                              waits=[t_w, t_h[t]] + fr1, sig=(c == DC - 1))
                tl = ta
                ko, ogt, fro = og.next()
                if wsw_ap is not None:
                    k2, p2t, fr2 = p2.next()
                    tb = None
                    for c in range(DC):
                        tb = P.pe(lambda e, c=c, wt=wt, p2t=p2t, t=t: e.matmul(p2t[:, :], wt[:, 1, c, :], hT[:, c, t * 512:(t + 1) * 512],
                                                                               start=(c == 0), stop=(c == DC - 1)),
                                  waits=fr2, sig=(c == DC - 1))
                    tl = tb
                    ct, stt = tabt[tab_idx]
                    kf1, f1t, frf1 = f1.next()
                    kf2, f2t, frf2 = f2.next()
                    td1 = P.dve(lambda e, f1t=f1t, p1t=p1t, ct=ct, t=t: e.tensor_tensor(out=f1t[:, :], in0=p1t[:, :], in1=ct[:, t * 512:(t + 1) * 512], op=ALU.mult),
                                waits=[ta, t_tab] + frf1, sig=True)
                    p1.release(k1, td1)
                    td2 = P.dve(lambda e, f2t=f2t, p2t=p2t, stt=stt, t=t: e.tensor_tensor(out=f2t[:, :], in0=p2t[:, :], in1=stt[:, t * 512:(t + 1) * 512], op=ALU.mult),
                                waits=[tb] + frf2, sig=True)
                    p2.release(k2, td2)
                    to = P.pool(lambda e, ogt=ogt, f1t=f1t, f2t=f2t: e.tensor_tensor(out=ogt[:, :], in0=f1t[:, :], in1=f2t[:, :], op=ALU.add),
                                waits=[td1, td2] + fro, sig=True)
                    f1.release(kf1, to)
                    f2.release(kf2, to)
                else:
                    to = P.act(lambda e, ogt=ogt, p1t=p1t, scale=scale: e.activation(out=ogt[:, :], in_=p1t[:, :], func=AF.Copy, scale=float(scale)),
                               waits=[ta] + fro, sig=True)
                    p1.release(k1, to)
                tst = P.dma("sync", out_dram[:, t * 512:(t + 1) * 512], ogt[:, :], ssem[ko], waits=[to])
                og.release(ko, tst)
                last_store.append(tst)
            wr.release(k, tl)
        if tm_specs:
            wtm = Ring([sb(st, nc, f"{name}_wtm{k}", [128, DC, 512], BF16) for k in range(2)])
            wtsem = P.dma_sems("wt", 2)
            otb = Ring([sb(st, nc, f"{name}_otb{k}", [128, 512], BF16) for k in range(2)])
            otf = Ring([sb(st, nc, f"{name}_otf{k}", [128, 512], F32) for k in range(2)])
            sbsem = P.dma_sems("sb", 2)
            sfsem = P.dma_sems("sf", 2)
            for (w_ap, func, out_dram, odt) in tm_specs:
                k, wt, fr = wtm.next()
                t_w = P.dma("gpsimd", wt[:, :, :], w_ap.rearrange("(c p) f -> p c f", p=128), wtsem[k], waits=fr)
                tl = None
                for tb_ in range(NT // 128):
                    k1, p1t, fr1 = p1.next()
                    ta = None
                    for c in range(DC):
                        ta = P.pe(lambda e, c=c, wt=wt, p1t=p1t, tb_=tb_: e.matmul(p1t[:, :], hT[:, c, tb_ * 128:(tb_ + 1) * 128], wt[:, c, :],
                                                                                   start=(c == 0), stop=(c == DC - 1)),
                                  waits=[t_w, t_h[tb_ // 4]] + fr1, sig=(c == DC - 1))
                    tl = ta
                    if odt == "bf16":
                        ko, ot, fro = otb.next()
                        sem = sbsem[ko]
                    else:
                        ko, ot, fro = otf.next()
                        sem = sfsem[ko]
                    to = P.act(lambda e, ot=ot, p1t=p1t, func=func: e.activation(out=ot[:, :], in_=p1t[:, :], func=func),
                               waits=[ta] + fro, sig=True)
                    p1.release(k1, to)
                    tst = P.dma("sync", out_dram[tb_ * 128:(tb_ + 1) * 128, :], ot[:, :], sem, waits=[to])
                    (otb if odt == "bf16" else otf).release(ko, tst)
                    last_store.append(tst)
                wtm.release(k, tl)
        P.op("sync", lambda e: e.nop(), waits=last_store[-8:])
        P.emit(blk)


def block_store_x(nc, cx, name, x_out):
    with contextlib.ExitStack() as st:
        P = Prog(nc, st, name)
        sem = P.dma_sems("st", 1)[0]
        blk = st.enter_context(nc.Block(name))
        xv = x_out.rearrange("(c p) t -> p c t", p=128)
        tok = None
        for c in range(DC):
            tok = P.dma("sync", xv[:, c, :], cx.xT[:, c, :], sem)
        P.op("sync", lambda e: e.nop(), waits=[tok])
        P.emit(blk)


def block_load_x(nc, cx, name, x_in, gains):
    xv = x_in.rearrange("(c p) t -> p c t", p=128)
    pairs = [(cx.xT[:, c, :], xv[:, c, :]) for c in range(DC)]
    pairs.append((cx.gT[:, :], gains))
    block_load(nc, name, pairs)


def block_final(nc, cx, name, gcol, y_out):
    with contextlib.ExitStack() as st:
        P = Prog(nc, st, name)
        nu = NormUnit(P, st, nc, name, cx.gT, cx.ones)
        yt = Ring([sb(st, nc, f"{name}_y{k}", [128, DC, 512], F32) for k in range(2)])
        ssem = P.dma_sems("s", 2)
        blk = st.enter_context(nc.Block(name))
        yv = y_out.rearrange("(c p) t -> p c t", p=128)
        last = []
        for t in range(NT // 512):
            k, y, fr = yt.next()
            th = nu.run(cx.xT, t * 512, gcol, lambda c, y=y: y[:, c, :], out_free=fr)
            tok = None
            for c in range(DC):
                tok = P.dma("sync", yv[:, c, t * 512:(t + 1) * 512], y[:, c, :], ssem[k], waits=[th])
            yt.release(k, tok)
            last.append(tok)
        P.op("sync", lambda e: e.nop(), waits=last[-2:])
        P.emit(blk)


def new_nc():
    return bass.Bass("TRN2", target_bir_lowering=False)


def setup_ctx(nc, st, name):
    cx = Ctx()
    cx.xT = sb(st, nc, name + "_xT", [128, DC, NT], F32)
    cx.gT = sb(st, nc, name + "_gT", [128, 56], F32)
    cx.ones, cx.eps = block_consts(nc, st, name)
    return cx


G_FFN1 = (0, 8)
G_MIX = (16, 24)
G_FFN2 = (32, 40)
G_FINAL = 48


def build_A():
    nc = new_nc()
    x_in = nc.dram_tensor("xT_in", [D, NT], F32, kind="ExternalInput").ap()
    gains = nc.dram_tensor("gains", [128, 56], F32, kind="ExternalInput").ap()
    w_in = nc.dram_tensor("w_in", [D, 2 * FF], F32, kind="ExternalInput").ap()
    w_out = nc.dram_tensor("w_out", [FF, D], F32, kind="ExternalInput").ap()
    w_hyb = nc.dram_tensor("w_hyb", [D, HYB_IN], F32, kind="ExternalInput").ap()
    w_hsw = nc.dram_tensor("w_hsw", [D, 2048], F32, kind="ExternalInput").ap()
    tabs_d = [nc.dram_tensor(n, [128, NT], F32, kind="ExternalInput").ap() for n in ("rc", "rs", "dc", "ds")]
    x_out = nc.dram_tensor("xT_out", [D, NT], F32, kind="ExternalOutput").ap()
    qk_fm = nc.dram_tensor("qk_fm", [2048, NT], BF16, kind="ExternalOutput").ap()
    v_tm = nc.dram_tensor("v_tm", [NT, 1024], BF16, kind="ExternalOutput").ap()
    g_tm = nc.dram_tensor("g_tm", [NT, 512], F32, kind="ExternalOutput").ap()
    with contextlib.ExitStack() as st:
        cx = setup_ctx(nc, st, "A")
        block_load_x(nc, cx, "A_ld", x_in, gains)
        block_ffn(nc, cx, "A_ffn", G_FFN1[0], w_in, w_out)
        block_store_x(nc, cx, "A_st", x_out)
        fm = []
        for i in range(4):
            fm.append((w_hyb[:, i * 128:(i + 1) * 128], w_hsw[:, i * 128:(i + 1) * 128], 0, 1.0, qk_fm[i * 128:(i + 1) * 128, :]))
        for i in range(4):
            fm.append((w_hyb[:, 512 + i * 128:512 + (i + 1) * 128], w_hsw[:, 512 + i * 128:512 + (i + 1) * 128], 0, 1.0,
                       qk_fm[512 + i * 128:512 + (i + 1) * 128, :]))
        for i in range(4):
            fm.append((w_hyb[:, 2048 + i * 128:2048 + (i + 1) * 128], w_hsw[:, 1024 + i * 128:1024 + (i + 1) * 128], 1, 1.0,
                       qk_fm[1024 + i * 128:1024 + (i + 1) * 128, :]))
        for i in range(4):
            fm.append((w_hyb[:, 2560 + i * 128:2560 + (i + 1) * 128], w_hsw[:, 1536 + i * 128:1536 + (i + 1) * 128], 1, 1.0,
                       qk_fm[1536 + i * 128:1536 + (i + 1) * 128, :]))
        tm = [
            (w_hyb[:, 1024:1536], AF.Copy, v_tm[:, 0:512], "bf16"),
            (w_hyb[:, 3072:3584], AF.Copy, v_tm[:, 512:1024], "bf16"),
            (w_hyb[:, 1536:2048], AF.Silu, g_tm[:, :], "f32"),
        ]
        block_proj(nc, cx, "A_pj", G_MIX[0], fm, tm, tabs=[(tabs_d[0], tabs_d[1]), (tabs_d[2], tabs_d[3])])
    return nc


def gains_table(inp):
    vecs = [inp["ffn1_norm"][0], inp["ffn1_norm"][1], inp["mix_norm"][0], inp["mix_norm"][1],
            inp["ffn2_norm"][0], inp["ffn2_norm"][1], inp["final_norm"]]
    cols = [np.asarray(v, np.float32).reshape(DC, 128).T for v in vecs]
    return np.ascontiguousarray(np.concatenate(cols, axis=1))


def rope_tables(rot_dim, theta, pos):
    half = rot_dim // 2
    inv_freq = (1.0 / (np.float32(theta) ** (np.arange(half, dtype=np.float32) / np.float32(half)))).astype(np.float32)
    ang = pos.astype(np.float32)[:, None] * inv_freq[None, :]
    cos, sin = np.cos(ang).astype(np.float32), np.sin(ang).astype(np.float32)
    C = np.ones((64, len(pos)), np.float32)
    Sg = np.zeros((64, len(pos)), np.float32)
    C[:half] = cos.T
    C[half:rot_dim] = cos.T
    Sg[:half] = -sin.T
    Sg[half:rot_dim] = sin.T
    return np.ascontiguousarray(np.concatenate([C, C], 0)), np.ascontiguousarray(np.concatenate([Sg, Sg], 0))


def swap_cols(w, rot_dim):
    half = rot_dim // 2
    n = w.shape[1] // 64
    idx = []
    for h in range(n):
        base = h * 64
        perm = list(range(64))
        for i in range(half):
            perm[i] = half + i
            perm[half + i] = i
        idx.extend(base + p for p in perm)
    return np.ascontiguousarray(w[:, idx])


def run_A(inp, xT_shards):
    nc = build_A()
    g = gains_table(inp)
    w_hyb = np.asarray(inp["hyb_w_in"][0], np.float32)
    w_hsw = np.concatenate([swap_cols(w_hyb[:, 0:512], 64), swap_cols(w_hyb[:, 512:1024], 64),
                            swap_cols(w_hyb[:, 2048:2560], 16), swap_cols(w_hyb[:, 2560:3072], 16)], axis=1)
    w_hsw = np.ascontiguousarray(w_hsw)
    maps = []
    for c in range(NCORE):
        pos = np.arange(NT) + (c % 4) * NT
        rc, rs = rope_tables(64, 10000.0, pos)
        dc, ds = rope_tables(16, 500000.0, pos)
        maps.append({"xT_in": xT_shards[c], "gains": g, "w_in": np.asarray(inp["ffn1_w_in"][0], np.float32),
                     "w_out": np.asarray(inp["ffn1_w_out"][0], np.float32), "w_hyb": w_hyb, "w_hsw": w_hsw,
                     "rc": rc, "rs": rs, "dc": dc, "ds": ds})
    res = run_bass_kernel_spmd(nc, maps, core_ids=list(range(NCORE)))
    return res.results


def shard_x(x):
    xf = np.asarray(x, np.float32).reshape(B * S, D)
    return [np.ascontiguousarray(xf[c * NT:(c + 1) * NT].T) for c in range(NCORE)]


def block_ret(nc, name, d):
    with contextlib.ExitStack() as st:
        P = Prog(nc, st, name)
        qT = sb(st, nc, name + "_qT", [128, S], BF16)
        kT = sb(st, nc, name + "_kT", [128, S], BF16)
        qd = sb(st, nc, name + "_qd", [128, S], BF16)
        v = sb(st, nc, name + "_v", [128, 64, 128], BF16)
        gate = sb(st, nc, name + "_gate", [128, 64, 128], F32)
        outT = sb(st, nc, name + "_outT", [128, S], BF16)
        decT = sb(st, nc, name + "_decT", [128, 256], F32)
        qdec = sb(st, nc, name + "_qdec", [128, 512], F32)
        kdec = sb(st, nc, name + "_kdec", [128, 2], F32)
        cdec = sb(st, nc, name + "_cdec", [128, 1], F32)
        gn = sb(st, nc, name + "_gn", [128, 128], F32)
        ident = sb(st, nc, name + "_ident", [128, 128], BF16)
        epsg = sb(st, nc, name + "_epsg", [128, 1], F32)
        S_f = sb(st, nc, name + "_Sf", [128, 64], F32)
        S_b = Ring([sb(st, nc, f"{name}_Sb{k}", [128, 64], BF16) for k in range(2)])
        PT = Ring([sb(st, nc, f"{name}_PT{k}", [128, 256], BF16) for k in range(2)])
        kd = Ring([sb(st, nc, f"{name}_kd{k}", [128, 128], BF16) for k in range(2)])
        stats = Ring([sb(st, nc, f"{name}_stats{k}", [128, 2, 6], F32) for k in range(2)])
        mv = Ring([sb(st, nc, f"{name}_mv{k}", [128, 2, 2], F32) for k in range(2)])
        sd = Ring([sb(st, nc, f"{name}_sd{k}", [128, 2], F32) for k in range(2)])
        rstd = Ring([sb(st, nc, f"{name}_rstd{k}", [128, 2], F32) for k in range(2)])
        y = Ring([sb(st, nc, f"{name}_y{k}", [128, 128], F32) for k in range(2)])
        m_tm = Ring([sb(st, nc, f"{name}_mtm{k}", [128, 128], BF16) for k in range(2)])
        scA = Ring([psb(st, nc, f"{name}_scA")])
        scB = Ring([psb(st, nc, f"{name}_scB")])
        tp = Ring([psb(st, nc, f"{name}_tp{k}", (128, 1024), BF16) for k in range(2)])
        obA = Ring([psb(st, nc, f"{name}_oA")])
        obB = Ring([psb(st, nc, f"{name}_oB")])
        kv = Ring([psb(st, nc, f"{name}_kv{k}") for k in range(2)])
        lsem = P.dma_sems("l", 1)[0]
        osem = P.dma_sems("o", 1)[0]
        blk = st.enter_context(nc.Block(name))
        P.dma("sync", qT[:, :], d["rq"], lsem)
        P.dma("sync", kT[:, :], d["rk"], lsem)
        P.dma("sync", v[:, :, :], d["rv"].rearrange("(n j) e -> j n e", j=128), lsem)
        P.dma("sync", gate[:, :, :], d["rg"].rearrange("(n j) e -> j n e", j=128), lsem)
        for t_, nm in ((decT, "decT"), (qdec, "qdec"), (kdec, "kdec"), (cdec, "cdec"), (gn, "gn"), (ident, "ident")):
            t_ld = P.dma("sync", t_[:, :], d[nm], lsem)
        P.dve(lambda e: e.memset(epsg[:, :], GN_EPS))
        if DBG.get('ret_chunks', 64) < 64:
            P.dve(lambda e: e.memset(outT[:, :], 0.0))
        t_qd = None
        for ch in range(16):
            t_qd = P.dve(lambda e, ch=ch: e.tensor_tensor(out=qd[:, ch * 512:(ch + 1) * 512], in0=qT[:, ch * 512:(ch + 1) * 512], in1=qdec[:, :], op=ALU.mult),
                         waits=[t_ld], sig=(ch == 15))
        NCH = DBG.get('ret_chunks', 64)
        cst = [dict() for _ in range(NCH)]
        sb_cur = [None]
        out_last = [None]

        def stage1(n):
            c_ = cst[n]
            cs = slice(n * 128, (n + 1) * 128)
            _, sctA, frscA = scA.next()
            _, sctB, frscB = scB.next()
            P.pe(lambda e, sctA=sctA, cs=cs: e.matmul(sctA[:, 0:128], kT[0:64, cs], qT[0:64, cs], start=True, stop=True), waits=[t_ld] + frscA + frscB)
            t_sc = P.pe(lambda e, sctB=sctB, cs=cs: e.matmul(sctB[:, 0:128], kT[64:128, cs], qT[64:128, cs], start=True, stop=True), sig=True)
            kpt, ptt, frpt = PT.next()
            P.dve(lambda e, ptt=ptt, sctA=sctA: e.tensor_tensor(out=ptt[:, 0:128], in0=sctA[:, 0:128], in1=decT[:, 0:128], op=ALU.mult),
                  waits=[t_sc] + frpt, free=True)
            t_pt = P.dve(lambda e, ptt=ptt, sctB=sctB: e.tensor_tensor(out=ptt[:, 128:256], in0=sctB[:, 0:128], in1=decT[:, 128:256], op=ALU.mult),
                         sig=True, free=True)
            scA.release(0, t_pt)
            scB.release(0, t_pt)
            ktp, tpt, frtp = tp.next()
            t_tp = P.pe(lambda e, tpt=tpt, cs=cs: e.transpose(tpt[:, 0:128], kT[:, cs], ident[:, :]), waits=frtp, sig=True)
            kkd, kdt, frkd = kd.next()
            P.dve(lambda e, kdt=kdt, tpt=tpt: e.tensor_scalar(out=kdt[:, 0:64], in0=tpt[:, 0:64], scalar1=kdec[:, 0:1], scalar2=None, op0=ALU.mult),
                  waits=[t_tp] + frkd, free=True)
            t_kd = P.dve(lambda e, kdt=kdt, tpt=tpt: e.tensor_scalar(out=kdt[:, 64:128], in0=tpt[:, 64:128], scalar1=kdec[:, 1:2], scalar2=None, op0=ALU.mult),
                         sig=True, free=True)
            tp.release(ktp, t_kd)
            c_.update(cs=cs, kpt=kpt, ptt=ptt, t_pt=t_pt, kkd=kkd, kdt=kdt, t_kd=t_kd)

        def stage2(n):
            c_ = cst[n]
            cs, kpt, ptt, t_pt, kkd, kdt, t_kd = c_["cs"], c_["kpt"], c_["ptt"], c_["t_pt"], c_["kkd"], c_["kdt"], c_["t_kd"]
            _, otA, froA = obA.next()
            _, otB, froB = obB.next()
            ots = (otA, otB)
            t_o = None
            for h in range(2):
                hs = slice(h * 64, (h + 1) * 64)
                ot = ots[h]
                t_o = P.pe(lambda e, ot=ot, ptt=ptt, n=n, h=h, hs=hs: e.matmul(ot[:, 0:64], ptt[:, h * 128:(h + 1) * 128], v[:, n, hs], start=True, stop=(n == 0)),
                           waits=[t_pt] + froA + froB, sig=(n == 0 and h == 1))
                if n > 0:
                    t_o = P.pe(lambda e, ot=ot, cs=cs, hs=hs, sbt=sb_cur[0][1]: e.matmul(ot[:, 0:64], qd[hs, cs], sbt[hs, :], start=False, stop=True),
                               waits=[t_qd, sb_cur[0][2]], sig=(h == 1))
            PT.release(kpt, t_o)
            if sb_cur[0] is not None:
                S_b.release(sb_cur[0][0], t_o)
            kkv, kvt, frkv = kv.next()
            P.pe(lambda e, kvt=kvt, kdt=kdt, n=n: e.matmul(kvt[0:64, 0:64], kdt[:, 0:64], v[:, n, 0:64], start=True, stop=True), waits=[t_kd] + frkv)
            t_kv = P.pe(lambda e, kvt=kvt, kdt=kdt, n=n: e.matmul(kvt[64:128, 0:64], kdt[:, 64:128], v[:, n, 64:128], start=True, stop=True), sig=True)
            kd.release(kkd, t_kv)
            if n < NCH - 1:
                if n == 0:
                    P.dve(lambda e, kvt=kvt: e.tensor_copy(out=S_f[:, :], in_=kvt[:, 0:64]), waits=[t_kv])
                else:
                    P.dve(lambda e, kvt=kvt: e.scalar_tensor_tensor(out=S_f[:, :], in0=S_f[:, :], scalar=cdec[:, 0:1], in1=kvt[:, 0:64], op0=ALU.mult, op1=ALU.add),
                          waits=[t_kv])
                ksb, sbt, frsb = S_b.next()
                t_sbn = P.dve(lambda e, sbt=sbt: e.tensor_copy(out=sbt[:, :], in_=S_f[:, :]), waits=frsb, sig=True)
                kv.release(kkv, t_sbn)
                sb_cur[0] = (ksb, sbt, t_sbn)
            kst, stt, _ = stats.next()
            kmv, mvt, frmv = mv.next()
            for h in range(2):
                P.dve(lambda e, stt=stt, ot=ots[h], h=h: e.bn_stats(out=stt[:, h, :], in_=ot[:, 0:64]), waits=[t_o])
            t_mv = None
            for h in range(2):
                t_mv = P.dve(lambda e, stt=stt, mvt=mvt, h=h: e.bn_aggr(out=mvt[:, h, :], in_=stt[:, h, :]), waits=frmv, sig=(h == 1))
            ksd, sdt, frsd = sd.next()
            t_sd = P.act(lambda e, sdt=sdt, mvt=mvt: e.activation(out=sdt[:, :], in_=mvt[:, :, 1], func=AF.Sqrt, bias=epsg[:, 0:1], scale=1.0),
                         waits=[t_mv] + frsd, sig=True, free=True)
            krs, rst, _ = rstd.next()
            P.dve(lambda e, rst=rst, sdt=sdt: e.reciprocal(out=rst[:, :], in_=sdt[:, :]), waits=[t_sd])
            ky, yt, fry = y.next()
            t_y = None
            for h in range(2):
                t_y = P.dve(lambda e, yt=yt, ot=ots[h], mvt=mvt, rst=rst, h=h: e.tensor_scalar(out=yt[:, h * 64:(h + 1) * 64], in0=ot[:, 0:64],
                                                                                              scalar1=mvt[:, h, 0:1], scalar2=rst[:, h:h + 1],
                                                                                              op0=ALU.subtract, op1=ALU.mult),
                            waits=fry, sig=(h == 1))
            obA.release(0, t_y)
            obB.release(0, t_y)
            sd.release(ksd, t_y)
            mv.release(kmv, t_y)
            t_y2 = P.dve(lambda e, yt=yt: e.tensor_tensor(out=yt[:, :], in0=yt[:, :], in1=gn[:, :], op=ALU.mult), sig=True)
            km, mt, frm = m_tm.next()
            t_m = P.pool(lambda e, mt=mt, yt=yt, n=n: e.tensor_tensor(out=mt[:, :], in0=yt[:, :], in1=gate[:, n, :], op=ALU.mult),
                         waits=[t_y2, t_ld] + frm, sig=True, free=True)
            y.release(ky, t_m)
            c_.update(km=km, mt=mt, t_m=t_m)

        def stage3(n):
            c_ = cst[n]
            cs, km, mt, t_m = c_["cs"], c_["km"], c_["mt"], c_["t_m"]
            ktp2, tpt2, frtp2 = tp.next()
            t_tp2 = P.pe(lambda e, tpt2=tpt2, mt=mt: e.transpose(tpt2[:, 0:128], mt[:, :], ident[:, :]), waits=[t_m] + frtp2, sig=True)
            m_tm.release(km, t_tp2)
            out_last[0] = P.act(lambda e, tpt2=tpt2, cs=cs: e.activation(out=outT[:, cs], in_=tpt2[:, 0:128], func=AF.Copy), waits=[t_tp2], sig=True, free=True)
            tp.release(ktp2, out_last[0])

        for i in range(-1, NCH + 1):
            if 0 <= i + 1 < NCH:
                stage1(i + 1)
            if 0 <= i < NCH:
                stage2(i)
            if 0 <= i - 1 < NCH:
                stage3(i - 1)
        t_out_last = out_last[0]
        t_st = P.dma("sync", d["ret_out"], outT[:, :], osem, waits=[t_out_last])
        P.op("sync", lambda e: e.nop(), waits=[t_st])
        P.emit(blk)


DIL = (1, 4, 16)


def block_dil(nc, name, d, ones):
    with contextlib.ExitStack() as st:
        P = Prog(nc, st, name)
        qT = sb(st, nc, name + "_qT", [128, S], BF16)
        kT = sb(st, nc, name + "_kT", [128, S], BF16)
        vp = [sb(st, nc, f"{name}_vp{p}", [128, 64, 128], BF16) for p in range(3)]
        acc = sb(st, nc, name + "_acc", [128, 2, S], F32)
        qDI = sb(st, nc, name + "_qDI", [128, S], BF16)
        kDI = sb(st, nc, name + "_kDI", [128, S], BF16)
        outT = qT
        dmask = sb(st, nc, name + "_dmask", [128, 512], BF16)
        sq = Ring([sb(st, nc, f"{name}_sq{k}", [128, 512], BF16) for k in range(2)])
        mx = sb(st, nc, name + "_mx", [128, 4, 16], F32)
        mx2 = sb(st, nc, name + "_mx2", [128, 4], F32)
        prod = sb(st, nc, name + "_prod", [128, 2], F32)
        bias = sb(st, nc, name + "_bias", [128, 2], F32)
        Pt = Ring([sb(st, nc, f"{name}_P{k}", [128, 512], BF16) for k in range(3)])
        spA = Ring([psb(st, nc, f"{name}_spA{k}") for k in range(2)])
        spB = Ring([psb(st, nc, f"{name}_spB{k}") for k in range(2)])
        nb_ = Ring([psb(st, nc, f"{name}_n{k}", (128, 4, 128)) for k in range(2)])
        nq = Ring([psb(st, nc, f"{name}_nq{k}") for k in range(2)])
        lsem = P.dma_sems("l", 1)[0]
        vsem = P.dma_sems("v", 1)[0]
        osem = P.dma_sems("o", 1)[0]
        blk = st.enter_context(nc.Block(name))
        P.dma("sync", qT[:, :], d["dq"], lsem)
        P.dma("sync", kT[:, :], d["dk"], lsem)
        t_ld = P.dma("sync", dmask[:, :], d["dmask"], lsem)
        t_v = None
        for p, dl in enumerate(DIL):
            nbc = 64 // dl
            src = d["dv"].rearrange("(nb i r) e -> r i nb e", i=128, r=dl)
            for r in range(dl):
                t_v = P.dma("sync", vp[p][:, r * nbc:(r + 1) * nbc, :], src[r], vsem)
        for ti, src_t in enumerate((qT, kT)):
            for ch in range(16):
                ks, sqt, frs = sq.next()
                t_sq = P.dve(lambda e, sqt=sqt, src_t=src_t, ch=ch: e.tensor_tensor(out=sqt[:, :], in0=src_t[:, ch * 512:(ch + 1) * 512],
                                                                                   in1=src_t[:, ch * 512:(ch + 1) * 512], op=ALU.mult),
                             waits=[t_ld] + frs, sig=True)
                t_last = None
                for h in range(2):
                    kn, nqt, frn = nq.next()
                    t_n = P.pe(lambda e, nqt=nqt, sqt=sqt, h=h: e.matmul(nqt[:, :], ones[h * 64:(h + 1) * 64, :], sqt[h * 64:(h + 1) * 64, :], start=True, stop=True),
                               waits=[t_sq] + frn, sig=True)
                    t_r = P.dve(lambda e, nqt=nqt, ti=ti, h=h, ch=ch: e.reduce_max(out=mx[:, ti * 2 + h, ch:ch + 1], in_=nqt[:, :], axis=AX.X),
                                waits=[t_n], sig=True)
                    nq.release(kn, t_r)
                    t_last = t_n
                sq.release(ks, t_last)
        P.dve(lambda e: e.reduce_max(out=mx2[:, :], in_=mx[:, :, :], axis=AX.X))
        t_p = P.dve(lambda e: e.tensor_tensor(out=prod[:, :], in0=mx2[:, 0:2], in1=mx2[:, 2:4], op=ALU.mult), sig=True)
        t_s = P.act(lambda e: e.activation(out=prod[:, :], in_=prod[:, :], func=AF.Sqrt), waits=[t_p], sig=True)
        t_bias = P.dve(lambda e: e.tensor_scalar(out=bias[:, :], in0=prod[:, :], scalar1=-1.02 / 8.0, scalar2=None, op0=ALU.mult), waits=[t_s], sig=True)
        units = []
        for p, dl in enumerate(DIL[:DBG.get('dil_patterns', 3)]):
            nbc = 64 // dl
            for r in range(dl):
                for nb in range(nbc):
                    units.append((p, dl, nbc, r, nb))
        ust = [dict() for _ in units]
        pe_last = [None]
        de_tok = {0: None}
        acc_last = [None]

        def stageA(i):
            p, dl, nbc, r, nb = units[i]
            L = S // dl
            u_ = ust[i]
            if p == 0:
                qs, ks_ = qT, kT
            else:
                qs, ks_ = qDI, kDI
                if p not in de_tok:
                    t_de = None
                    for rr in range(dl):
                        P.dve(lambda e, rr=rr, dl=dl, L=L: e.tensor_copy(out=qDI[:, rr * L:(rr + 1) * L], in_=qT[:, rr:rr + (L - 1) * dl + 1:dl]),
                              waits=[t_ld, pe_last[0]])
                        t_de = P.dve(lambda e, rr=rr, dl=dl, L=L: e.tensor_copy(out=kDI[:, rr * L:(rr + 1) * L], in_=kT[:, rr:rr + (L - 1) * dl + 1:dl]), sig=True)
                    de_tok[p] = t_de
            t_de = de_tok[p]
            ctoks = slice(r * L + nb * 128, r * L + (nb + 1) * 128)
            ptoks = slice(r * L + (nb - 1) * 128, r * L + nb * 128)
            kspA, sptA, frspA = spA.next()
            kspB, sptB, frspB = spB.next()
            spts = (sptA, sptB)
            t_s_ = None
            first = True
            for h in range(2):
                hs = slice(h * 64, (h + 1) * 64)
                spt = spts[h]
                t_s_ = P.pe(lambda e, spt=spt, hs=hs, ctoks=ctoks, qs=qs, ks_=ks_: e.matmul(spt[:, 0:128], ks_[hs, ctoks], qs[hs, ctoks], start=True, stop=True),
                            waits=([t_ld, t_de] + frspA + frspB) if first else (), sig=(nb == 0 and h == 1))
                first = False
                if nb > 0:
                    t_s_ = P.pe(lambda e, spt=spt, hs=hs, ctoks=ctoks, ptoks=ptoks, qs=qs, ks_=ks_: e.matmul(spt[:, 128:256], ks_[hs, ptoks], qs[hs, ctoks],
                                                                                                           start=True, stop=True), sig=(h == 1))
            pe_last[0] = t_s_
            kP, Ptt, frP = Pt.next()
            w = 256 if nb > 0 else 128
            t_e = None
            for h in range(2):
                t_e = P.act(lambda e, Ptt=Ptt, spt=spts[h], h=h, w=w: e.activation(out=Ptt[:, h * 256:h * 256 + w], in_=spt[:, 0:w], func=AF.Exp,
                                                                                  bias=bias[:, h:h + 1], scale=0.125),
                            waits=[t_s_, t_bias] + frP, sig=(h == 1), free=True)
            spA.release(kspA, t_e)
            spB.release(kspB, t_e)
            if nb > 0:
                t_m = P.pool(lambda e, Ptt=Ptt: e.tensor_tensor(out=Ptt[:, :], in0=Ptt[:, :], in1=dmask[:, :], op=ALU.mult), waits=[t_e, t_ld], sig=True, free=True)
            else:
                P.pool(lambda e, Ptt=Ptt: e.tensor_tensor(out=Ptt[:, 0:128], in0=Ptt[:, 0:128], in1=dmask[:, 0:128], op=ALU.mult), waits=[t_e, t_ld], free=True)
                t_m = P.pool(lambda e, Ptt=Ptt: e.tensor_tensor(out=Ptt[:, 256:384], in0=Ptt[:, 256:384], in1=dmask[:, 256:384], op=ALU.mult), sig=True, free=True)
            u_["kP"], u_["Ptt"], u_["t_m"] = kP, Ptt, t_m

        def stageB(i):
            p, dl, nbc, r, nb = units[i]
            u_ = ust[i]
            kP, Ptt, t_m = u_["kP"], u_["Ptt"], u_["t_m"]
            u = r * nbc + nb
            start = nb * 128 * dl + r
            toks = slice(start, start + 127 * dl + 1, dl)
            kn, nt, frn = nb_.next()
            t_n = None
            first = True
            for which in range(2):
                for h in range(2):
                    hs = slice(h * 64, (h + 1) * 64)
                    lhs_c = vp[p][:, u, hs] if which == 0 else ones[:, 0:64]
                    t_n = P.pe(lambda e, nt=nt, hs=hs, lhs_c=lhs_c, Ptt=Ptt, h=h, which=which, nb=nb: e.matmul(nt[hs, which, :], lhs_c, Ptt[:, h * 256:h * 256 + 128],
                                                                                                             start=True, stop=(nb == 0)),
                               waits=([t_m, t_v] + frn) if first else (), sig=(nb == 0 and which == 1 and h == 1))
                    first = False
                    if nb > 0:
                        lhs_p = vp[p][:, u - 1, hs] if which == 0 else ones[:, 0:64]
                        t_n = P.pe(lambda e, nt=nt, hs=hs, lhs_p=lhs_p, Ptt=Ptt, h=h, which=which: e.matmul(nt[hs, which, :], lhs_p, Ptt[:, h * 256 + 128:h * 256 + 256],
                                                                                                        start=False, stop=True),
                                   sig=(which == 1 and h == 1))
            Pt.release(kP, t_n)
            if p == 0:
                t_acc = P.dve(lambda e, nt=nt, toks=toks: e.tensor_copy(out=acc[:, :, toks], in_=nt[:, 0:2, :]), waits=[t_n], sig=True, free=True)
            else:
                t_acc = P.dve(lambda e, nt=nt, toks=toks: e.tensor_tensor(out=acc[:, :, toks], in0=acc[:, :, toks], in1=nt[:, 0:2, :], op=ALU.add),
                              waits=[t_n], sig=True, free=True, hard=[acc_last[0]])
            acc_last[0] = t_acc
            nb_.release(kn, t_acc)

        NU = len(units)
        for i in range(-1, NU):
            if i + 1 < NU:
                stageA(i + 1)
            if i >= 0:
                stageB(i)
        t_pe_last = pe_last[0]
        t_o = None
        for ch in range(4):
            cs = slice(ch * 2048, (ch + 1) * 2048)
            P.dve(lambda e, cs=cs: e.reciprocal(out=acc[:, 1, cs], in_=acc[:, 1, cs]), hard=[acc_last[0]])
            t_o = P.dve(lambda e, cs=cs: e.tensor_tensor(out=outT[:, cs], in0=acc[:, 0, cs], in1=acc[:, 1, cs], op=ALU.mult), waits=[t_pe_last], sig=True)
        t_st = P.dma("sync", d["dil_out"], outT[:, :], osem, waits=[t_o])
        P.op("sync", lambda e: e.nop(), waits=[t_st])
        P.emit(blk)


def ret_consts(g):
    i = np.arange(128, dtype=np.float64)
    decT = np.zeros((128, 256), np.float32)
    qdec = np.zeros((128, 512), np.float32)
    kdec = np.zeros((128, 2), np.float32)
    cdec = np.zeros((128, 1), np.float32)
    for hh in range(2):
        h = 2 * g + hh
        lg = np.log(1.0 - 2.0 ** (-5.0 - h))
        diff = i[None, :] - i[:, None]
        dm = np.where(diff >= 0, np.exp(np.maximum(diff, 0) * lg), 0.0) / 8.0
        decT[:, hh * 128:(hh + 1) * 128] = dm
        qd = np.exp((i + 1) * lg) / 8.0
        qdec[hh * 64:(hh + 1) * 64, :] = np.tile(qd, 4)[None, :]
        kdec[:, hh] = np.exp((127 - i) * lg)
        cdec[hh * 64:(hh + 1) * 64, 0] = np.exp(128 * lg)
    return decT, qdec, kdec, cdec


def dil_mask():
    k = np.arange(128)[:, None]
    q = np.arange(128)[None, :]
    cur = (k <= q).astype(np.float32)
    prev = (k >= q).astype(np.float32)
    m = np.concatenate([cur, prev, cur, prev], axis=1)
    return m.astype(NPBF)


def build_B(which="both"):
    nc = new_nc()
    d = {}
    d["rqk"] = nc.dram_tensor("rqk", [2, 128, S], BF16, kind="ExternalInput").ap()
    d["dqk"] = nc.dram_tensor("dqk", [2, 128, S], BF16, kind="ExternalInput").ap()
    d["rv"] = nc.dram_tensor("rv", [S, 128], BF16, kind="ExternalInput").ap()
    d["dv"] = nc.dram_tensor("dv", [S, 128], BF16, kind="ExternalInput").ap()
    d["rg"] = nc.dram_tensor("rg", [S, 128], F32, kind="ExternalInput").ap()
    d["decT"] = nc.dram_tensor("decT", [128, 256], F32, kind="ExternalInput").ap()
    d["qdec"] = nc.dram_tensor("qdec", [128, 512], F32, kind="ExternalInput").ap()
    d["kdec"] = nc.dram_tensor("kdec", [128, 2], F32, kind="ExternalInput").ap()
    d["cdec"] = nc.dram_tensor("cdec", [128, 1], F32, kind="ExternalInput").ap()
    d["gn"] = nc.dram_tensor("gn", [128, 128], F32, kind="ExternalInput").ap()
    d["ident"] = nc.dram_tensor("ident", [128, 128], BF16, kind="ExternalInput").ap()
    d["dmask"] = nc.dram_tensor("dmask", [128, 512], BF16, kind="ExternalInput").ap()
    d["mixT"] = nc.dram_tensor("mixT", [256, S], BF16, kind="ExternalOutput").ap()
    d["rq"], d["rk"], d["dq"], d["dk"] = d["rqk"][0], d["rqk"][1], d["dqk"][0], d["dqk"][1]
    d["ret_out"], d["dil_out"] = d["mixT"][0:128, :], d["mixT"][128:256, :]
    with contextlib.ExitStack() as st:
        ones, eps = block_consts(nc, st, "B")
        if which in ("both", "ret"):
            block_ret(nc, "B_ret", d)
        if which in ("both", "dil"):
            block_dil(nc, "B_dil", d, ones)
    return nc


def run_B(inp, resA, which="both"):
    nc = build_B(which)
    ident = np.eye(128, dtype=np.float32).astype(NPBF)
    dm = dil_mask()
    gnv = np.asarray(inp["ret_gn"][0], np.float32)
    maps = []
    for c in range(NCORE):
        b, g = c // 4, c % 4
        ra = [resA[b * 4 + i] for i in range(4)]
        qk = np.concatenate([np.asarray(r["qk_fm"]) for r in ra], axis=1)
        vt = np.concatenate([np.asarray(r["v_tm"]) for r in ra], axis=0)
        gt = np.concatenate([np.asarray(r["g_tm"]) for r in ra], axis=0)
        sl = slice(g * 128, (g + 1) * 128)
        rqk = np.ascontiguousarray(np.stack([qk[0:512][sl], qk[512:1024][sl]]))
        dqk = np.ascontiguousarray(np.stack([qk[1024:1536][sl], qk[1536:2048][sl]]))
        decT, qdec, kdec, cdec = ret_consts(g)
        maps.append({"rqk": rqk, "dqk": dqk,
                     "rv": np.ascontiguousarray(vt[:, g * 128:(g + 1) * 128]),
                     "dv": np.ascontiguousarray(vt[:, 512 + g * 128:512 + (g + 1) * 128]),
                     "rg": np.ascontiguousarray(gt[:, g * 128:(g + 1) * 128]),
                     "decT": decT, "qdec": qdec, "kdec": kdec, "cdec": cdec,
                     "gn": np.ascontiguousarray(np.broadcast_to(gnv[g * 128:(g + 1) * 128][None, :], (128, 128))),
                     "ident": ident, "dmask": dm})
    res = run_bass_kernel_spmd(nc, maps, core_ids=list(range(NCORE)))
    return res.results


def mix_from_B(resB):
    out = []
    for c in range(NCORE):
        b, ts = c // 4, c % 4
        tsl = slice(ts * NT, (ts + 1) * NT)
        ret = np.concatenate([np.asarray(resB[b * 4 + g]["mixT"])[0:128, tsl] for g in range(4)], axis=0)
        dil = np.concatenate([np.asarray(resB[b * 4 + g]["mixT"])[128:256, tsl] for g in range(4)], axis=0)
        out.append(np.ascontiguousarray(np.concatenate([ret, dil], axis=0)))
    return out


def block_sb(nc, name, d):
    with contextlib.ExitStack() as st:
        P = Prog(nc, st, name)
        qT = [sb(st, nc, f"{name}_qT{p}", [128, S], BF16) for p in range(2)]
        kT = [sb(st, nc, f"{name}_kT{p}", [128, S], BF16) for p in range(2)]
        v = sb(st, nc, name + "_v", [128, 64, 256], BF16)
        outT = [sb(st, nc, f"{name}_oT{p}", [128, S], BF16) for p in range(2)]
        masks = sb(st, nc, name + "_masks", [128, 4, 512], BF16)
        negU = sb(st, nc, name + "_negU", [128, 128], BF16)
        negones = sb(st, nc, name + "_negones", [128, 128], BF16)
        Eb = Ring([sb(st, nc, f"{name}_E{k}", [128, 1024], F32) for k in range(2)])
        Lb = Ring([sb(st, nc, f"{name}_L{k}", [128, 1024], BF16) for k in range(3)])
        Ab = Ring([sb(st, nc, f"{name}_A{k}", [128, 1024], BF16) for k in range(2)])
        Accb = Ring([sb(st, nc, f"{name}_Acc{k}", [128, 1024], BF16) for k in range(3)])
        Zb = Ring([psb(st, nc, f"{name}_Z{k}", (128, 1024)) for k in range(3)])
        Ob = Ring([psb(st, nc, f"{name}_O{k}") for k in range(2)])
        lsem = P.dma_sems("l", 1)[0]
        osem = P.dma_sems("o", 1)[0]
        blk = st.enter_context(nc.Block(name))
        for p in range(2):
            P.dma("sync", qT[p][:, :], d["q"][p], lsem)
            P.dma("sync", kT[p][:, :], d["k"][p], lsem)
        P.dma("sync", v[:, :, :], d["v"].rearrange("(n j) e -> j n e", j=128), lsem)
        P.dma("sync", masks[:, :, :], d["masks"], lsem)
        t_ld = P.dma("sync", negU[:, :], d["negU"], lsem)
        t_c = P.dve(lambda e: e.memset(negones[:, :], -1.0), sig=True)

        nqt = DBG.get("sb_qt", 16)
        npair = DBG.get("sb_pairs", 2)
        if nqt < 16 or npair < 2:
            for p in range(2):
                P.dve(lambda e, p=p: e.memset(outT[p][:, :], 0.0))
        blocks = []
        for p in range(npair):
            for qt in range(nqt):
                kbs = list(range(4 * qt + 3, -1, -1))
                for kb in kbs:
                    a = kb - 4 * qt
                    blocks.append(dict(p=p, qt=qt, kb=kb, a=(a if a >= 0 else None), first=(kb == kbs[0]), last=(kb == 0)))
        N = len(blocks)
        stt = [dict() for _ in range(N)]
        acc_cur = [None]
        o_cur = [None]
        last_evacs = []
        H2 = (slice(0, 512), slice(512, 1024))

        def st1(i):
            b = blocks[i]
            s_ = stt[i]
            kz, zt, frz = Zb.next()
            p = b["p"]
            qs = slice(b["qt"] * 512, (b["qt"] + 1) * 512)
            ks = slice(b["kb"] * 128, (b["kb"] + 1) * 128)
            P.pe(lambda e, zt=zt, p=p, qs=qs, ks=ks: e.matmul(zt[:, 0:512], kT[p][0:64, ks], qT[p][0:64, qs], start=True, stop=True),
                 waits=[t_ld] + frz)
            s_["t_qk"] = P.pe(lambda e, zt=zt, p=p, qs=qs, ks=ks: e.matmul(zt[:, 512:1024], kT[p][64:128, ks], qT[p][64:128, qs], start=True, stop=True),
                              sig=True)
            s_["kz"], s_["zt"] = kz, zt

        def st2(i):
            b = blocks[i]
            s_ = stt[i]
            zt = s_["zt"]
            ke, et, fre = Eb.next()
            t_e = P.act(lambda e, et=et, zt=zt: e.activation(out=et[:, :], in_=zt[:, :], func=AF.Exp), waits=[s_["t_qk"]] + fre, sig=True, free=True)
            s_["ke"], s_["et"], s_["t_e"] = ke, et, t_e

        def st2b(i):
            b = blocks[i]
            s_ = stt[i]
            ke, et, t_e = s_["ke"], s_["et"], s_["t_e"]
            kl, lt, frl = Lb.next()
            t_l = P.act(lambda e, lt=lt, et=et: e.activation(out=lt[:, :], in_=et[:, :], func=AF.Ln, bias=1.0, scale=1.0), waits=frl, sig=True,
                        free=True, hard=[t_e])
            Eb.release(ke, t_l)
            if b["a"] is not None:
                a = b["a"]
                P.dve(lambda e, lt=lt, a=a: e.tensor_tensor(out=lt[:, 0:512], in0=lt[:, 0:512], in1=masks[:, a, :], op=ALU.mult), waits=[t_l, t_ld], free=True)
                t_l = P.dve(lambda e, lt=lt, a=a: e.tensor_tensor(out=lt[:, 512:1024], in0=lt[:, 512:1024], in1=masks[:, a, :], op=ALU.mult), sig=True, free=True)
            s_["kl"], s_["lt"], s_["t_l"] = kl, lt, t_l
            s_["acc"] = None if b["first"] else acc_cur[0]
            if not b["last"]:
                ka, at, fra = Accb.next()
                if b["first"]:
                    t_a = P.pool(lambda e, at=at, lt=lt: e.tensor_copy(out=at[:, :], in_=lt[:, :]), waits=[t_l] + fra, sig=True, free=True)
                else:
                    prev = acc_cur[0]
                    t_a = P.pool(lambda e, at=at, lt=lt, pt=prev[1]: e.tensor_tensor(out=at[:, :], in0=pt[:, :], in1=lt[:, :], op=ALU.add),
                                 waits=[t_l, prev[2]] + fra, sig=True)
                acc_cur[0] = (ka, at, t_a)
                s_["t_accupd"] = t_a
            else:
                s_["t_accupd"] = None

        def st3(i):
            b = blocks[i]
            s_ = stt[i]
            zt, lt = s_["zt"], s_["lt"]
            t_u = None
            for hh in range(2):
                t_u = P.pe(lambda e, zt=zt, lt=lt, hh=hh: e.matmul(zt[:, H2[hh]], negU[:, :], lt[:, H2[hh]], start=False, stop=True, skip_group_check=True),
                           waits=[s_["t_l"], t_c], sig=(hh == 1))
            if s_["acc"] is not None:
                ka, at, t_a = s_["acc"]
                for hh in range(2):
                    t_u = P.pe(lambda e, zt=zt, at=at, hh=hh: e.matmul(zt[:, H2[hh]], negones[:, :], at[:, H2[hh]], start=False, stop=True, skip_group_check=True),
                               waits=[t_a], sig=(hh == 1))
                Accb.release(ka, t_u)
                if s_["t_accupd"] is not None:
                    Accb.release(ka, s_["t_accupd"])
            s_["t_u"] = t_u
            rel = [t_u]
            if s_["t_accupd"] is not None:
                rel.append(s_["t_accupd"])
            Lb.release(s_["kl"], *rel)

        def st4(i):
            b = blocks[i]
            s_ = stt[i]
            zt = s_["zt"]
            kA, At, frA = Ab.next()
            t_A = P.act(lambda e, At=At, zt=zt: e.activation(out=At[:, :], in_=zt[:, :], func=AF.Exp), waits=[s_["t_u"]] + frA, sig=True, free=True)
            Zb.release(s_["kz"], t_A)
            if b["a"] is not None:
                a = b["a"]
                P.dve(lambda e, At=At, a=a: e.tensor_tensor(out=At[:, 0:512], in0=At[:, 0:512], in1=masks[:, a, :], op=ALU.mult), waits=[t_A], free=True)
                t_A = P.dve(lambda e, At=At, a=a: e.tensor_tensor(out=At[:, 512:1024], in0=At[:, 512:1024], in1=masks[:, a, :], op=ALU.mult), sig=True, free=True)
            s_["kA"], s_["At"], s_["t_A"] = kA, At, t_A

        def st5(i):
            b = blocks[i]
            s_ = stt[i]
            At = s_["At"]
            p = b["p"]
            if b["first"]:
                ko, ot, fro = Ob.next()
                o_cur[0] = (ko, ot, fro)
            ko, ot, fro = o_cur[0]
            t_av = None
            for hh in range(2):
                hs = slice(hh * 64, (hh + 1) * 64)
                hcol = slice((2 * p + hh) * 64, (2 * p + hh + 1) * 64)
                t_av = P.pe(lambda e, ot=ot, hs=hs, At=At, kb=b["kb"], hcol=hcol, hh=hh, first=b["first"], last=b["last"]:
                            e.matmul(ot[hs, :], v[:, kb, hcol], At[:, H2[hh]], start=first, stop=last, skip_group_check=True),
                            waits=[s_["t_A"]] + (fro if b["first"] else []), sig=(hh == 1))
            Ab.release(s_["kA"], t_av)
            if b["last"]:
                qs = slice(b["qt"] * 512, (b["qt"] + 1) * 512)
                t_ev = P.dve(lambda e, ot=ot, p=p, qs=qs: e.tensor_copy(out=outT[p][:, qs], in_=ot[:, :]), waits=[t_av], sig=True, free=True)
                Ob.release(ko, t_ev)
                last_evacs.append(t_ev)

        for s_i in range(-2, N):
            if 0 <= s_i + 2 < N:
                st1(s_i + 2)
            if 0 <= s_i + 1 < N:
                st2(s_i + 1)
            if 0 <= s_i < N:
                st4(s_i)
            if 0 <= s_i + 1 < N:
                st2b(s_i + 1)
            if 0 <= s_i < N:
                st5(s_i)
            if 0 <= s_i + 1 < N:
                st3(s_i + 1)
        toks = []
        for p in range(2):
            toks.append(P.dma("sync", d["o"][p], outT[p][:, :], osem, waits=last_evacs[-2:]))
        P.op("sync", lambda e: e.nop(), waits=toks)
        P.emit(blk)


def sb_consts():
    m = np.zeros((128, 4, 512), np.float32)
    i = np.arange(128)[:, None]
    j = np.arange(512)[None, :]
    for a in range(4):
        m[:, a, :] = ((a * 128 + i) < j).astype(np.float32)
    jj = np.arange(128)[:, None]
    ss = np.arange(128)[None, :]
    negU = -(jj >= ss).astype(np.float32)
    return m.astype(NPBF), negU.astype(NPBF)


def build_D():
    nc = new_nc()
    d = {}
    d["qk"] = nc.dram_tensor("qk", [2, 2, 128, S], BF16, kind="ExternalInput").ap()
    d["v"] = nc.dram_tensor("v", [S, 256], BF16, kind="ExternalInput").ap()
    d["masks"] = nc.dram_tensor("masks", [128, 4, 512], BF16, kind="ExternalInput").ap()
    d["negU"] = nc.dram_tensor("negU", [128, 128], BF16, kind="ExternalInput").ap()
    d["oT"] = nc.dram_tensor("oT", [256, S], BF16, kind="ExternalOutput").ap()
    d["q"] = [d["qk"][0, p] for p in range(2)]
    d["k"] = [d["qk"][1, p] for p in range(2)]
    d["o"] = [d["oT"][p * 128:(p + 1) * 128, :] for p in range(2)]
    block_sb(nc, "D_sb", d)
    return nc


def run_D(resC):
    nc = build_D()
    masks, negU = sb_consts()
    maps = []
    for c in range(NCORE):
        b, g = c // 4, c % 4
        rc = [resC[b * 4 + i] for i in range(4)]
        qk = np.concatenate([np.asarray(r["qk_fm"]) for r in rc], axis=1)
        vt = np.concatenate([np.asarray(r["v_tm"]) for r in rc], axis=0)
        q = qk[0:1024][g * 256:(g + 1) * 256].reshape(2, 128, S)
        k = qk[1024:2048][g * 256:(g + 1) * 256].reshape(2, 128, S)
        maps.append({"qk": np.ascontiguousarray(np.stack([q, k])), "v": np.ascontiguousarray(vt[:, g * 256:(g + 1) * 256]),
                     "masks": masks, "negU": negU})
    res = run_bass_kernel_spmd(nc, maps, core_ids=list(range(NCORE)))
    return res.results


def mix_from_D(resD):
    out = []
    for c in range(NCORE):
        b, ts = c // 4, c % 4
        tsl = slice(ts * NT, (ts + 1) * NT)
        out.append(np.ascontiguousarray(np.concatenate([np.asarray(resD[b * 4 + g]["oT"])[:, tsl] for g in range(4)], axis=0)))
    return out


def build_C():
    nc = new_nc()
    x_in = nc.dram_tensor("xT_in", [D, NT], F32, kind="ExternalInput").ap()
    gains = nc.dram_tensor("gains", [128, 56], F32, kind="ExternalInput").ap()
    mix = nc.dram_tensor("mix", [D, NT], BF16, kind="ExternalInput").ap()
    w_ho = nc.dram_tensor("w_ho", [D, D], F32, kind="ExternalInput").ap()
    w_in2 = nc.dram_tensor("w_in2", [D, 2 * FF], F32, kind="ExternalInput").ap()
    w_out2 = nc.dram_tensor("w_out2", [FF, D], F32, kind="ExternalInput").ap()
    w_in1 = nc.dram_tensor("w_in1", [D, 2 * FF], F32, kind="ExternalInput").ap()
    w_out1 = nc.dram_tensor("w_out1", [FF, D], F32, kind="ExternalInput").ap()
    w_sb = nc.dram_tensor("w_sb", [D, 3 * D], F32, kind="ExternalInput").ap()
    x_out = nc.dram_tensor("xT_out", [D, NT], F32, kind="ExternalOutput").ap()
    qk_fm = nc.dram_tensor("qk_fm", [2048, NT], BF16, kind="ExternalOutput").ap()
    v_tm = nc.dram_tensor("v_tm", [NT, 1024], BF16, kind="ExternalOutput").ap()
    with contextlib.ExitStack() as st:
        cx = setup_ctx(nc, st, "C")
        block_load_x(nc, cx, "C_ld", x_in, gains)
        block_outproj(nc, cx, "C_op", mix, w_ho)
        block_ffn(nc, cx, "C_f2", G_FFN2[0], w_in2, w_out2)
        block_ffn(nc, cx, "C_f1", G_FFN1[1], w_in1, w_out1)
        block_store_x(nc, cx, "C_st", x_out)
        fm = []
        for i in range(8):
            fm.append((w_sb[:, i * 128:(i + 1) * 128], None, None, 0.125, qk_fm[i * 128:(i + 1) * 128, :]))
        for i in range(8):
            fm.append((w_sb[:, 1024 + i * 128:1024 + (i + 1) * 128], None, None, 1.0, qk_fm[1024 + i * 128:1024 + (i + 1) * 128, :]))
        tm = [(w_sb[:, 2048:2560], AF.Copy, v_tm[:, 0:512], "bf16"), (w_sb[:, 2560:3072], AF.Copy, v_tm[:, 512:1024], "bf16")]
        block_proj(nc, cx, "C_pj", G_MIX[1], fm, tm)
    return nc


def run_C(inp, xT_shards, mix_shards):
    nc = build_C()
    g = gains_table(inp)
    maps = []
    for c in range(NCORE):
        maps.append({"xT_in": xT_shards[c], "gains": g, "mix": mix_shards[c],
                     "w_ho": np.asarray(inp["hyb_w_out"][0], np.float32),
                     "w_in2": np.asarray(inp["ffn2_w_in"][0], np.float32), "w_out2": np.asarray(inp["ffn2_w_out"][0], np.float32),
                     "w_in1": np.asarray(inp["ffn1_w_in"][1], np.float32), "w_out1": np.asarray(inp["ffn1_w_out"][1], np.float32),
                     "w_sb": np.asarray(inp["sb_w_in"][0], np.float32)})
    res = run_bass_kernel_spmd(nc, maps, core_ids=list(range(NCORE)))
    return res.results


def build_E():
    nc = new_nc()
    x_in = nc.dram_tensor("xT_in", [D, NT], F32, kind="ExternalInput").ap()
    gains = nc.dram_tensor("gains", [128, 56], F32, kind="ExternalInput").ap()
    mix = nc.dram_tensor("mix", [D, NT], BF16, kind="ExternalInput").ap()
    w_so = nc.dram_tensor("w_so", [D, D], F32, kind="ExternalInput").ap()
    w_in2 = nc.dram_tensor("w_in2", [D, 2 * FF], F32, kind="ExternalInput").ap()
    w_out2 = nc.dram_tensor("w_out2", [FF, D], F32, kind="ExternalInput").ap()
    y_out = nc.dram_tensor("yT_out", [D, NT], F32, kind="ExternalOutput").ap()
    with contextlib.ExitStack() as st:
        cx = setup_ctx(nc, st, "E")
        block_load_x(nc, cx, "E_ld", x_in, gains)
        block_outproj(nc, cx, "E_op", mix, w_so)
        block_ffn(nc, cx, "E_f2", G_FFN2[1], w_in2, w_out2)
        block_final(nc, cx, "E_fin", G_FINAL, y_out)
    return nc


def run_E(inp, xT_shards, mix_shards):
    nc = build_E()
    g = gains_table(inp)
    maps = []
    for c in range(NCORE):
        maps.append({"xT_in": xT_shards[c], "gains": g, "mix": mix_shards[c],
                     "w_so": np.asarray(inp["sb_w_out"][0], np.float32),
                     "w_in2": np.asarray(inp["ffn2_w_in"][1], np.float32), "w_out2": np.asarray(inp["ffn2_w_out"][1], np.float32)})
    res = run_bass_kernel_spmd(nc, maps, core_ids=list(range(NCORE)))
    return res.results


def kernel(**inp):
    inp = {k: np.asarray(v) for k, v in inp.items()}
    xs = shard_x(inp["x"])
    resA = run_A(inp, xs)
    resB = run_B(inp, resA)
    xs1 = [np.asarray(r["xT_out"]) for r in resA]
    resC = run_C(inp, xs1, mix_from_B(resB))
    resD = run_D(resC)
    xs2 = [np.asarray(r["xT_out"]) for r in resC]
    resE = run_E(inp, xs2, mix_from_D(resD))
    y = np.concatenate([np.asarray(r["yT_out"], np.float32).T for r in resE], axis=0)
    return np.ascontiguousarray(y.reshape(B, S, D).astype(np.float32))


def build_fused():
    nc = new_nc()
    dt_ = nc.dram_tensor
    x_in = dt_("xT_in", [D, S], F32, kind="ExternalInput").ap()
    gains = dt_("gains", [128, 56], F32, kind="ExternalInput").ap()
    w_in1 = [dt_(f"w_in1_{l}", [D, 2 * FF], F32, kind="ExternalInput").ap() for l in range(2)]
    w_out1 = [dt_(f"w_out1_{l}", [FF, D], F32, kind="ExternalInput").ap() for l in range(2)]
    w_in2 = [dt_(f"w_in2_{l}", [D, 2 * FF], F32, kind="ExternalInput").ap() for l in range(2)]
    w_out2 = [dt_(f"w_out2_{l}", [FF, D], F32, kind="ExternalInput").ap() for l in range(2)]
    w_hyb = dt_("w_hyb", [D, HYB_IN], F32, kind="ExternalInput").ap()
    w_hsw = dt_("w_hsw", [D, 2048], F32, kind="ExternalInput").ap()
    w_ho = dt_("w_ho", [D, D], F32, kind="ExternalInput").ap()
    w_sb = dt_("w_sb", [D, 3 * D], F32, kind="ExternalInput").ap()
    w_so = dt_("w_so", [D, D], F32, kind="ExternalInput").ap()
    tabs_d = [dt_(n, [4, 128, NT], F32, kind="ExternalInput").ap() for n in ("rc", "rs", "dc", "ds")]
    decT = dt_("decT", [4, 128, 256], F32, kind="ExternalInput").ap()
    qdec = dt_("qdec", [4, 128, 512], F32, kind="ExternalInput").ap()
    kdec = dt_("kdec", [4, 128, 2], F32, kind="ExternalInput").ap()
    cdec = dt_("cdec", [4, 128, 1], F32, kind="ExternalInput").ap()
    gn = dt_("gn", [4, 128, 128], F32, kind="ExternalInput").ap()
    ident = dt_("ident", [128, 128], BF16, kind="ExternalInput").ap()
    dmask = dt_("dmask", [128, 512], BF16, kind="ExternalInput").ap()
    masks = dt_("masks", [128, 4, 512], BF16, kind="ExternalInput").ap()
    negU = dt_("negU", [128, 128], BF16, kind="ExternalInput").ap()
    y_out = dt_("yT_out", [D, S], F32, kind="ExternalOutput").ap()
    x_scr = dt_("x_scr", [D, S], F32, kind="Internal").ap()
    qk_scr = dt_("qk_scr", [2048, S], BF16, kind="Internal").ap()
    v_scr = dt_("v_scr", [S, 1024], BF16, kind="Internal").ap()
    g_scr = dt_("g_scr", [S, 512], F32, kind="Internal").ap()
    mix_scr = dt_("mix_scr", [D, S], BF16, kind="Internal").ap()
    NQ = DBG.get("fused_quarters", 4)

    def qs(q):
        return slice(q * NT, (q + 1) * NT)

    PH = DBG.get('fused_phases', 'ABCDE')
    NG = DBG.get('fused_groups', 4)
    for q in range(NQ if 'A' in PH else 0):
        with contextlib.ExitStack() as st:
            cx = setup_ctx(nc, st, f"A{q}")
            block_load_x(nc, cx, f"A{q}_ld", x_in[:, qs(q)], gains)
            block_ffn(nc, cx, f"A{q}_ffn", G_FFN1[0], w_in1[0], w_out1[0])
            block_store_x(nc, cx, f"A{q}_st", x_scr[:, qs(q)])
            fm = []
            for i in range(4):
                fm.append((w_hyb[:, i * 128:(i + 1) * 128], w_hsw[:, i * 128:(i + 1) * 128], 0, 1.0, qk_scr[i * 128:(i + 1) * 128, qs(q)]))
            for i in range(4):
                fm.append((w_hyb[:, 512 + i * 128:512 + (i + 1) * 128], w_hsw[:, 512 + i * 128:512 + (i + 1) * 128], 0, 1.0,
                           qk_scr[512 + i * 128:512 + (i + 1) * 128, qs(q)]))
            for i in range(4):
                fm.append((w_hyb[:, 2048 + i * 128:2048 + (i + 1) * 128], w_hsw[:, 1024 + i * 128:1024 + (i + 1) * 128], 1, 1.0,
                           qk_scr[1024 + i * 128:1024 + (i + 1) * 128, qs(q)]))
            for i in range(4):
                fm.append((w_hyb[:, 2560 + i * 128:2560 + (i + 1) * 128], w_hsw[:, 1536 + i * 128:1536 + (i + 1) * 128], 1, 1.0,
                           qk_scr[1536 + i * 128:1536 + (i + 1) * 128, qs(q)]))
            tm = [
                (w_hyb[:, 1024:1536], AF.Copy, v_scr[qs(q), 0:512], "bf16"),
                (w_hyb[:, 3072:3584], AF.Copy, v_scr[qs(q), 512:1024], "bf16"),
                (w_hyb[:, 1536:2048], AF.Silu, g_scr[qs(q), :], "f32"),
            ]
            block_proj(nc, cx, f"A{q}_pj", G_MIX[0], fm, tm, tabs=[(tabs_d[0][q], tabs_d[1][q]), (tabs_d[2][q], tabs_d[3][q])])
    for g in range(NG if 'B' in PH else 0):
        with contextlib.ExitStack() as st:
            ones, eps = block_consts(nc, st, f"B{g}")
            d = {"rq": qk_scr[g * 128:(g + 1) * 128, :], "rk": qk_scr[512 + g * 128:512 + (g + 1) * 128, :],
                 "dq": qk_scr[1024 + g * 128:1024 + (g + 1) * 128, :], "dk": qk_scr[1536 + g * 128:1536 + (g + 1) * 128, :],
                 "rv": v_scr[:, g * 128:(g + 1) * 128], "dv": v_scr[:, 512 + g * 128:512 + (g + 1) * 128], "rg": g_scr[:, g * 128:(g + 1) * 128],
                 "decT": decT[g], "qdec": qdec[g], "kdec": kdec[g], "cdec": cdec[g], "gn": gn[g], "ident": ident, "dmask": dmask,
                 "ret_out": mix_scr[g * 128:(g + 1) * 128, :], "dil_out": mix_scr[512 + g * 128:512 + (g + 1) * 128, :]}
            block_ret(nc, f"B{g}_ret", d)
            block_dil(nc, f"B{g}_dil", d, ones)
    for q in range(NQ if 'C' in PH else 0):
        with contextlib.ExitStack() as st:
            cx = setup_ctx(nc, st, f"C{q}")
            block_load_x(nc, cx, f"C{q}_ld", x_scr[:, qs(q)], gains)
            block_outproj(nc, cx, f"C{q}_op", mix_scr[:, qs(q)], w_ho)
            block_ffn(nc, cx, f"C{q}_f2", G_FFN2[0], w_in2[0], w_out2[0])
            block_ffn(nc, cx, f"C{q}_f1", G_FFN1[1], w_in1[1], w_out1[1])
            block_store_x(nc, cx, f"C{q}_st", x_scr[:, qs(q)])
            fm = []
            for i in range(8):
                fm.append((w_sb[:, i * 128:(i + 1) * 128], None, None, 0.125, qk_scr[i * 128:(i + 1) * 128, qs(q)]))
            for i in range(8):
                fm.append((w_sb[:, 1024 + i * 128:1024 + (i + 1) * 128], None, None, 1.0, qk_scr[1024 + i * 128:1024 + (i + 1) * 128, qs(q)]))
            tm = [(w_sb[:, 2048:2560], AF.Copy, v_scr[qs(q), 0:512], "bf16"), (w_sb[:, 2560:3072], AF.Copy, v_scr[qs(q), 512:1024], "bf16")]
            block_proj(nc, cx, f"C{q}_pj", G_MIX[1], fm, tm)
    for g in range(NG if 'D' in PH else 0):
        d = {"q": [qk_scr[g * 256 + p * 128:g * 256 + (p + 1) * 128, :] for p in range(2)],
             "k": [qk_scr[1024 + g * 256 + p * 128:1024 + g * 256 + (p + 1) * 128, :] for p in range(2)],
             "v": v_scr[:, g * 256:(g + 1) * 256], "masks": masks, "negU": negU,
             "o": [mix_scr[g * 256 + p * 128:g * 256 + (p + 1) * 128, :] for p in range(2)]}
        block_sb(nc, f"D{g}_sb", d)
    for q in range(NQ if 'E' in PH else 0):
        with contextlib.ExitStack() as st:
            cx = setup_ctx(nc, st, f"E{q}")
            block_load_x(nc, cx, f"E{q}_ld", x_scr[:, qs(q)], gains)
            block_outproj(nc, cx, f"E{q}_op", mix_scr[:, qs(q)], w_so)
            block_ffn(nc, cx, f"E{q}_f2", G_FFN2[1], w_in2[1], w_out2[1])
            block_final(nc, cx, f"E{q}_fin", G_FINAL, y_out[:, qs(q)])
    return nc


def fused_inputs(inp):
    g = gains_table(inp)
    w_hyb = np.asarray(inp["hyb_w_in"][0], np.float32)
    w_hsw = np.ascontiguousarray(np.concatenate([swap_cols(w_hyb[:, 0:512], 64), swap_cols(w_hyb[:, 512:1024], 64),
                                                 swap_cols(w_hyb[:, 2048:2560], 16), swap_cols(w_hyb[:, 2560:3072], 16)], axis=1))
    tabs = {k: [] for k in ("rc", "rs", "dc", "ds")}
    for q in range(4):
        pos = np.arange(NT) + q * NT
        rc, rs = rope_tables(64, 10000.0, pos)
        dc, ds = rope_tables(16, 500000.0, pos)
        for k, v_ in zip(("rc", "rs", "dc", "ds"), (rc, rs, dc, ds)):
            tabs[k].append(v_)
    tabs = {k: np.ascontiguousarray(np.stack(v_)) for k, v_ in tabs.items()}
    rcs = [ret_consts(gg) for gg in range(4)]
    gnv = np.asarray(inp["ret_gn"][0], np.float32)
    masks, negU = sb_consts()
    common = {"gains": g, "w_hyb": w_hyb, "w_hsw": w_hsw,
              "w_ho": np.asarray(inp["hyb_w_out"][0], np.float32), "w_sb": np.asarray(inp["sb_w_in"][0], np.float32),
              "w_so": np.asarray(inp["sb_w_out"][0], np.float32),
              "decT": np.ascontiguousarray(np.stack([r[0] for r in rcs])), "qdec": np.ascontiguousarray(np.stack([r[1] for r in rcs])),
              "kdec": np.ascontiguousarray(np.stack([r[2] for r in rcs])), "cdec": np.ascontiguousarray(np.stack([r[3] for r in rcs])),
              "gn": np.ascontiguousarray(np.stack([np.broadcast_to(gnv[gg * 128:(gg + 1) * 128][None, :], (128, 128)) for gg in range(4)])),
              "ident": np.eye(128, dtype=np.float32).astype(NPBF), "dmask": dil_mask(), "masks": masks, "negU": negU}
    common.update(tabs)
    for l in range(2):
        common[f"w_in1_{l}"] = np.asarray(inp["ffn1_w_in"][l], np.float32)
        common[f"w_out1_{l}"] = np.asarray(inp["ffn1_w_out"][l], np.float32)
        common[f"w_in2_{l}"] = np.asarray(inp["ffn2_w_in"][l], np.float32)
        common[f"w_out2_{l}"] = np.asarray(inp["ffn2_w_out"][l], np.float32)
    x = np.asarray(inp["x"], np.float32)
    maps = []
    for c in range(NCORE):
        m = dict(common)
        m["xT_in"] = np.ascontiguousarray(x[c // 4].T)
        maps.append(m)
    return maps


def kernel_fused_replicated(**inp):
    inp = {k: np.asarray(v) for k, v in inp.items()}
    nc = build_fused()
    maps = fused_inputs(inp)
    res = run_bass_kernel_spmd(nc, maps, core_ids=list(range(NCORE))).results
    y = np.stack([np.asarray(res[0]["yT_out"], np.float32).T, np.asarray(res[4]["yT_out"], np.float32).T], axis=0)
    return np.ascontiguousarray(y.astype(np.float32))
```

```python
import contextlib
import numpy as np
import ml_dtypes
import concourse.bass as bass
import concourse.mybir as mybir
from concourse.bass_utils import run_bass_kernel_spmd

F32 = mybir.dt.float32
BF16 = mybir.dt.bfloat16
AF = mybir.ActivationFunctionType
ALU = mybir.AluOpType
AX = mybir.AxisListType
NPBF = ml_dtypes.bfloat16

D = 1024
S = 8192
B = 2
NCORE = 8
NT = 2048
DC = 8
FF = 2816
FC = 22
HYB_IN = 3584
EPS = 1e-6
GN_EPS = 1e-5
GT = 1024
DBG = {}


class Prog:
    ENGS = ("sync", "scalar", "vector", "gpsimd", "tensor")

    def __init__(self, nc, stack, name):
        self.nc = nc
        self.stack = stack
        self.name = name
        self.ops = {e: [] for e in self.ENGS}
        self.esem = {}
        self.ecount = {e: 0 for e in self.ENGS}
        self.waited = {e: {} for e in self.ENGS}
        self.dcount = {}
        self.nsem = 0

    def new_sem(self, tag):
        self.nsem += 1
        return self.stack.enter_context(self.nc.semaphore(f"{self.name}_{tag}_{self.nsem}"))

    def dma_sems(self, tag, n):
        sems = [self.new_sem(tag) for _ in range(n)]
        for s in sems:
            self.dcount[id(s)] = [s, 0]
        return sems

    def _filter_waits(self, eng, waits):
        ws = []
        best = {}
        for t in waits:
            if t is None:
                continue
            if id(t[0]) not in best or best[id(t[0])][1] < t[1]:
                best[id(t[0])] = t
        for t in best.values():
            sem, val = t
            if eng in self.esem and sem is self.esem[eng]:
                continue
            key = id(sem)
            if self.waited[eng].get(key, 0) >= val:
                continue
            self.waited[eng][key] = val
            ws.append((sem, val))
        return ws

    def op(self, eng, fn, waits=(), sig=False, free=False, hard=()):
        ws = self._filter_waits(eng, waits)
        tok = None
        strict = DBG.get("strict", True) and eng in ("scalar", "vector", "gpsimd")
        if DBG.get("strict_all") and eng == "tensor":
            strict = True
        if strict:
            sig = True
            if self.ecount[eng] > 0 and not free:
                ws.append((self.esem[eng], self.ecount[eng]))
            else:
                for t in hard:
                    if t is not None:
                        ws.append(t)
        if sig:
            if eng not in self.esem:
                self.esem[eng] = self.new_sem("e" + eng)
            self.ecount[eng] += 1
            tok = (self.esem[eng], self.ecount[eng])
        self.ops[eng].append((fn, ws, tok, 1))
        return tok

    def pe(self, fn, waits=(), sig=False, free=False, hard=()):
        return self.op("tensor", fn, waits, sig, free, hard)

    def act(self, fn, waits=(), sig=False, free=False, hard=()):
        return self.op("scalar", fn, waits, sig, free, hard)

    def dve(self, fn, waits=(), sig=False, free=False, hard=()):
        return self.op("vector", fn, waits, sig, free, hard)

    def pool(self, fn, waits=(), sig=False, free=False, hard=()):
        return self.op("gpsimd", fn, waits, sig, free, hard)

    def dma(self, queue, out, in_, sem, waits=()):
        ws = self._filter_waits(queue, waits)
        ent = self.dcount[id(sem)]
        ent[1] += 16
        tok = (sem, ent[1])
        self.ops[queue].append((lambda e, o=out, i=in_: e.dma_start(out=o, in_=i), ws, tok, 16))
        return tok

    def emit(self, block):
        for eng in self.ENGS:
            ops = self.ops[eng]
            if not ops:
                continue

            def body(e, ops=ops):
                for fn, ws, tok, inc in ops:
                    for sem, val in ws:
                        e.wait_ge(sem, val)
                    ins = fn(e)
                    if tok is not None:
                        ins.then_inc(tok[0], inc)

            getattr(block, eng)(body)


class Ring:
    def __init__(self, tiles):
        self.tiles = tiles
        self.n = len(tiles)
        self.i = 0
        self.free = [[] for _ in tiles]

    def next(self):
        k = self.i % self.n
        self.i += 1
        toks = self.free[k]
        self.free[k] = []
        return k, self.tiles[k], toks

    def release(self, k, *toks):
        self.free[k].extend(t for t in toks if t is not None)


def sb(stack, nc, name, shape, dt):
    return stack.enter_context(nc.sbuf_tensor(name, list(shape), dt))


def psb(stack, nc, name, shape=(128, 512), dt=F32):
    return stack.enter_context(nc.psum_tensor(name, list(shape), dt))


class Ctx:
    pass


class NormUnit:
    def __init__(self, P, st, nc, name, gT, ones_bf):
        self.P = P
        self.gT = gT
        self.ones = ones_bf
        self.sq = sb(st, nc, name + "_sq", [128, DC, 512], BF16)
        self.tmp = sb(st, nc, name + "_tmp", [128, 512], F32)
        self.rstd = sb(st, nc, name + "_rstd", [128, 512], F32)
        self.ss = psb(st, nc, name + "_ss")
        self.t_pe = None
        self.t_sqrt = None
        self.t_rec = None

    def run(self, xT, t0, gcol, out_fn, x_ready=(), out_free=(), final=False):
        P = self.P
        sq, tmp, rstd, ss = self.sq, self.tmp, self.rstd, self.ss
        t_sq = None
        for c in range(DC):
            t_sq = P.act(lambda e, c=c: e.activation(out=sq[:, c, :], in_=xT[:, c, t0:t0 + 512], func=AF.Square),
                         waits=list(x_ready) + [self.t_pe], sig=(c == DC - 1))
        t_mm = None
        for c in range(DC):
            t_mm = P.pe(lambda e, c=c: e.matmul(ss[:, :], self.ones[:, :], sq[:, c, :], start=(c == 0), stop=(c == DC - 1)),
                        waits=[t_sq, self.t_sqrt], sig=(c == DC - 1))
        self.t_pe = t_mm
        t_s = P.act(lambda e: e.activation(out=tmp[:, :], in_=ss[:, :], func=AF.Sqrt, bias=EPS_TILE[0][:, 0:1], scale=1.0 / D),
                    waits=[t_mm, self.t_rec], sig=True)
        self.t_sqrt = t_s
        t_r = P.dve(lambda e: e.reciprocal(out=rstd[:, :], in_=tmp[:, :]), waits=[t_s], sig=True)
        self.t_rec = t_r
        t_h = None
        for c in range(DC):
            t_h = P.dve(lambda e, c=c: e.scalar_tensor_tensor(out=out_fn(c), in0=xT[:, c, t0:t0 + 512],
                                                              scalar=self.gT[:, gcol + c:gcol + c + 1],
                                                              in1=rstd[:, :], op0=ALU.mult, op1=ALU.mult),
                        waits=list(x_ready) + list(out_free), sig=(c == DC - 1))
        return t_h


EPS_TILE = [None]


def block_consts(nc, st, name):
    ones = sb(st, nc, name + "_ones", [128, 128], BF16)
    eps = sb(st, nc, name + "_eps", [128, 2], F32)
    with nc.Block(name + "_c") as blk:
        @blk.vector
        def _(v):
            v.memset(ones[:, :], 1.0)
            v.memset(eps[:, 0:1], EPS)
            v.memset(eps[:, 1:2], GN_EPS)
    EPS_TILE[0] = eps
    return ones, eps


def block_load(nc, name, pairs, queue="sync"):
    with contextlib.ExitStack() as st:
        P = Prog(nc, st, name)
        sem = P.dma_sems("ld", 1)[0]
        blk = st.enter_context(nc.Block(name))
        tok = None
        for o, i in pairs:
            tok = P.dma(queue, o, i, sem)
        P.op(queue, lambda e: e.nop(), waits=[tok])
        P.emit(blk)


def block_ffn(nc, cx, name, gcol, w_in, w_out):
    xT = cx.xT
    with contextlib.ExitStack() as st:
        P = Prog(nc, st, name)
        nu = NormUnit(P, st, nc, name, cx.gT, cx.ones)
        hT = sb(st, nc, name + "_hT", [128, DC, GT], BF16)
        actT = sb(st, nc, name + "_actT", [128, FC, GT], BF16)
        wgu = Ring([sb(st, nc, f"{name}_wgu{k}", [128, 2, DC, 128], BF16) for k in range(3)])
        wo = Ring([sb(st, nc, f"{name}_wo{k}", [128, FC, 128], BF16) for k in range(2)])
        sg = Ring([sb(st, nc, f"{name}_sg{k}", [128, 512], F32) for k in range(2)])
        pg = Ring([psb(st, nc, f"{name}_pg{k}") for k in range(2)])
        pu = Ring([psb(st, nc, f"{name}_pu{k}") for k in range(2)])
        py = Ring([psb(st, nc, f"{name}_py{k}") for k in range(2)])
        wsem = P.dma_sems("w", 3)
        osem = P.dma_sems("o", 2)
        blk = st.enter_context(nc.Block(name))
        w_in_v = w_in.rearrange("(c p) f -> p c f", p=128)
        w_out_v = w_out.rearrange("(j p) d -> p j d", p=128)
        ntile = GT // 512
        h_free = []
        a_free = []
        for gi in range(NT // GT):
            g0 = gi * GT
            t_h = []
            for t in range(ntile):
                th = nu.run(xT, g0 + t * 512, gcol, lambda c, t=t: hT[:, c, t * 512:(t + 1) * 512], out_free=h_free)
                t_h.append(th)
            h_free = []
            t_act_last = None
            for j in range(FC):
                k, wt, fr = wgu.next()
                P.dma("gpsimd", wt[:, 0, :, :], w_in_v[:, :, j * 128:(j + 1) * 128], wsem[k], waits=fr)
                t_w = P.dma("gpsimd", wt[:, 1, :, :], w_in_v[:, :, FF + j * 128:FF + (j + 1) * 128], wsem[k])
                t_pe_last = None
                for t in range(ntile):
                    kg, pgt, frg = pg.next()
                    ku, put, fru = pu.next()
                    ks, sgt, frs = sg.next()
                    tg = None
                    for c in range(DC):
                        tg = P.pe(lambda e, c=c, wt=wt, pgt=pgt, t=t: e.matmul(pgt[:, :], wt[:, 0, c, :], hT[:, c, t * 512:(t + 1) * 512],
                                                                               start=(c == 0), stop=(c == DC - 1)),
                                  waits=[t_w, t_h[t]] + frg, sig=(c == DC - 1))
                    tu = None
                    for c in range(DC):
                        tu = P.pe(lambda e, c=c, wt=wt, put=put, t=t: e.matmul(put[:, :], wt[:, 1, c, :], hT[:, c, t * 512:(t + 1) * 512],
                                                                               start=(c == 0), stop=(c == DC - 1)),
                                  waits=fru, sig=(c == DC - 1))
                    t_pe_last = tu
                    ts = P.act(lambda e, sgt=sgt, pgt=pgt: e.activation(out=sgt[:, :], in_=pgt[:, :], func=AF.Silu),
                               waits=[tg] + frs, sig=True)
                    pg.release(kg, ts)
                    ta = P.dve(lambda e, sgt=sgt, put=put, j=j, t=t: e.tensor_tensor(out=actT[:, j, t * 512:(t + 1) * 512], in0=sgt[:, :], in1=put[:, :],
                                                                                   op=ALU.mult),
                               waits=[ts, tu] + a_free, sig=True)
                    pu.release(ku, ta)
                    sg.release(ks, ta)
                    t_act_last = ta
                wgu.release(k, t_pe_last)
                if j == FC - 1:
                    h_free = [t_pe_last]
            a_free = []
            for c in range(DC):
                k, wt, fr = wo.next()
                t_w = P.dma("gpsimd", wt[:, :, :], w_out_v[:, :, c * 128:(c + 1) * 128], osem[k], waits=fr)
                t_pe_last = None
                for t in range(ntile):
                    ky, pyt, fry = py.next()
                    tp = None
                    for j in range(FC):
                        tp = P.pe(lambda e, j=j, wt=wt, pyt=pyt, t=t: e.matmul(pyt[:, :], wt[:, j, :], actT[:, j, t * 512:(t + 1) * 512],
                                                                               start=(j == 0), stop=(j == FC - 1)),
                                  waits=[t_w, t_act_last] + fry, sig=(j == FC - 1))
                    t_pe_last = tp
                    tx = P.dve(lambda e, pyt=pyt, c=c, t=t, g0=g0: e.scalar_tensor_tensor(out=xT[:, c, g0 + t * 512:g0 + (t + 1) * 512], in0=pyt[:, :], scalar=0.5,
                                                                                  in1=xT[:, c, g0 + t * 512:g0 + (t + 1) * 512], op0=ALU.mult, op1=ALU.add),
                               waits=[tp], sig=True)
                    py.release(ky, tx)
                wo.release(k, t_pe_last)
                if c == DC - 1:
                    a_free = [t_pe_last]
        P.emit(blk)


def block_outproj(nc, cx, name, mix_dram, w_out):
    xT = cx.xT
    with contextlib.ExitStack() as st:
        P = Prog(nc, st, name)
        mT = sb(st, nc, name + "_mT", [128, DC, NT], BF16)
        wr = Ring([sb(st, nc, f"{name}_w{k}", [128, DC, 128], BF16) for k in range(2)])
        py = Ring([psb(st, nc, f"{name}_py{k}") for k in range(2)])
        msem = P.dma_sems("m", 1)[0]
        wsem = P.dma_sems("w", 2)
        blk = st.enter_context(nc.Block(name))
        t_m = None
        mv = mix_dram.rearrange("(c p) t -> p c t", p=128)
        for c in range(DC):
            t_m = P.dma("sync", mT[:, c, :], mv[:, c, :], msem)
        w_v = w_out.rearrange("(c p) d -> p c d", p=128)
        for oc in range(DC):
            k, wt, fr = wr.next()
            t_w = P.dma("gpsimd", wt[:, :, :], w_v[:, :, oc * 128:(oc + 1) * 128], wsem[k], waits=fr)
            tl = None
            for t in range(NT // 512):
                ky, pyt, fry = py.next()
                tp = None
                for c in range(DC):
                    tp = P.pe(lambda e, c=c, wt=wt, pyt=pyt, t=t: e.matmul(pyt[:, :], wt[:, c, :], mT[:, c, t * 512:(t + 1) * 512],
                                                                           start=(c == 0), stop=(c == DC - 1)),
                              waits=[t_w, t_m] + fry, sig=(c == DC - 1))
                tl = tp
                tx = P.dve(lambda e, pyt=pyt, oc=oc, t=t: e.tensor_tensor(out=xT[:, oc, t * 512:(t + 1) * 512], in0=pyt[:, :],
                                                                         in1=xT[:, oc, t * 512:(t + 1) * 512], op=ALU.add),
                           waits=[tp], sig=True)
                py.release(ky, tx)
            wr.release(k, tl)
        P.emit(blk)


def block_proj(nc, cx, name, gcol, fm_specs, tm_specs, tabs=None):
    xT = cx.xT
    with contextlib.ExitStack() as st:
        P = Prog(nc, st, name)
        nu = NormUnit(P, st, nc, name, cx.gT, cx.ones)
        hT = sb(st, nc, name + "_hT", [128, DC, NT], BF16)
        wr = Ring([sb(st, nc, f"{name}_w{k}", [128, 2, DC, 128], BF16) for k in range(2)])
        wsem = P.dma_sems("w", 2)
        p1 = Ring([psb(st, nc, f"{name}_p1{k}") for k in range(2)])
        p2 = Ring([psb(st, nc, f"{name}_p2{k}") for k in range(2)])
        tabt = []
        tsem = P.dma_sems("t", 1)[0]
        blk = st.enter_context(nc.Block(name))
        t_tab = None
        if tabs:
            for i, (cd, sd) in enumerate(tabs):
                ct = sb(st, nc, f"{name}_tc{i}", [128, NT], F32)
                stt = sb(st, nc, f"{name}_ts{i}", [128, NT], F32)
                P.dma("sync", ct[:, :], cd, tsem)
                t_tab = P.dma("sync", stt[:, :], sd, tsem)
                tabt.append((ct, stt))
        t_h = []
        for t in range(NT // 512):
            t_h.append(nu.run(xT, t * 512, gcol, lambda c, t=t: hT[:, c, t * 512:(t + 1) * 512]))
        f1 = Ring([sb(st, nc, f"{name}_f1{k}", [128, 512], F32) for k in range(2)])
        f2 = Ring([sb(st, nc, f"{name}_f2{k}", [128, 512], F32) for k in range(2)])
        og = Ring([sb(st, nc, f"{name}_og{k}", [128, 512], BF16) for k in range(3)])
        ssem = P.dma_sems("s", 3)
        last_store = []
        for (w_ap, wsw_ap, tab_idx, scale, out_dram) in fm_specs:
            k, wt, fr = wr.next()
            wv = w_ap.rearrange("(c p) f -> p c f", p=128)
            t_w = P.dma("gpsimd", wt[:, 0, :, :], wv, wsem[k], waits=fr)
            if wsw_ap is not None:
                t_w = P.dma("gpsimd", wt[:, 1, :, :], wsw_ap.rearrange("(c p) f -> p c f", p=128), wsem[k])
            tl = None
            for t in range(NT // 512):
                k1, p1t, fr1 = p1.next()
                ta = None
                for c in range(DC):
                    ta = P.pe(lambda e, c=c, wt=wt, p1t=p1t, t=t: e.matmul(p1t[:, :], wt[:, 0, c, :], hT[:, c, t * 512:(t + 1) * 512],
                                                                           start=(c == 0), stop=(c == DC - 1)),
                              waits=[t_w, t_h[t]] + fr1, sig=(c == DC - 1))
                tl = ta
                ko, ogt, fro = og.next()
                if wsw_ap is not None:
                    k2, p2t, fr2 = p2.next()
                    tb = None
                    for c in range(DC):
                        tb = P.pe(lambda e, c=c, wt=wt, p2t=p2t, t=t: e.matmul(p2t[:, :], wt[:, 1, c, :], hT[:, c, t * 512:(t + 1) * 512],
                                                                               start=(c == 0), stop=(c == DC - 1)),
                                  waits=fr2, sig=(c == DC - 1))
                    tl = tb
                    ct, stt = tabt[tab_idx]
                    kf1, f1t, frf1 = f1.next()
                    kf2, f2t, frf2 = f2.next()
                    td1 = P.dve(lambda e, f1t=f1t, p1t=p1t, ct=ct, t=t: e.tensor_tensor(out=f1t[:, :], in0=p1t[:, :], in1=ct[:, t * 512:(t + 1) * 512], op=ALU.mult),
                                waits=[ta, t_tab] + frf1, sig=True)
                    p1.release(k1, td1)
                    td2 = P.dve(lambda e, f2t=f2t, p2t=p2t, stt=stt, t=t: e.tensor_tensor(out=f2t[:, :], in0=p2t[:, :], in1=stt[:, t * 512:(t + 1) * 512], op=ALU.mult),
                                waits=[tb] + frf2, sig=True)
                    p2.release(k2, td2)
                    to = P.pool(lambda e, ogt=ogt, f1t=f1t, f2t=f2t: e.tensor_tensor(out=ogt[:, :], in0=f1t[:, :], in1=f2t[:, :], op=ALU.add),
                                waits=[td1, td2] + fro, sig=True)
                    f1.release(kf1, to)
                    f2.release(kf2, to)
                else:
                    to = P.act(lambda e, ogt=ogt, p1t=p1t, scale=scale: e.activation(out=ogt[:, :], in_=p1t[:, :], func=AF.Copy, scale=float(scale)),
                               waits=[ta] + fro, sig=True)
                    p1.release(k1, to)
                tst = P.dma("sync", out_dram[:, t * 512:(t + 1) * 512], ogt[:, :], ssem[ko], waits=[to])
                og.release(ko, tst)
                last_store.append(tst)
            wr.release(k, tl)
        if tm_specs:
            wtm = Ring([sb(st, nc, f"{name}_wtm{k}", [128, DC, 512], BF16) for k in range(2)])
            wtsem = P.dma_sems("wt", 2)
            otb = Ring([sb(st, nc, f"{name}_otb{k}", [128, 512], BF16) for k in range(2)])
            otf = Ring([sb(st, nc, f"{name}_otf{k}", [128, 512], F32) for k in range(2)])
            sbsem = P.dma_sems("sb", 2)
            sfsem = P.dma_sems("sf", 2)
            for (w_ap, func, out_dram, odt) in tm_specs:
                k, wt, fr = wtm.next()
                t_w = P.dma("gpsimd", wt[:, :, :], w_ap.rearrange("(c p) f -> p c f", p=128), wtsem[k], waits=fr)
                tl = None
                for tb_ in range(NT // 128):
                    k1, p1t, fr1 = p1.next()
                    ta = None
                    for c in range(DC):
                        ta = P.pe(lambda e, c=c, wt=wt, p1t=p1t, tb_=tb_: e.matmul(p1t[:, :], hT[:, c, tb_ * 128:(tb_ + 1) * 128], wt[:, c, :],
                                                                                   start=(c == 0), stop=(c == DC - 1)),
                                  waits=[t_w, t_h[tb_ // 4]] + fr1, sig=(c == DC - 1))
                    tl = ta
                    if odt == "bf16":
                        ko, ot, fro = otb.next()
                        sem = sbsem[ko]
                    else:
                        ko, ot, fro = otf.next()
                        sem = sfsem[ko]
                    to = P.act(lambda e, ot=ot, p1t=p1t, func=func: e.activation(out=ot[:, :], in_=p1t[:, :], func=func),
                               waits=[ta] + fro, sig=True)
                    p1.release(k1, to)
                    tst = P.dma("sync", out_dram[tb_ * 128:(tb_ + 1) * 128, :], ot[:, :], sem, waits=[to])
                    (otb if odt == "bf16" else otf).release(ko, tst)
                    last_store.append(tst)
                wtm.release(k, tl)
        P.op("sync", lambda e: e.nop(), waits=last_store[-8:])
        P.emit(blk)


def block_store_x(nc, cx, name, x_out):
    with contextlib.ExitStack() as st:
        P = Prog(nc, st, name)
        sem = P.dma_sems("st", 1)[0]
        blk = st.enter_context(nc.Block(name))
        xv = x_out.rearrange("(c p) t -> p c t", p=128)
        tok = None
        for c in range(DC):
            tok = P.dma("sync", xv[:, c, :], cx.xT[:, c, :], sem)
        P.op("sync", lambda e: e.nop(), waits=[tok])
        P.emit(blk)


def block_load_x(nc, cx, name, x_in, gains):
    xv = x_in.rearrange("(c p) t -> p c t", p=128)
    pairs = [(cx.xT[:, c, :], xv[:, c, :]) for c in range(DC)]
    pairs.append((cx.gT[:, :], gains))
    block_load(nc, name, pairs)


def block_final(nc, cx, name, gcol, y_out):
    with contextlib.ExitStack() as st:
        P = Prog(nc, st, name)
        nu = NormUnit(P, st, nc, name, cx.gT, cx.ones)
        yt = Ring([sb(st, nc, f"{name}_y{k}", [128, DC, 512], F32) for k in range(2)])
        ssem = P.dma_sems("s", 2)
        blk = st.enter_context(nc.Block(name))
        yv = y_out.rearrange("(c p) t -> p c t", p=128)
        last = []
        for t in range(NT // 512):
            k, y, fr = yt.next()
            th = nu.run(cx.xT, t * 512, gcol, lambda c, y=y: y[:, c, :], out_free=fr)
            tok = None
            for c in range(DC):
                tok = P.dma("sync", yv[:, c, t * 512:(t + 1) * 512], y[:, c, :], ssem[k], waits=[th])
            yt.release(k, tok)
            last.append(tok)
        P.op("sync", lambda e: e.nop(), waits=last[-2:])
        P.emit(blk)


def new_nc():
    return bass.Bass("TRN2", target_bir_lowering=False)


def setup_ctx(nc, st, name):
    cx = Ctx()
    cx.xT = sb(st, nc, name + "_xT", [128, DC, NT], F32)
    cx.gT = sb(st, nc, name + "_gT", [128, 56], F32)
    cx.ones, cx.eps = block_consts(nc, st, name)
    return cx


G_FFN1 = (0, 8)
G_MIX = (16, 24)
G_FFN2 = (32, 40)
G_FINAL = 48


def build_A():
    nc = new_nc()
    x_in = nc.dram_tensor("xT_in", [D, NT], F32, kind="ExternalInput").ap()
    gains = nc.dram_tensor("gains", [128, 56], F32, kind="ExternalInput").ap()
    w_in = nc.dram_tensor("w_in", [D, 2 * FF], F32, kind="ExternalInput").ap()
    w_out = nc.dram_tensor("w_out", [FF, D], F32, kind="ExternalInput").ap()
    w_hyb = nc.dram_tensor("w_hyb", [D, HYB_IN], F32, kind="ExternalInput").ap()
    w_hsw = nc.dram_tensor("w_hsw", [D, 2048], F32, kind="ExternalInput").ap()
    tabs_d = [nc.dram_tensor(n, [128, NT], F32, kind="ExternalInput").ap() for n in ("rc", "rs", "dc", "ds")]
    x_out = nc.dram_tensor("xT_out", [D, NT], F32, kind="ExternalOutput").ap()
    qk_fm = nc.dram_tensor("qk_fm", [2048, NT], BF16, kind="ExternalOutput").ap()
    v_tm = nc.dram_tensor("v_tm", [NT, 1024], BF16, kind="ExternalOutput").ap()
    g_tm = nc.dram_tensor("g_tm", [NT, 512], F32, kind="ExternalOutput").ap()
    with contextlib.ExitStack() as st:
        cx = setup_ctx(nc, st, "A")
        block_load_x(nc, cx, "A_ld", x_in, gains)
        block_ffn(nc, cx, "A_ffn", G_FFN1[0], w_in, w_out)
        block_store_x(nc, cx, "A_st", x_out)
        fm = []
        for i in range(4):
            fm.append((w_hyb[:, i * 128:(i + 1) * 128], w_hsw[:, i * 128:(i + 1) * 128], 0, 1.0, qk_fm[i * 128:(i + 1) * 128, :]))
        for i in range(4):
            fm.append((w_hyb[:, 512 + i * 128:512 + (i + 1) * 128], w_hsw[:, 512 + i * 128:512 + (i + 1) * 128], 0, 1.0,
                       qk_fm[512 + i * 128:512 + (i + 1) * 128, :]))
        for i in range(4):
            fm.append((w_hyb[:, 2048 + i * 128:2048 + (i + 1) * 128], w_hsw[:, 1024 + i * 128:1024 + (i + 1) * 128], 1, 1.0,
                       qk_fm[1024 + i * 128:1024 + (i + 1) * 128, :]))
        for i in range(4):
            fm.append((w_hyb[:, 2560 + i * 128:2560 + (i + 1) * 128], w_hsw[:, 1536 + i * 128:1536 + (i + 1) * 128], 1, 1.0,
                       qk_fm[1536 + i * 128:1536 + (i + 1) * 128, :]))
        tm = [
            (w_hyb[:, 1024:1536], AF.Copy, v_tm[:, 0:512], "bf16"),
            (w_hyb[:, 3072:3584], AF.Copy, v_tm[:, 512:1024], "bf16"),
            (w_hyb[:, 1536:2048], AF.Silu, g_tm[:, :], "f32"),
        ]
        block_proj(nc, cx, "A_pj", G_MIX[0], fm, tm, tabs=[(tabs_d[0], tabs_d[1]), (tabs_d[2], tabs_d[3])])
    return nc


def gains_table(inp):
    vecs = [inp["ffn1_norm"][0], inp["ffn1_norm"][1], inp["mix_norm"][0], inp["mix_norm"][1],
            inp["ffn2_norm"][0], inp["ffn2_norm"][1], inp["final_norm"]]
    cols = [np.asarray(v, np.float32).reshape(DC, 128).T for v in vecs]
    return np.ascontiguousarray(np.concatenate(cols, axis=1))


def rope_tables(rot_dim, theta, pos):
    half = rot_dim // 2
    inv_freq = (1.0 / (np.float32(theta) ** (np.arange(half, dtype=np.float32) / np.float32(half)))).astype(np.float32)
    ang = pos.astype(np.float32)[:, None] * inv_freq[None, :]
    cos, sin = np.cos(ang).astype(np.float32), np.sin(ang).astype(np.float32)
    C = np.ones((64, len(pos)), np.float32)
    Sg = np.zeros((64, len(pos)), np.float32)
    C[:half] = cos.T
    C[half:rot_dim] = cos.T
    Sg[:half] = -sin.T
    Sg[half:rot_dim] = sin.T
    return np.ascontiguousarray(np.concatenate([C, C], 0)), np.ascontiguousarray(np.concatenate([Sg, Sg], 0))


def swap_cols(w, rot_dim):
    half = rot_dim // 2
    n = w.shape[1] // 64
    idx = []
    for h in range(n):
        base = h * 64
        perm = list(range(64))
        for i in range(half):
            perm[i] = half + i
            perm[half + i] = i
        idx.extend(base + p for p in perm)
    return np.ascontiguousarray(w[:, idx])


def run_A(inp, xT_shards):
    nc = build_A()
    g = gains_table(inp)
    w_hyb = np.asarray(inp["hyb_w_in"][0], np.float32)
    w_hsw = np.concatenate([swap_cols(w_hyb[:, 0:512], 64), swap_cols(w_hyb[:, 512:1024], 64),
                            swap_cols(w_hyb[:, 2048:2560], 16), swap_cols(w_hyb[:, 2560:3072], 16)], axis=1)
    w_hsw = np.ascontiguousarray(w_hsw)
    maps = []
    for c in range(NCORE):
        pos = np.arange(NT) + (c % 4) * NT
        rc, rs = rope_tables(64, 10000.0, pos)
        dc, ds = rope_tables(16, 500000.0, pos)
        maps.append({"xT_in": xT_shards[c], "gains": g, "w_in": np.asarray(inp["ffn1_w_in"][0], np.float32),
                     "w_out": np.asarray(inp["ffn1_w_out"][0], np.float32), "w_hyb": w_hyb, "w_hsw": w_hsw,
                     "rc": rc, "rs": rs, "dc": dc, "ds": ds})
    res = run_bass_kernel_spmd(nc, maps, core_ids=list(range(NCORE)))
    return res.results


def shard_x(x):
    xf = np.asarray(x, np.float32).reshape(B * S, D)
    return [np.ascontiguousarray(xf[c * NT:(c + 1) * NT].T) for c in range(NCORE)]


def block_ret(nc, name, d):
    with contextlib.ExitStack() as st:
        P = Prog(nc, st, name)
        qT = sb(st, nc, name + "_qT", [128, S], BF16)
        kT = sb(st, nc, name + "_kT", [128, S], BF16)
        qd = sb(st, nc, name + "_qd", [128, S], BF16)
        v = sb(st, nc, name + "_v", [128, 64, 128], BF16)
        gate = sb(st, nc, name + "_gate", [128, 64, 128], F32)
        outT = sb(st, nc, name + "_outT", [128, S], BF16)
        decT = sb(st, nc, name + "_decT", [128, 256], F32)
        qdec = sb(st, nc, name + "_qdec", [128, 512], F32)
        kdec = sb(st, nc, name + "_kdec", [128, 2], F32)
        cdec = sb(st, nc, name + "_cdec", [128, 1], F32)
        gn = sb(st, nc, name + "_gn", [128, 128], F32)
        ident = sb(st, nc, name + "_ident", [128, 128], BF16)
        epsg = sb(st, nc, name + "_epsg", [128, 1], F32)
        S_f = sb(st, nc, name + "_Sf", [128, 64], F32)
        S_b = Ring([sb(st, nc, f"{name}_Sb{k}", [128, 64], BF16) for k in range(2)])
        PT = Ring([sb(st, nc, f"{name}_PT{k}", [128, 256], BF16) for k in range(2)])
        kd = Ring([sb(st, nc, f"{name}_kd{k}", [128, 128], BF16) for k in range(2)])
        stats = Ring([sb(st, nc, f"{name}_stats{k}", [128, 2, 6], F32) for k in range(2)])
        mv = Ring([sb(st, nc, f"{name}_mv{k}", [128, 2, 2], F32) for k in range(2)])
        sd = Ring([sb(st, nc, f"{name}_sd{k}", [128, 2], F32) for k in range(2)])
        rstd = Ring([sb(st, nc, f"{name}_rstd{k}", [128, 2], F32) for k in range(2)])
        y = Ring([sb(st, nc, f"{name}_y{k}", [128, 128], F32) for k in range(2)])
        m_tm = Ring([sb(st, nc, f"{name}_mtm{k}", [128, 128], BF16) for k in range(2)])
        scA = Ring([psb(st, nc, f"{name}_scA")])
        scB = Ring([psb(st, nc, f"{name}_scB")])
        tp = Ring([psb(st, nc, f"{name}_tp{k}", (128, 1024), BF16) for k in range(2)])
        obA = Ring([psb(st, nc, f"{name}_oA")])
        obB = Ring([psb(st, nc, f"{name}_oB")])
        kv = Ring([psb(st, nc, f"{name}_kv{k}") for k in range(2)])
        lsem = P.dma_sems("l", 1)[0]
        osem = P.dma_sems("o", 1)[0]
        blk = st.enter_context(nc.Block(name))
        P.dma("sync", qT[:, :], d["rq"], lsem)
        P.dma("sync", kT[:, :], d["rk"], lsem)
        P.dma("sync", v[:, :, :], d["rv"].rearrange("(n j) e -> j n e", j=128), lsem)
        P.dma("sync", gate[:, :, :], d["rg"].rearrange("(n j) e -> j n e", j=128), lsem)
        for t_, nm in ((decT, "decT"), (qdec, "qdec"), (kdec, "kdec"), (cdec, "cdec"), (gn, "gn"), (ident, "ident")):
            t_ld = P.dma("sync", t_[:, :], d[nm], lsem)
        P.dve(lambda e: e.memset(epsg[:, :], GN_EPS))
        if DBG.get('ret_chunks', 64) < 64:
            P.dve(lambda e: e.memset(outT[:, :], 0.0))
        t_qd = None
        for ch in range(16):
            t_qd = P.dve(lambda e, ch=ch: e.tensor_tensor(out=qd[:, ch * 512:(ch + 1) * 512], in0=qT[:, ch * 512:(ch + 1) * 512], in1=qdec[:, :], op=ALU.mult),
                         waits=[t_ld], sig=(ch == 15))
        NCH = DBG.get('ret_chunks', 64)
        cst = [dict() for _ in range(NCH)]
        sb_cur = [None]
        out_last = [None]

        def stage1(n):
            c_ = cst[n]
            cs = slice(n * 128, (n + 1) * 128)
            _, sctA, frscA = scA.next()
            _, sctB, frscB = scB.next()
            P.pe(lambda e, sctA=sctA, cs=cs: e.matmul(sctA[:, 0:128], kT[0:64, cs], qT[0:64, cs], start=True, stop=True), waits=[t_ld] + frscA + frscB)
            t_sc = P.pe(lambda e, sctB=sctB, cs=cs: e.matmul(sctB[:, 0:128], kT[64:128, cs], qT[64:128, cs], start=True, stop=True), sig=True)
            kpt, ptt, frpt = PT.next()
            P.dve(lambda e, ptt=ptt, sctA=sctA: e.tensor_tensor(out=ptt[:, 0:128], in0=sctA[:, 0:128], in1=decT[:, 0:128], op=ALU.mult),
                  waits=[t_sc] + frpt, free=True)
            t_pt = P.dve(lambda e, ptt=ptt, sctB=sctB: e.tensor_tensor(out=ptt[:, 128:256], in0=sctB[:, 0:128], in1=decT[:, 128:256], op=ALU.mult),
                         sig=True, free=True)
            scA.release(0, t_pt)
            scB.release(0, t_pt)
            ktp, tpt, frtp = tp.next()
            t_tp = P.pe(lambda e, tpt=tpt, cs=cs: e.transpose(tpt[:, 0:128], kT[:, cs], ident[:, :]), waits=frtp, sig=True)
            kkd, kdt, frkd = kd.next()
            P.dve(lambda e, kdt=kdt, tpt=tpt: e.tensor_scalar(out=kdt[:, 0:64], in0=tpt[:, 0:64], scalar1=kdec[:, 0:1], scalar2=None, op0=ALU.mult),
                  waits=[t_tp] + frkd, free=True)
            t_kd = P.dve(lambda e, kdt=kdt, tpt=tpt: e.tensor_scalar(out=kdt[:, 64:128], in0=tpt[:, 64:128], scalar1=kdec[:, 1:2], scalar2=None, op0=ALU.mult),
                         sig=True, free=True)
            tp.release(ktp, t_kd)
            c_.update(cs=cs, kpt=kpt, ptt=ptt, t_pt=t_pt, kkd=kkd, kdt=kdt, t_kd=t_kd)

        def stage2(n):
            c_ = cst[n]
            cs, kpt, ptt, t_pt, kkd, kdt, t_kd = c_["cs"], c_["kpt"], c_["ptt"], c_["t_pt"], c_["kkd"], c_["kdt"], c_["t_kd"]
            _, otA, froA = obA.next()
            _, otB, froB = obB.next()
            ots = (otA, otB)
            t_o = None
            for h in range(2):
                hs = slice(h * 64, (h + 1) * 64)
                ot = ots[h]
                t_o = P.pe(lambda e, ot=ot, ptt=ptt, n=n, h=h, hs=hs: e.matmul(ot[:, 0:64], ptt[:, h * 128:(h + 1) * 128], v[:, n, hs], start=True, stop=(n == 0)),
                           waits=[t_pt] + froA + froB, sig=(n == 0 and h == 1))
                if n > 0:
                    t_o = P.pe(lambda e, ot=ot, cs=cs, hs=hs, sbt=sb_cur[0][1]: e.matmul(ot[:, 0:64], qd[hs, cs], sbt[hs, :], start=False, stop=True),
                               waits=[t_qd, sb_cur[0][2]], sig=(h == 1))
            PT.release(kpt, t_o)
            if sb_cur[0] is not None:
                S_b.release(sb_cur[0][0], t_o)
            kkv, kvt, frkv = kv.next()
            P.pe(lambda e, kvt=kvt, kdt=kdt, n=n: e.matmul(kvt[0:64, 0:64], kdt[:, 0:64], v[:, n, 0:64], start=True, stop=True), waits=[t_kd] + frkv)
            t_kv = P.pe(lambda e, kvt=kvt, kdt=kdt, n=n: e.matmul(kvt[64:128, 0:64], kdt[:, 64:128], v[:, n, 64:128], start=True, stop=True), sig=True)
            kd.release(kkd, t_kv)
            if n < NCH - 1:
                if n == 0:
                    P.dve(lambda e, kvt=kvt: e.tensor_copy(out=S_f[:, :], in_=kvt[:, 0:64]), waits=[t_kv])
                else:
                    P.dve(lambda e, kvt=kvt: e.scalar_tensor_tensor(out=S_f[:, :], in0=S_f[:, :], scalar=cdec[:, 0:1], in1=kvt[:, 0:64], op0=ALU.mult, op1=ALU.add),
                          waits=[t_kv])
                ksb, sbt, frsb = S_b.next()
                t_sbn = P.dve(lambda e, sbt=sbt: e.tensor_copy(out=sbt[:, :], in_=S_f[:, :]), waits=frsb, sig=True)
                kv.release(kkv, t_sbn)
                sb_cur[0] = (ksb, sbt, t_sbn)
            kst, stt, _ = stats.next()
            kmv, mvt, frmv = mv.next()
            for h in range(2):
                P.dve(lambda e, stt=stt, ot=ots[h], h=h: e.bn_stats(out=stt[:, h, :], in_=ot[:, 0:64]), waits=[t_o])
            t_mv = None
            for h in range(2):
                t_mv = P.dve(lambda e, stt=stt, mvt=mvt, h=h: e.bn_aggr(out=mvt[:, h, :], in_=stt[:, h, :]), waits=frmv, sig=(h == 1))
            ksd, sdt, frsd = sd.next()
            t_sd = P.act(lambda e, sdt=sdt, mvt=mvt: e.activation(out=sdt[:, :], in_=mvt[:, :, 1], func=AF.Sqrt, bias=epsg[:, 0:1], scale=1.0),
                         waits=[t_mv] + frsd, sig=True, free=True)
            krs, rst, _ = rstd.next()
            P.dve(lambda e, rst=rst, sdt=sdt: e.reciprocal(out=rst[:, :], in_=sdt[:, :]), waits=[t_sd])
            ky, yt, fry = y.next()
            t_y = None
            for h in range(2):
                t_y = P.dve(lambda e, yt=yt, ot=ots[h], mvt=mvt, rst=rst, h=h: e.tensor_scalar(out=yt[:, h * 64:(h + 1) * 64], in0=ot[:, 0:64],
                                                                                              scalar1=mvt[:, h, 0:1], scalar2=rst[:, h:h + 1],
                                                                                              op0=ALU.subtract, op1=ALU.mult),
                            waits=fry, sig=(h == 1))
            obA.release(0, t_y)
            obB.release(0, t_y)
            sd.release(ksd, t_y)
            mv.release(kmv, t_y)
            t_y2 = P.dve(lambda e, yt=yt: e.tensor_tensor(out=yt[:, :], in0=yt[:, :], in1=gn[:, :], op=ALU.mult), sig=True)
            km, mt, frm = m_tm.next()
            t_m = P.pool(lambda e, mt=mt, yt=yt, n=n: e.tensor_tensor(out=mt[:, :], in0=yt[:, :], in1=gate[:, n, :], op=ALU.mult),
                         waits=[t_y2, t_ld] + frm, sig=True, free=True)
            y.release(ky, t_m)
            c_.update(km=km, mt=mt, t_m=t_m)

        def stage3(n):
            c_ = cst[n]
            cs, km, mt, t_m = c_["cs"], c_["km"], c_["mt"], c_["t_m"]
            ktp2, tpt2, frtp2 = tp.next()
            t_tp2 = P.pe(lambda e, tpt2=tpt2, mt=mt: e.transpose(tpt2[:, 0:128], mt[:, :], ident[:, :]), waits=[t_m] + frtp2, sig=True)
            m_tm.release(km, t_tp2)
            out_last[0] = P.act(lambda e, tpt2=tpt2, cs=cs: e.activation(out=outT[:, cs], in_=tpt2[:, 0:128], func=AF.Copy), waits=[t_tp2], sig=True, free=True)
            tp.release(ktp2, out_last[0])

        for i in range(-1, NCH + 1):
            if 0 <= i + 1 < NCH:
                stage1(i + 1)
            if 0 <= i < NCH:
                stage2(i)
            if 0 <= i - 1 < NCH:
                stage3(i - 1)
        t_out_last = out_last[0]
        t_st = P.dma("sync", d["ret_out"], outT[:, :], osem, waits=[t_out_last])
        P.op("sync", lambda e: e.nop(), waits=[t_st])
        P.emit(blk)


DIL = (1, 4, 16)


def block_dil(nc, name, d, ones):
    with contextlib.ExitStack() as st:
        P = Prog(nc, st, name)
        qT = sb(st, nc, name + "_qT", [128, S], BF16)
        kT = sb(st, nc, name + "_kT", [128, S], BF16)
        vp = [sb(st, nc, f"{name}_vp{p}", [128, 64, 128], BF16) for p in range(3)]
        acc = sb(st, nc, name + "_acc", [128, 2, S], F32)
        qDI = sb(st, nc, name + "_qDI", [128, S], BF16)
        kDI = sb(st, nc, name + "_kDI", [128, S], BF16)
        outT = qT
        dmask = sb(st, nc, name + "_dmask", [128, 512], BF16)
        sq = Ring([sb(st, nc, f"{name}_sq{k}", [128, 512], BF16) for k in range(2)])
        mx = sb(st, nc, name + "_mx", [128, 4, 16], F32)
        mx2 = sb(st, nc, name + "_mx2", [128, 4], F32)
        prod = sb(st, nc, name + "_prod", [128, 2], F32)
        bias = sb(st, nc, name + "_bias", [128, 2], F32)
        Pt = Ring([sb(st, nc, f"{name}_P{k}", [128, 512], BF16) for k in range(3)])
        spA = Ring([psb(st, nc, f"{name}_spA{k}") for k in range(2)])
        spB = Ring([psb(st, nc, f"{name}_spB{k}") for k in range(2)])
        nb_ = Ring([psb(st, nc, f"{name}_n{k}", (128, 4, 128)) for k in range(2)])
        nq = Ring([psb(st, nc, f"{name}_nq{k}") for k in range(2)])
        lsem = P.dma_sems("l", 1)[0]
        vsem = P.dma_sems("v", 1)[0]
        osem = P.dma_sems("o", 1)[0]
        blk = st.enter_context(nc.Block(name))
        P.dma("sync", qT[:, :], d["dq"], lsem)
        P.dma("sync", kT[:, :], d["dk"], lsem)
        t_ld = P.dma("sync", dmask[:, :], d["dmask"], lsem)
        t_v = None
        for p, dl in enumerate(DIL):
            nbc = 64 // dl
            src = d["dv"].rearrange("(nb i r) e -> r i nb e", i=128, r=dl)
            for r in range(dl):
                t_v = P.dma("sync", vp[p][:, r * nbc:(r + 1) * nbc, :], src[r], vsem)
        for ti, src_t in enumerate((qT, kT)):
            for ch in range(16):
                ks, sqt, frs = sq.next()
                t_sq = P.dve(lambda e, sqt=sqt, src_t=src_t, ch=ch: e.tensor_tensor(out=sqt[:, :], in0=src_t[:, ch * 512:(ch + 1) * 512],
                                                                                   in1=src_t[:, ch * 512:(ch + 1) * 512], op=ALU.mult),
                             waits=[t_ld] + frs, sig=True)
                t_last = None
                for h in range(2):
                    kn, nqt, frn = nq.next()
                    t_n = P.pe(lambda e, nqt=nqt, sqt=sqt, h=h: e.matmul(nqt[:, :], ones[h * 64:(h + 1) * 64, :], sqt[h * 64:(h + 1) * 64, :], start=True, stop=True),
                               waits=[t_sq] + frn, sig=True)
                    t_r = P.dve(lambda e, nqt=nqt, ti=ti, h=h, ch=ch: e.reduce_max(out=mx[:, ti * 2 + h, ch:ch + 1], in_=nqt[:, :], axis=AX.X),
                                waits=[t_n], sig=True)
                    nq.release(kn, t_r)
                    t_last = t_n
                sq.release(ks, t_last)
        P.dve(lambda e: e.reduce_max(out=mx2[:, :], in_=mx[:, :, :], axis=AX.X))
        t_p = P.dve(lambda e: e.tensor_tensor(out=prod[:, :], in0=mx2[:, 0:2], in1=mx2[:, 2:4], op=ALU.mult), sig=True)
        t_s = P.act(lambda e: e.activation(out=prod[:, :], in_=prod[:, :], func=AF.Sqrt), waits=[t_p], sig=True)
        t_bias = P.dve(lambda e: e.tensor_scalar(out=bias[:, :], in0=prod[:, :], scalar1=-1.02 / 8.0, scalar2=None, op0=ALU.mult), waits=[t_s], sig=True)
        units = []
        for p, dl in enumerate(DIL[:DBG.get('dil_patterns', 3)]):
            nbc = 64 // dl
            for r in range(dl):
                for nb in range(nbc):
                    units.append((p, dl, nbc, r, nb))
        ust = [dict() for _ in units]
        pe_last = [None]
        de_tok = {0: None}
        acc_last = [None]

        def stageA(i):
            p, dl, nbc, r, nb = units[i]
            L = S // dl
            u_ = ust[i]
            if p == 0:
                qs, ks_ = qT, kT
            else:
                qs, ks_ = qDI, kDI
                if p not in de_tok:
                    t_de = None
                    for rr in range(dl):
                        P.dve(lambda e, rr=rr, dl=dl, L=L: e.tensor_copy(out=qDI[:, rr * L:(rr + 1) * L], in_=qT[:, rr:rr + (L - 1) * dl + 1:dl]),
                              waits=[t_ld, pe_last[0]])
                        t_de = P.dve(lambda e, rr=rr, dl=dl, L=L: e.tensor_copy(out=kDI[:, rr * L:(rr + 1) * L], in_=kT[:, rr:rr + (L - 1) * dl + 1:dl]), sig=True)
                    de_tok[p] = t_de
            t_de = de_tok[p]
            ctoks = slice(r * L + nb * 128, r * L + (nb + 1) * 128)
            ptoks = slice(r * L + (nb - 1) * 128, r * L + nb * 128)
            kspA, sptA, frspA = spA.next()
            kspB, sptB, frspB = spB.next()
            spts = (sptA, sptB)
            t_s_ = None
            first = True
            for h in range(2):
                hs = slice(h * 64, (h + 1) * 64)
                spt = spts[h]
                t_s_ = P.pe(lambda e, spt=spt, hs=hs, ctoks=ctoks, qs=qs, ks_=ks_: e.matmul(spt[:, 0:128], ks_[hs, ctoks], qs[hs, ctoks], start=True, stop=True),
                            waits=([t_ld, t_de] + frspA + frspB) if first else (), sig=(nb == 0 and h == 1))
                first = False
                if nb > 0:
                    t_s_ = P.pe(lambda e, spt=spt, hs=hs, ctoks=ctoks, ptoks=ptoks, qs=qs, ks_=ks_: e.matmul(spt[:, 128:256], ks_[hs, ptoks], qs[hs, ctoks],
                                                                                                           start=True, stop=True), sig=(h == 1))
            pe_last[0] = t_s_
            kP, Ptt, frP = Pt.next()
            w = 256 if nb > 0 else 128
            t_e = None
            for h in range(2):
                t_e = P.act(lambda e, Ptt=Ptt, spt=spts[h], h=h, w=w: e.activation(out=Ptt[:, h * 256:h * 256 + w], in_=spt[:, 0:w], func=AF.Exp,
                                                                                  bias=bias[:, h:h + 1], scale=0.125),
                            waits=[t_s_, t_bias] + frP, sig=(h == 1), free=True)
            spA.release(kspA, t_e)
            spB.release(kspB, t_e)
            if nb > 0:
                t_m = P.pool(lambda e, Ptt=Ptt: e.tensor_tensor(out=Ptt[:, :], in0=Ptt[:, :], in1=dmask[:, :], op=ALU.mult), waits=[t_e, t_ld], sig=True, free=True)
            else:
                P.pool(lambda e, Ptt=Ptt: e.tensor_tensor(out=Ptt[:, 0:128], in0=Ptt[:, 0:128], in1=dmask[:, 0:128], op=ALU.mult), waits=[t_e, t_ld], free=True)
                t_m = P.pool(lambda e, Ptt=Ptt: e.tensor_tensor(out=Ptt[:, 256:384], in0=Ptt[:, 256:384], in1=dmask[:, 256:384], op=ALU.mult), sig=True, free=True)
            u_["kP"], u_["Ptt"], u_["t_m"] = kP, Ptt, t_m

        def stageB(i):
            p, dl, nbc, r, nb = units[i]
            u_ = ust[i]
            kP, Ptt, t_m = u_["kP"], u_["Ptt"], u_["t_m"]
            u = r * nbc + nb
            start = nb * 128 * dl + r
            toks = slice(start, start + 127 * dl + 1, dl)
            kn, nt, frn = nb_.next()
            t_n = None
            first = True
            for which in range(2):
                for h in range(2):
                    hs = slice(h * 64, (h + 1) * 64)
                    lhs_c = vp[p][:, u, hs] if which == 0 else ones[:, 0:64]
                    t_n = P.pe(lambda e, nt=nt, hs=hs, lhs_c=lhs_c, Ptt=Ptt, h=h, which=which, nb=nb: e.matmul(nt[hs, which, :], lhs_c, Ptt[:, h * 256:h * 256 + 128],
                                                                                                             start=True, stop=(nb == 0)),
                               waits=([t_m, t_v] + frn) if first else (), sig=(nb == 0 and which == 1 and h == 1))
                    first = False
                    if nb > 0:
                        lhs_p = vp[p][:, u - 1, hs] if which == 0 else ones[:, 0:64]
                        t_n = P.pe(lambda e, nt=nt, hs=hs, lhs_p=lhs_p, Ptt=Ptt, h=h, which=which: e.matmul(nt[hs, which, :], lhs_p, Ptt[:, h * 256 + 128:h * 256 + 256],
                                                                                                        start=False, stop=True),
                                   sig=(which == 1 and h == 1))
            Pt.release(kP, t_n)
            if p == 0:
                t_acc = P.dve(lambda e, nt=nt, toks=toks: e.tensor_copy(out=acc[:, :, toks], in_=nt[:, 0:2, :]), waits=[t_n], sig=True, free=True)
            else:
                t_acc = P.dve(lambda e, nt=nt, toks=toks: e.tensor_tensor(out=acc[:, :, toks], in0=acc[:, :, toks], in1=nt[:, 0:2, :], op=ALU.add),
                              waits=[t_n], sig=True, free=True, hard=[acc_last[0]])
            acc_last[0] = t_acc
            nb_.release(kn, t_acc)

        NU = len(units)
        for i in range(-1, NU):
            if i + 1 < NU:
                stageA(i + 1)
            if i >= 0:
                stageB(i)
        t_pe_last = pe_last[0]
        t_o = None
        for ch in range(4):
            cs = slice(ch * 2048, (ch + 1) * 2048)
            P.dve(lambda e, cs=cs: e.reciprocal(out=acc[:, 1, cs], in_=acc[:, 1, cs]), hard=[acc_last[0]])
            t_o = P.dve(lambda e, cs=cs: e.tensor_tensor(out=outT[:, cs], in0=acc[:, 0, cs], in1=acc[:, 1, cs], op=ALU.mult), waits=[t_pe_last], sig=True)
        t_st = P.dma("sync", d["dil_out"], outT[:, :], osem, waits=[t_o])
        P.op("sync", lambda e: e.nop(), waits=[t_st])
        P.emit(blk)


def ret_consts(g):
    i = np.arange(128, dtype=np.float64)
    decT = np.zeros((128, 256), np.float32)
    qdec = np.zeros((128, 512), np.float32)
    kdec = np.zeros((128, 2), np.float32)
    cdec = np.zeros((128, 1), np.float32)
    for hh in range(2):
        h = 2 * g + hh
        lg = np.log(1.0 - 2.0 ** (-5.0 - h))
        diff = i[None, :] - i[:, None]
        dm = np.where(diff >= 0, np.exp(np.maximum(diff, 0) * lg), 0.0) / 8.0
        decT[:, hh * 128:(hh + 1) * 128] = dm
        qd = np.exp((i + 1) * lg) / 8.0
        qdec[hh * 64:(hh + 1) * 64, :] = np.tile(qd, 4)[None, :]
        kdec[:, hh] = np.exp((127 - i) * lg)
        cdec[hh * 64:(hh + 1) * 64, 0] = np.exp(128 * lg)
    return decT, qdec, kdec, cdec


def dil_mask():
    k = np.arange(128)[:, None]
    q = np.arange(128)[None, :]
    cur = (k <= q).astype(np.float32)
    prev = (k >= q).astype(np.float32)
    m = np.concatenate([cur, prev, cur, prev], axis=1)
    return m.astype(NPBF)


def build_B(which="both"):
    nc = new_nc()
    d = {}
    d["rqk"] = nc.dram_tensor("rqk", [2, 128, S], BF16, kind="ExternalInput").ap()
    d["dqk"] = nc.dram_tensor("dqk", [2, 128, S], BF16, kind="ExternalInput").ap()
    d["rv"] = nc.dram_tensor("rv", [S, 128], BF16, kind="ExternalInput").ap()
    d["dv"] = nc.dram_tensor("dv", [S, 128], BF16, kind="ExternalInput").ap()
    d["rg"] = nc.dram_tensor("rg", [S, 128], F32, kind="ExternalInput").ap()
    d["decT"] = nc.dram_tensor("decT", [128, 256], F32, kind="ExternalInput").ap()
    d["qdec"] = nc.dram_tensor("qdec", [128, 512], F32, kind="ExternalInput").ap()
    d["kdec"] = nc.dram_tensor("kdec", [128, 2], F32, kind="ExternalInput").ap()
    d["cdec"] = nc.dram_tensor("cdec", [128, 1], F32, kind="ExternalInput").ap()
    d["gn"] = nc.dram_tensor("gn", [128, 128], F32, kind="ExternalInput").ap()
    d["ident"] = nc.dram_tensor("ident", [128, 128], BF16, kind="ExternalInput").ap()
    d["dmask"] = nc.dram_tensor("dmask", [128, 512], BF16, kind="ExternalInput").ap()
    d["mixT"] = nc.dram_tensor("mixT", [256, S], BF16, kind="ExternalOutput").ap()
    d["rq"], d["rk"], d["dq"], d["dk"] = d["rqk"][0], d["rqk"][1], d["dqk"][0], d["dqk"][1]
    d["ret_out"], d["dil_out"] = d["mixT"][0:128, :], d["mixT"][128:256, :]
    with contextlib.ExitStack() as st:
        ones, eps = block_consts(nc, st, "B")
        if which in ("both", "ret"):
            block_ret(nc, "B_ret", d)
        if which in ("both", "dil"):
            block_dil(nc, "B_dil", d, ones)
    return nc


def run_B(inp, resA, which="both"):
    nc = build_B(which)
    ident = np.eye(128, dtype=np.float32).astype(NPBF)
    dm = dil_mask()
    gnv = np.asarray(inp["ret_gn"][0], np.float32)
    maps = []
    for c in range(NCORE):
        b, g = c // 4, c % 4
        ra = [resA[b * 4 + i] for i in range(4)]
        qk = np.concatenate([np.asarray(r["qk_fm"]) for r in ra], axis=1)
        vt = np.concatenate([np.asarray(r["v_tm"]) for r in ra], axis=0)
        gt = np.concatenate([np.asarray(r["g_tm"]) for r in ra], axis=0)
        sl = slice(g * 128, (g + 1) * 128)
        rqk = np.ascontiguousarray(np.stack([qk[0:512][sl], qk[512:1024][sl]]))
        dqk = np.ascontiguousarray(np.stack([qk[1024:1536][sl], qk[1536:2048][sl]]))
        decT, qdec, kdec, cdec = ret_consts(g)
        maps.append({"rqk": rqk, "dqk": dqk,
                     "rv": np.ascontiguousarray(vt[:, g * 128:(g + 1) * 128]),
                     "dv": np.ascontiguousarray(vt[:, 512 + g * 128:512 + (g + 1) * 128]),
                     "rg": np.ascontiguousarray(gt[:, g * 128:(g + 1) * 128]),
                     "decT": decT, "qdec": qdec, "kdec": kdec, "cdec": cdec,
                     "gn": np.ascontiguousarray(np.broadcast_to(gnv[g * 128:(g + 1) * 128][None, :], (128, 128))),
                     "ident": ident, "dmask": dm})
    res = run_bass_kernel_spmd(nc, maps, core_ids=list(range(NCORE)))
    return res.results


def mix_from_B(resB):
    out = []
    for c in range(NCORE):
        b, ts = c // 4, c % 4
        tsl = slice(ts * NT, (ts + 1) * NT)
        ret = np.concatenate([np.asarray(resB[b * 4 + g]["mixT"])[0:128, tsl] for g in range(4)], axis=0)
        dil = np.concatenate([np.asarray(resB[b * 4 + g]["mixT"])[128:256, tsl] for g in range(4)], axis=0)
        out.append(np.ascontiguousarray(np.concatenate([ret, dil], axis=0)))
    return out


def block_sb(nc, name, d):
    with contextlib.ExitStack() as st:
        P = Prog(nc, st, name)
        qT = [sb(st, nc, f"{name}_qT{p}", [128, S], BF16) for p in range(2)]
        kT = [sb(st, nc, f"{name}_kT{p}", [128, S], BF16) for p in range(2)]
        v = sb(st, nc, name + "_v", [128, 64, 256], BF16)
        outT = [sb(st, nc, f"{name}_oT{p}", [128, S], BF16) for p in range(2)]
        masks = sb(st, nc, name + "_masks", [128, 4, 512], BF16)
        negU = sb(st, nc, name + "_negU", [128, 128], BF16)
        negones = sb(st, nc, name + "_negones", [128, 128], BF16)
        Eb = Ring([sb(st, nc, f"{name}_E{k}", [128, 1024], F32) for k in range(2)])
        Lb = Ring([sb(st, nc, f"{name}_L{k}", [128, 1024], BF16) for k in range(3)])
        Ab = Ring([sb(st, nc, f"{name}_A{k}", [128, 1024], BF16) for k in range(2)])
        Accb = Ring([sb(st, nc, f"{name}_Acc{k}", [128, 1024], BF16) for k in range(3)])
        Zb = Ring([psb(st, nc, f"{name}_Z{k}", (128, 1024)) for k in range(3)])
        Ob = Ring([psb(st, nc, f"{name}_O{k}") for k in range(2)])
        lsem = P.dma_sems("l", 1)[0]
        osem = P.dma_sems("o", 1)[0]
        blk = st.enter_context(nc.Block(name))
        for p in range(2):
            P.dma("sync", qT[p][:, :], d["q"][p], lsem)
            P.dma("sync", kT[p][:, :], d["k"][p], lsem)
        P.dma("sync", v[:, :, :], d["v"].rearrange("(n j) e -> j n e", j=128), lsem)
        P.dma("sync", masks[:, :, :], d["masks"], lsem)
        t_ld = P.dma("sync", negU[:, :], d["negU"], lsem)
        t_c = P.dve(lambda e: e.memset(negones[:, :], -1.0), sig=True)

        nqt = DBG.get("sb_qt", 16)
        npair = DBG.get("sb_pairs", 2)
        if nqt < 16 or npair < 2:
            for p in range(2):
                P.dve(lambda e, p=p: e.memset(outT[p][:, :], 0.0))
        blocks = []
        for p in range(npair):
            for qt in range(nqt):
                kbs = list(range(4 * qt + 3, -1, -1))
                for kb in kbs:
                    a = kb - 4 * qt
                    blocks.append(dict(p=p, qt=qt, kb=kb, a=(a if a >= 0 else None), first=(kb == kbs[0]), last=(kb == 0)))
        N = len(blocks)
        stt = [dict() for _ in range(N)]
        acc_cur = [None]
        o_cur = [None]
        last_evacs = []
        H2 = (slice(0, 512), slice(512, 1024))

        def st1(i):
            b = blocks[i]
            s_ = stt[i]
            kz, zt, frz = Zb.next()
            p = b["p"]
            qs = slice(b["qt"] * 512, (b["qt"] + 1) * 512)
            ks = slice(b["kb"] * 128, (b["kb"] + 1) * 128)
            P.pe(lambda e, zt=zt, p=p, qs=qs, ks=ks: e.matmul(zt[:, 0:512], kT[p][0:64, ks], qT[p][0:64, qs], start=True, stop=True),
                 waits=[t_ld] + frz)
            s_["t_qk"] = P.pe(lambda e, zt=zt, p=p, qs=qs, ks=ks: e.matmul(zt[:, 512:1024], kT[p][64:128, ks], qT[p][64:128, qs], start=True, stop=True),
                              sig=True)
            s_["kz"], s_["zt"] = kz, zt

        def st2(i):
            b = blocks[i]
            s_ = stt[i]
            zt = s_["zt"]
            ke, et, fre = Eb.next()
            t_e = P.act(lambda e, et=et, zt=zt: e.activation(out=et[:, :], in_=zt[:, :], func=AF.Exp), waits=[s_["t_qk"]] + fre, sig=True, free=True)
            s_["ke"], s_["et"], s_["t_e"] = ke, et, t_e

        def st2b(i):
            b = blocks[i]
            s_ = stt[i]
            ke, et, t_e = s_["ke"], s_["et"], s_["t_e"]
            kl, lt, frl = Lb.next()
            t_l = P.act(lambda e, lt=lt, et=et: e.activation(out=lt[:, :], in_=et[:, :], func=AF.Ln, bias=1.0, scale=1.0), waits=frl, sig=True,
                        free=True, hard=[t_e])
            Eb.release(ke, t_l)
            if b["a"] is not None:
                a = b["a"]
                P.dve(lambda e, lt=lt, a=a: e.tensor_tensor(out=lt[:, 0:512], in0=lt[:, 0:512], in1=masks[:, a, :], op=ALU.mult), waits=[t_l, t_ld], free=True)
                t_l = P.dve(lambda e, lt=lt, a=a: e.tensor_tensor(out=lt[:, 512:1024], in0=lt[:, 512:1024], in1=masks[:, a, :], op=ALU.mult), sig=True, free=True)
            s_["kl"], s_["lt"], s_["t_l"] = kl, lt, t_l
            s_["acc"] = None if b["first"] else acc_cur[0]
            if not b["last"]:
                ka, at, fra = Accb.next()
                if b["first"]:
                    t_a = P.pool(lambda e, at=at, lt=lt: e.tensor_copy(out=at[:, :], in_=lt[:, :]), waits=[t_l] + fra, sig=True, free=True)
                else:
                    prev = acc_cur[0]
                    t_a = P.pool(lambda e, at=at, lt=lt, pt=prev[1]: e.tensor_tensor(out=at[:, :], in0=pt[:, :], in1=lt[:, :], op=ALU.add),
                                 waits=[t_l, prev[2]] + fra, sig=True)
                acc_cur[0] = (ka, at, t_a)
                s_["t_accupd"] = t_a
            else:
                s_["t_accupd"] = None

        def st3(i):
            b = blocks[i]
            s_ = stt[i]
            zt, lt = s_["zt"], s_["lt"]
            t_u = None
            for hh in range(2):
                t_u = P.pe(lambda e, zt=zt, lt=lt, hh=hh: e.matmul(zt[:, H2[hh]], negU[:, :], lt[:, H2[hh]], start=False, stop=True, skip_group_check=True),
                           waits=[s_["t_l"], t_c], sig=(hh == 1))
            if s_["acc"] is not None:
                ka, at, t_a = s_["acc"]
                for hh in range(2):
                    t_u = P.pe(lambda e, zt=zt, at=at, hh=hh: e.matmul(zt[:, H2[hh]], negones[:, :], at[:, H2[hh]], start=False, stop=True, skip_group_check=True),
                               waits=[t_a], sig=(hh == 1))
                Accb.release(ka, t_u)
                if s_["t_accupd"] is not None:
                    Accb.release(ka, s_["t_accupd"])
            s_["t_u"] = t_u
            rel = [t_u]
            if s_["t_accupd"] is not None:
                rel.append(s_["t_accupd"])
            Lb.release(s_["kl"], *rel)

        def st4(i):
            b = blocks[i]
            s_ = stt[i]
            zt = s_["zt"]
            kA, At, frA = Ab.next()
            t_A = P.act(lambda e, At=At, zt=zt: e.activation(out=At[:, :], in_=zt[:, :], func=AF.Exp), waits=[s_["t_u"]] + frA, sig=True, free=True)
            Zb.release(s_["kz"], t_A)
            if b["a"] is not None:
                a = b["a"]
                P.dve(lambda e, At=At, a=a: e.tensor_tensor(out=At[:, 0:512], in0=At[:, 0:512], in1=masks[:, a, :], op=ALU.mult), waits=[t_A], free=True)
                t_A = P.dve(lambda e, At=At, a=a: e.tensor_tensor(out=At[:, 512:1024], in0=At[:, 512:1024], in1=masks[:, a, :], op=ALU.mult), sig=True, free=True)
            s_["kA"], s_["At"], s_["t_A"] = kA, At, t_A

        def st5(i):
            b = blocks[i]
            s_ = stt[i]
            At = s_["At"]
            p = b["p"]
            if b["first"]:
                ko, ot, fro = Ob.next()
                o_cur[0] = (ko, ot, fro)
            ko, ot, fro = o_cur[0]
            t_av = None
            for hh in range(2):
                hs = slice(hh * 64, (hh + 1) * 64)
                hcol = slice((2 * p + hh) * 64, (2 * p + hh + 1) * 64)
                t_av = P.pe(lambda e, ot=ot, hs=hs, At=At, kb=b["kb"], hcol=hcol, hh=hh, first=b["first"], last=b["last"]:
                            e.matmul(ot[hs, :], v[:, kb, hcol], At[:, H2[hh]], start=first, stop=last, skip_group_check=True),
                            waits=[s_["t_A"]] + (fro if b["first"] else []), sig=(hh == 1))
            Ab.release(s_["kA"], t_av)
            if b["last"]:
                qs = slice(b["qt"] * 512, (b["qt"] + 1) * 512)
                t_ev = P.dve(lambda e, ot=ot, p=p, qs=qs: e.tensor_copy(out=outT[p][:, qs], in_=ot[:, :]), waits=[t_av], sig=True, free=True)
                Ob.release(ko, t_ev)
                last_evacs.append(t_ev)

        for s_i in range(-3, N):
            if 0 <= s_i + 2 < N:
                st2(s_i + 2)
            if 0 <= s_i < N:
                st4(s_i)
            if 0 <= s_i + 2 < N:
                st2b(s_i + 2)
            if 0 <= s_i < N:
                st5(s_i)
            if 0 <= s_i + 3 < N:
                st1(s_i + 3)
            if 0 <= s_i + 2 < N:
                st3(s_i + 2)
        toks = []
        for p in range(2):
            toks.append(P.dma("sync", d["o"][p], outT[p][:, :], osem, waits=last_evacs[-2:]))
        P.op("sync", lambda e: e.nop(), waits=toks)
        P.emit(blk)


def sb_consts():
    m = np.zeros((128, 4, 512), np.float32)
    i = np.arange(128)[:, None]
    j = np.arange(512)[None, :]
    for a in range(4):
        m[:, a, :] = ((a * 128 + i) < j).astype(np.float32)
    jj = np.arange(128)[:, None]
    ss = np.arange(128)[None, :]
    negU = -(jj >= ss).astype(np.float32)
    return m.astype(NPBF), negU.astype(NPBF)


def build_D():
    nc = new_nc()
    d = {}
    d["qk"] = nc.dram_tensor("qk", [2, 2, 128, S], BF16, kind="ExternalInput").ap()
    d["v"] = nc.dram_tensor("v", [S, 256], BF16, kind="ExternalInput").ap()
    d["masks"] = nc.dram_tensor("masks", [128, 4, 512], BF16, kind="ExternalInput").ap()
    d["negU"] = nc.dram_tensor("negU", [128, 128], BF16, kind="ExternalInput").ap()
    d["oT"] = nc.dram_tensor("oT", [256, S], BF16, kind="ExternalOutput").ap()
    d["q"] = [d["qk"][0, p] for p in range(2)]
    d["k"] = [d["qk"][1, p] for p in range(2)]
    d["o"] = [d["oT"][p * 128:(p + 1) * 128, :] for p in range(2)]
    block_sb(nc, "D_sb", d)
    return nc


def run_D(resC):
    nc = build_D()
    masks, negU = sb_consts()
    maps = []
    for c in range(NCORE):
        b, g = c // 4, c % 4
        rc = [resC[b * 4 + i] for i in range(4)]
        qk = np.concatenate([np.asarray(r["qk_fm"]) for r in rc], axis=1)
        vt = np.concatenate([np.asarray(r["v_tm"]) for r in rc], axis=0)
        q = qk[0:1024][g * 256:(g + 1) * 256].reshape(2, 128, S)
        k = qk[1024:2048][g * 256:(g + 1) * 256].reshape(2, 128, S)
        maps.append({"qk": np.ascontiguousarray(np.stack([q, k])), "v": np.ascontiguousarray(vt[:, g * 256:(g + 1) * 256]),
                     "masks": masks, "negU": negU})
    res = run_bass_kernel_spmd(nc, maps, core_ids=list(range(NCORE)))
    return res.results


def mix_from_D(resD):
    out = []
    for c in range(NCORE):
        b, ts = c // 4, c % 4
        tsl = slice(ts * NT, (ts + 1) * NT)
        out.append(np.ascontiguousarray(np.concatenate([np.asarray(resD[b * 4 + g]["oT"])[:, tsl] for g in range(4)], axis=0)))
    return out


def build_C():
    nc = new_nc()
    x_in = nc.dram_tensor("xT_in", [D, NT], F32, kind="ExternalInput").ap()
    gains = nc.dram_tensor("gains", [128, 56], F32, kind="ExternalInput").ap()
    mix = nc.dram_tensor("mix", [D, NT], BF16, kind="ExternalInput").ap()
    w_ho = nc.dram_tensor("w_ho", [D, D], F32, kind="ExternalInput").ap()
    w_in2 = nc.dram_tensor("w_in2", [D, 2 * FF], F32, kind="ExternalInput").ap()
    w_out2 = nc.dram_tensor("w_out2", [FF, D], F32, kind="ExternalInput").ap()
    w_in1 = nc.dram_tensor("w_in1", [D, 2 * FF], F32, kind="ExternalInput").ap()
    w_out1 = nc.dram_tensor("w_out1", [FF, D], F32, kind="ExternalInput").ap()
    w_sb = nc.dram_tensor("w_sb", [D, 3 * D], F32, kind="ExternalInput").ap()
    x_out = nc.dram_tensor("xT_out", [D, NT], F32, kind="ExternalOutput").ap()
    qk_fm = nc.dram_tensor("qk_fm", [2048, NT], BF16, kind="ExternalOutput").ap()
    v_tm = nc.dram_tensor("v_tm", [NT, 1024], BF16, kind="ExternalOutput").ap()
    with contextlib.ExitStack() as st:
        cx = setup_ctx(nc, st, "C")
        block_load_x(nc, cx, "C_ld", x_in, gains)
        block_outproj(nc, cx, "C_op", mix, w_ho)
        block_ffn(nc, cx, "C_f2", G_FFN2[0], w_in2, w_out2)
        block_ffn(nc, cx, "C_f1", G_FFN1[1], w_in1, w_out1)
        block_store_x(nc, cx, "C_st", x_out)
        fm = []
        for i in range(8):
            fm.append((w_sb[:, i * 128:(i + 1) * 128], None, None, 0.125, qk_fm[i * 128:(i + 1) * 128, :]))
        for i in range(8):
            fm.append((w_sb[:, 1024 + i * 128:1024 + (i + 1) * 128], None, None, 1.0, qk_fm[1024 + i * 128:1024 + (i + 1) * 128, :]))
        tm = [(w_sb[:, 2048:2560], AF.Copy, v_tm[:, 0:512], "bf16"), (w_sb[:, 2560:3072], AF.Copy, v_tm[:, 512:1024], "bf16")]
        block_proj(nc, cx, "C_pj", G_MIX[1], fm, tm)
    return nc


def run_C(inp, xT_shards, mix_shards):
    nc = build_C()
    g = gains_table(inp)
    maps = []
    for c in range(NCORE):
        maps.append({"xT_in": xT_shards[c], "gains": g, "mix": mix_shards[c],
                     "w_ho": np.asarray(inp["hyb_w_out"][0], np.float32),
                     "w_in2": np.asarray(inp["ffn2_w_in"][0], np.float32), "w_out2": np.asarray(inp["ffn2_w_out"][0], np.float32),
                     "w_in1": np.asarray(inp["ffn1_w_in"][1], np.float32), "w_out1": np.asarray(inp["ffn1_w_out"][1], np.float32),
                     "w_sb": np.asarray(inp["sb_w_in"][0], np.float32)})
    res = run_bass_kernel_spmd(nc, maps, core_ids=list(range(NCORE)))
    return res.results


def build_E():
    nc = new_nc()
    x_in = nc.dram_tensor("xT_in", [D, NT], F32, kind="ExternalInput").ap()
    gains = nc.dram_tensor("gains", [128, 56], F32, kind="ExternalInput").ap()
    mix = nc.dram_tensor("mix", [D, NT], BF16, kind="ExternalInput").ap()
    w_so = nc.dram_tensor("w_so", [D, D], F32, kind="ExternalInput").ap()
    w_in2 = nc.dram_tensor("w_in2", [D, 2 * FF], F32, kind="ExternalInput").ap()
    w_out2 = nc.dram_tensor("w_out2", [FF, D], F32, kind="ExternalInput").ap()
    y_out = nc.dram_tensor("yT_out", [D, NT], F32, kind="ExternalOutput").ap()
    with contextlib.ExitStack() as st:
        cx = setup_ctx(nc, st, "E")
        block_load_x(nc, cx, "E_ld", x_in, gains)
        block_outproj(nc, cx, "E_op", mix, w_so)
        block_ffn(nc, cx, "E_f2", G_FFN2[1], w_in2, w_out2)
        block_final(nc, cx, "E_fin", G_FINAL, y_out)
    return nc


def run_E(inp, xT_shards, mix_shards):
    nc = build_E()
    g = gains_table(inp)
    maps = []
    for c in range(NCORE):
        maps.append({"xT_in": xT_shards[c], "gains": g, "mix": mix_shards[c],
                     "w_so": np.asarray(inp["sb_w_out"][0], np.float32),
                     "w_in2": np.asarray(inp["ffn2_w_in"][1], np.float32), "w_out2": np.asarray(inp["ffn2_w_out"][1], np.float32)})
    res = run_bass_kernel_spmd(nc, maps, core_ids=list(range(NCORE)))
    return res.results


def kernel(**inp):
    inp = {k: np.asarray(v) for k, v in inp.items()}
    xs = shard_x(inp["x"])
    resA = run_A(inp, xs)
    resB = run_B(inp, resA)
    xs1 = [np.asarray(r["xT_out"]) for r in resA]
    resC = run_C(inp, xs1, mix_from_B(resB))
    resD = run_D(resC)
    xs2 = [np.asarray(r["xT_out"]) for r in resC]
    resE = run_E(inp, xs2, mix_from_D(resD))
    y = np.concatenate([np.asarray(r["yT_out"], np.float32).T for r in resE], axis=0)
    return np.ascontiguousarray(y.reshape(B, S, D).astype(np.float32))


def build_fused():
    nc = new_nc()
    dt_ = nc.dram_tensor
    x_in = dt_("xT_in", [D, S], F32, kind="ExternalInput").ap()
    gains = dt_("gains", [128, 56], F32, kind="ExternalInput").ap()
    w_in1 = [dt_(f"w_in1_{l}", [D, 2 * FF], F32, kind="ExternalInput").ap() for l in range(2)]
    w_out1 = [dt_(f"w_out1_{l}", [FF, D], F32, kind="ExternalInput").ap() for l in range(2)]
    w_in2 = [dt_(f"w_in2_{l}", [D, 2 * FF], F32, kind="ExternalInput").ap() for l in range(2)]
    w_out2 = [dt_(f"w_out2_{l}", [FF, D], F32, kind="ExternalInput").ap() for l in range(2)]
    w_hyb = dt_("w_hyb", [D, HYB_IN], F32, kind="ExternalInput").ap()
    w_hsw = dt_("w_hsw", [D, 2048], F32, kind="ExternalInput").ap()
    w_ho = dt_("w_ho", [D, D], F32, kind="ExternalInput").ap()
    w_sb = dt_("w_sb", [D, 3 * D], F32, kind="ExternalInput").ap()
    w_so = dt_("w_so", [D, D], F32, kind="ExternalInput").ap()
    tabs_d = [dt_(n, [4, 128, NT], F32, kind="ExternalInput").ap() for n in ("rc", "rs", "dc", "ds")]
    decT = dt_("decT", [4, 128, 256], F32, kind="ExternalInput").ap()
    qdec = dt_("qdec", [4, 128, 512], F32, kind="ExternalInput").ap()
    kdec = dt_("kdec", [4, 128, 2], F32, kind="ExternalInput").ap()
    cdec = dt_("cdec", [4, 128, 1], F32, kind="ExternalInput").ap()
    gn = dt_("gn", [4, 128, 128], F32, kind="ExternalInput").ap()
    ident = dt_("ident", [128, 128], BF16, kind="ExternalInput").ap()
    dmask = dt_("dmask", [128, 512], BF16, kind="ExternalInput").ap()
    masks = dt_("masks", [128, 4, 512], BF16, kind="ExternalInput").ap()
    negU = dt_("negU", [128, 128], BF16, kind="ExternalInput").ap()
    y_out = dt_("yT_out", [D, S], F32, kind="ExternalOutput").ap()
    x_scr = dt_("x_scr", [D, S], F32, kind="Internal").ap()
    qk_scr = dt_("qk_scr", [2048, S], BF16, kind="Internal").ap()
    v_scr = dt_("v_scr", [S, 1024], BF16, kind="Internal").ap()
    g_scr = dt_("g_scr", [S, 512], F32, kind="Internal").ap()
    mix_scr = dt_("mix_scr", [D, S], BF16, kind="Internal").ap()
    NQ = DBG.get("fused_quarters", 4)

    def qs(q):
        return slice(q * NT, (q + 1) * NT)

    PH = DBG.get('fused_phases', 'ABCDE')
    NG = DBG.get('fused_groups', 4)
    for q in range(NQ if 'A' in PH else 0):
        with contextlib.ExitStack() as st:
            cx = setup_ctx(nc, st, f"A{q}")
            block_load_x(nc, cx, f"A{q}_ld", x_in[:, qs(q)], gains)
            block_ffn(nc, cx, f"A{q}_ffn", G_FFN1[0], w_in1[0], w_out1[0])
            block_store_x(nc, cx, f"A{q}_st", x_scr[:, qs(q)])
            fm = []
            for i in range(4):
                fm.append((w_hyb[:, i * 128:(i + 1) * 128], w_hsw[:, i * 128:(i + 1) * 128], 0, 1.0, qk_scr[i * 128:(i + 1) * 128, qs(q)]))
            for i in range(4):
                fm.append((w_hyb[:, 512 + i * 128:512 + (i + 1) * 128], w_hsw[:, 512 + i * 128:512 + (i + 1) * 128], 0, 1.0,
                           qk_scr[512 + i * 128:512 + (i + 1) * 128, qs(q)]))
            for i in range(4):
                fm.append((w_hyb[:, 2048 + i * 128:2048 + (i + 1) * 128], w_hsw[:, 1024 + i * 128:1024 + (i + 1) * 128], 1, 1.0,
                           qk_scr[1024 + i * 128:1024 + (i + 1) * 128, qs(q)]))
            for i in range(4):
                fm.append((w_hyb[:, 2560 + i * 128:2560 + (i + 1) * 128], w_hsw[:, 1536 + i * 128:1536 + (i + 1) * 128], 1, 1.0,
                           qk_scr[1536 + i * 128:1536 + (i + 1) * 128, qs(q)]))
            tm = [
                (w_hyb[:, 1024:1536], AF.Copy, v_scr[qs(q), 0:512], "bf16"),
                (w_hyb[:, 3072:3584], AF.Copy, v_scr[qs(q), 512:1024], "bf16"),
                (w_hyb[:, 1536:2048], AF.Silu, g_scr[qs(q), :], "f32"),
            ]
            block_proj(nc, cx, f"A{q}_pj", G_MIX[0], fm, tm, tabs=[(tabs_d[0][q], tabs_d[1][q]), (tabs_d[2][q], tabs_d[3][q])])
    for g in range(NG if 'B' in PH else 0):
        with contextlib.ExitStack() as st:
            ones, eps = block_consts(nc, st, f"B{g}")
            d = {"rq": qk_scr[g * 128:(g + 1) * 128, :], "rk": qk_scr[512 + g * 128:512 + (g + 1) * 128, :],
                 "dq": qk_scr[1024 + g * 128:1024 + (g + 1) * 128, :], "dk": qk_scr[1536 + g * 128:1536 + (g + 1) * 128, :],
                 "rv": v_scr[:, g * 128:(g + 1) * 128], "dv": v_scr[:, 512 + g * 128:512 + (g + 1) * 128], "rg": g_scr[:, g * 128:(g + 1) * 128],
                 "decT": decT[g], "qdec": qdec[g], "kdec": kdec[g], "cdec": cdec[g], "gn": gn[g], "ident": ident, "dmask": dmask,
                 "ret_out": mix_scr[g * 128:(g + 1) * 128, :], "dil_out": mix_scr[512 + g * 128:512 + (g + 1) * 128, :]}
            block_ret(nc, f"B{g}_ret", d)
            block_dil(nc, f"B{g}_dil", d, ones)
    for q in range(NQ if 'C' in PH else 0):
        with contextlib.ExitStack() as st:
            cx = setup_ctx(nc, st, f"C{q}")
            block_load_x(nc, cx, f"C{q}_ld", x_scr[:, qs(q)], gains)
            block_outproj(nc, cx, f"C{q}_op", mix_scr[:, qs(q)], w_ho)
            block_ffn(nc, cx, f"C{q}_f2", G_FFN2[0], w_in2[0], w_out2[0])
            block_ffn(nc, cx, f"C{q}_f1", G_FFN1[1], w_in1[1], w_out1[1])
            block_store_x(nc, cx, f"C{q}_st", x_scr[:, qs(q)])
            fm = []
            for i in range(8):
                fm.append((w_sb[:, i * 128:(i + 1) * 128], None, None, 0.125, qk_scr[i * 128:(i + 1) * 128, qs(q)]))
            for i in range(8):
                fm.append((w_sb[:, 1024 + i * 128:1024 + (i + 1) * 128], None, None, 1.0, qk_scr[1024 + i * 128:1024 + (i + 1) * 128, qs(q)]))
            tm = [(w_sb[:, 2048:2560], AF.Copy, v_scr[qs(q), 0:512], "bf16"), (w_sb[:, 2560:3072], AF.Copy, v_scr[qs(q), 512:1024], "bf16")]
            block_proj(nc, cx, f"C{q}_pj", G_MIX[1], fm, tm)
    for g in range(NG if 'D' in PH else 0):
        d = {"q": [qk_scr[g * 256 + p * 128:g * 256 + (p + 1) * 128, :] for p in range(2)],
             "k": [qk_scr[1024 + g * 256 + p * 128:1024 + g * 256 + (p + 1) * 128, :] for p in range(2)],
             "v": v_scr[:, g * 256:(g + 1) * 256], "masks": masks, "negU": negU,
             "o": [mix_scr[g * 256 + p * 128:g * 256 + (p + 1) * 128, :] for p in range(2)]}
        block_sb(nc, f"D{g}_sb", d)
    for q in range(NQ if 'E' in PH else 0):
        with contextlib.ExitStack() as st:
            cx = setup_ctx(nc, st, f"E{q}")
            block_load_x(nc, cx, f"E{q}_ld", x_scr[:, qs(q)], gains)
            block_outproj(nc, cx, f"E{q}_op", mix_scr[:, qs(q)], w_so)
            block_ffn(nc, cx, f"E{q}_f2", G_FFN2[1], w_in2[1], w_out2[1])
            block_final(nc, cx, f"E{q}_fin", G_FINAL, y_out[:, qs(q)])
    return nc


def fused_inputs(inp):
    g = gains_table(inp)
    w_hyb = np.asarray(inp["hyb_w_in"][0], np.float32)
    w_hsw = np.ascontiguousarray(np.concatenate([swap_cols(w_hyb[:, 0:512], 64), swap_cols(w_hyb[:, 512:1024], 64),
                                                 swap_cols(w_hyb[:, 2048:2560], 16), swap_cols(w_hyb[:, 2560:3072], 16)], axis=1))
    tabs = {k: [] for k in ("rc", "rs", "dc", "ds")}
    for q in range(4):
        pos = np.arange(NT) + q * NT
        rc, rs = rope_tables(64, 10000.0, pos)
        dc, ds = rope_tables(16, 500000.0, pos)
        for k, v_ in zip(("rc", "rs", "dc", "ds"), (rc, rs, dc, ds)):
            tabs[k].append(v_)
    tabs = {k: np.ascontiguousarray(np.stack(v_)) for k, v_ in tabs.items()}
    rcs = [ret_consts(gg) for gg in range(4)]
    gnv = np.asarray(inp["ret_gn"][0], np.float32)
    masks, negU = sb_consts()
    common = {"gains": g, "w_hyb": w_hyb, "w_hsw": w_hsw,
              "w_ho": np.asarray(inp["hyb_w_out"][0], np.float32), "w_sb": np.asarray(inp["sb_w_in"][0], np.float32),
              "w_so": np.asarray(inp["sb_w_out"][0], np.float32),
              "decT": np.ascontiguousarray(np.stack([r[0] for r in rcs])), "qdec": np.ascontiguousarray(np.stack([r[1] for r in rcs])),
              "kdec": np.ascontiguousarray(np.stack([r[2] for r in rcs])), "cdec": np.ascontiguousarray(np.stack([r[3] for r in rcs])),
              "gn": np.ascontiguousarray(np.stack([np.broadcast_to(gnv[gg * 128:(gg + 1) * 128][None, :], (128, 128)) for gg in range(4)])),
              "ident": np.eye(128, dtype=np.float32).astype(NPBF), "dmask": dil_mask(), "masks": masks, "negU": negU}
    common.update(tabs)
    for l in range(2):
        common[f"w_in1_{l}"] = np.asarray(inp["ffn1_w_in"][l], np.float32)
        common[f"w_out1_{l}"] = np.asarray(inp["ffn1_w_out"][l], np.float32)
        common[f"w_in2_{l}"] = np.asarray(inp["ffn2_w_in"][l], np.float32)
        common[f"w_out2_{l}"] = np.asarray(inp["ffn2_w_out"][l], np.float32)
    x = np.asarray(inp["x"], np.float32)
    maps = []
    for c in range(NCORE):
        m = dict(common)
        m["xT_in"] = np.ascontiguousarray(x[c // 4].T)
        maps.append(m)
    return maps


def kernel_fused_replicated(**inp):
    inp = {k: np.asarray(v) for k, v in inp.items()}
    nc = build_fused()
    maps = fused_inputs(inp)
    res = run_bass_kernel_spmd(nc, maps, core_ids=list(range(NCORE))).results
    y = np.stack([np.asarray(res[0]["yT_out"], np.float32).T, np.asarray(res[4]["yT_out"], np.float32).T], axis=0)
    return np.ascontiguousarray(y.astype(np.float32))
```

```python
import contextlib
import numpy as np
import ml_dtypes
import concourse.bass as bass
import concourse.mybir as mybir
from concourse.bass_utils import run_bass_kernel_spmd

F32 = mybir.dt.float32
BF16 = mybir.dt.bfloat16
AF = mybir.ActivationFunctionType
ALU = mybir.AluOpType
AX = mybir.AxisListType
NPBF = ml_dtypes.bfloat16

D = 1024
S = 8192
B = 2
NCORE = 8
NT = 2048
DC = 8
FF = 2816
FC = 22
HYB_IN = 3584
EPS = 1e-6
GN_EPS = 1e-5
GT = 1024
DBG = {}


class Prog:
    ENGS = ("sync", "scalar", "vector", "gpsimd", "tensor")

    def __init__(self, nc, stack, name):
        self.nc = nc
        self.stack = stack
        self.name = name
        self.ops = {e: [] for e in self.ENGS}
        self.esem = {}
        self.ecount = {e: 0 for e in self.ENGS}
        self.waited = {e: {} for e in self.ENGS}
        self.dcount = {}
        self.nsem = 0

    def new_sem(self, tag):
        self.nsem += 1
        return self.stack.enter_context(self.nc.semaphore(f"{self.name}_{tag}_{self.nsem}"))

    def dma_sems(self, tag, n):
        sems = [self.new_sem(tag) for _ in range(n)]
        for s in sems:
            self.dcount[id(s)] = [s, 0]
        return sems

    def _filter_waits(self, eng, waits):
        ws = []
        best = {}
        for t in waits:
            if t is None:
                continue
            if id(t[0]) not in best or best[id(t[0])][1] < t[1]:
                best[id(t[0])] = t
        for t in best.values():
            sem, val = t
            if eng in self.esem and sem is self.esem[eng]:
                continue
            key = id(sem)
            if self.waited[eng].get(key, 0) >= val:
                continue
            self.waited[eng][key] = val
            ws.append((sem, val))
        return ws

    def op(self, eng, fn, waits=(), sig=False, free=False, hard=()):
        ws = self._filter_waits(eng, waits)
        tok = None
        strict = DBG.get("strict", True) and eng in ("scalar", "vector", "gpsimd")
        if DBG.get("strict_all") and eng == "tensor":
            strict = True
        if strict:
            sig = True
            if self.ecount[eng] > 0 and not free:
                ws.append((self.esem[eng], self.ecount[eng]))
            else:
                for t in hard:
                    if t is not None:
                        ws.append(t)
        if sig:
            if eng not in self.esem:
                self.esem[eng] = self.new_sem("e" + eng)
            self.ecount[eng] += 1
            tok = (self.esem[eng], self.ecount[eng])
        self.ops[eng].append((fn, ws, tok, 1))
        return tok

    def pe(self, fn, waits=(), sig=False, free=False, hard=()):
        return self.op("tensor", fn, waits, sig, free, hard)

    def act(self, fn, waits=(), sig=False, free=False, hard=()):
        return self.op("scalar", fn, waits, sig, free, hard)

    def dve(self, fn, waits=(), sig=False, free=False, hard=()):
        return self.op("vector", fn, waits, sig, free, hard)

    def pool(self, fn, waits=(), sig=False, free=False, hard=()):
        return self.op("gpsimd", fn, waits, sig, free, hard)

    def dma(self, queue, out, in_, sem, waits=()):
        ws = self._filter_waits(queue, waits)
        ent = self.dcount[id(sem)]
        ent[1] += 16
        tok = (sem, ent[1])
        self.ops[queue].append((lambda e, o=out, i=in_: e.dma_start(out=o, in_=i), ws, tok, 16))
        return tok

    def emit(self, block):
        for eng in self.ENGS:
            ops = self.ops[eng]
            if not ops:
                continue

            def body(e, ops=ops):
                for fn, ws, tok, inc in ops:
                    for sem, val in ws:
                        e.wait_ge(sem, val)
                    ins = fn(e)
                    if tok is not None:
                        ins.then_inc(tok[0], inc)

            getattr(block, eng)(body)


class Ring:
    def __init__(self, tiles):
        self.tiles = tiles
        self.n = len(tiles)
        self.i = 0
        self.free = [[] for _ in tiles]

    def next(self):
        k = self.i % self.n
        self.i += 1
        toks = self.free[k]
        self.free[k] = []
        return k, self.tiles[k], toks

    def release(self, k, *toks):
        self.free[k].extend(t for t in toks if t is not None)


def sb(stack, nc, name, shape, dt):
    return stack.enter_context(nc.sbuf_tensor(name, list(shape), dt))


def psb(stack, nc, name, shape=(128, 512), dt=F32):
    return stack.enter_context(nc.psum_tensor(name, list(shape), dt))


class Ctx:
    pass


class NormUnit:
    def __init__(self, P, st, nc, name, gT, ones_bf):
        self.P = P
        self.gT = gT
        self.ones = ones_bf
        self.sq = sb(st, nc, name + "_sq", [128, DC, 512], BF16)
        self.tmp = sb(st, nc, name + "_tmp", [128, 512], F32)
        self.rstd = sb(st, nc, name + "_rstd", [128, 512], F32)
        self.ss = psb(st, nc, name + "_ss")
        self.t_pe = None
        self.t_sqrt = None
        self.t_rec = None

    def run(self, xT, t0, gcol, out_fn, x_ready=(), out_free=(), final=False):
        P = self.P
        sq, tmp, rstd, ss = self.sq, self.tmp, self.rstd, self.ss
        t_sq = None
        for c in range(DC):
            t_sq = P.act(lambda e, c=c: e.activation(out=sq[:, c, :], in_=xT[:, c, t0:t0 + 512], func=AF.Square),
                         waits=list(x_ready) + [self.t_pe], sig=(c == DC - 1))
        t_mm = None
        for c in range(DC):
            t_mm = P.pe(lambda e, c=c: e.matmul(ss[:, :], self.ones[:, :], sq[:, c, :], start=(c == 0), stop=(c == DC - 1)),
                        waits=[t_sq, self.t_sqrt], sig=(c == DC - 1))
        self.t_pe = t_mm
        t_s = P.act(lambda e: e.activation(out=tmp[:, :], in_=ss[:, :], func=AF.Sqrt, bias=EPS_TILE[0][:, 0:1], scale=1.0 / D),
                    waits=[t_mm, self.t_rec], sig=True)
        self.t_sqrt = t_s
        t_r = P.dve(lambda e: e.reciprocal(out=rstd[:, :], in_=tmp[:, :]), waits=[t_s], sig=True)
        self.t_rec = t_r
        t_h = None
        for c in range(DC):
            t_h = P.dve(lambda e, c=c: e.scalar_tensor_tensor(out=out_fn(c), in0=xT[:, c, t0:t0 + 512],
                                                              scalar=self.gT[:, gcol + c:gcol + c + 1],
                                                              in1=rstd[:, :], op0=ALU.mult, op1=ALU.mult),
                        waits=list(x_ready) + list(out_free), sig=(c == DC - 1))
        return t_h


EPS_TILE = [None]


def block_consts(nc, st, name):
    ones = sb(st, nc, name + "_ones", [128, 128], BF16)
    eps = sb(st, nc, name + "_eps", [128, 2], F32)
    with nc.Block(name + "_c") as blk:
        @blk.vector
        def _(v):
            v.memset(ones[:, :], 1.0)
            v.memset(eps[:, 0:1], EPS)
            v.memset(eps[:, 1:2], GN_EPS)
    EPS_TILE[0] = eps
    return ones, eps


def block_load(nc, name, pairs, queue="sync"):
    with contextlib.ExitStack() as st:
        P = Prog(nc, st, name)
        sem = P.dma_sems("ld", 1)[0]
        blk = st.enter_context(nc.Block(name))
        tok = None
        for o, i in pairs:
            tok = P.dma(queue, o, i, sem)
        P.op(queue, lambda e: e.nop(), waits=[tok])
        P.emit(blk)


def block_ffn(nc, cx, name, gcol, w_in, w_out):
    xT = cx.xT
    with contextlib.ExitStack() as st:
        P = Prog(nc, st, name)
        nu = NormUnit(P, st, nc, name, cx.gT, cx.ones)
        hT = sb(st, nc, name + "_hT", [128, DC, GT], BF16)
        actT = sb(st, nc, name + "_actT", [128, FC, GT], BF16)
        wgu = Ring([sb(st, nc, f"{name}_wgu{k}", [128, 2, DC, 128], BF16) for k in range(3)])
        wo = Ring([sb(st, nc, f"{name}_wo{k}", [128, FC, 128], BF16) for k in range(2)])
        sg = Ring([sb(st, nc, f"{name}_sg{k}", [128, 512], F32) for k in range(2)])
        pg = Ring([psb(st, nc, f"{name}_pg{k}") for k in range(2)])
        pu = Ring([psb(st, nc, f"{name}_pu{k}") for k in range(2)])
        py = Ring([psb(st, nc, f"{name}_py{k}") for k in range(2)])
        wsem = P.dma_sems("w", 3)
        osem = P.dma_sems("o", 2)
        blk = st.enter_context(nc.Block(name))
        w_in_v = w_in.rearrange("(c p) f -> p c f", p=128)
        w_out_v = w_out.rearrange("(j p) d -> p j d", p=128)
        ntile = GT // 512
        h_free = []
        a_free = []
        for gi in range(NT // GT):
            g0 = gi * GT
            t_h = []
            for t in range(ntile):
                th = nu.run(xT, g0 + t * 512, gcol, lambda c, t=t: hT[:, c, t * 512:(t + 1) * 512], out_free=h_free)
                t_h.append(th)
            h_free = []
            t_act_last = None
            for j in range(FC):
                k, wt, fr = wgu.next()
                P.dma("gpsimd", wt[:, 0, :, :], w_in_v[:, :, j * 128:(j + 1) * 128], wsem[k], waits=fr)
                t_w = P.dma("gpsimd", wt[:, 1, :, :], w_in_v[:, :, FF + j * 128:FF + (j + 1) * 128], wsem[k])
                t_pe_last = None
                for t in range(ntile):
                    kg, pgt, frg = pg.next()
                    ku, put, fru = pu.next()
                    ks, sgt, frs = sg.next()
                    tg = None
                    for c in range(DC):
                        tg = P.pe(lambda e, c=c, wt=wt, pgt=pgt, t=t: e.matmul(pgt[:, :], wt[:, 0, c, :], hT[:, c, t * 512:(t + 1) * 512],
                                                                               start=(c == 0), stop=(c == DC - 1)),
                                  waits=[t_w, t_h[t]] + frg, sig=(c == DC - 1))
                    tu = None
                    for c in range(DC):
                        tu = P.pe(lambda e, c=c, wt=wt, put=put, t=t: e.matmul(put[:, :], wt[:, 1, c, :], hT[:, c, t * 512:(t + 1) * 512],
                                                                               start=(c == 0), stop=(c == DC - 1)),
                                  waits=fru, sig=(c == DC - 1))
                    t_pe_last = tu
                    ts = P.act(lambda e, sgt=sgt, pgt=pgt: e.activation(out=sgt[:, :], in_=pgt[:, :], func=AF.Silu),
                               waits=[tg] + frs, sig=True)
                    pg.release(kg, ts)
                    ta = P.dve(lambda e, sgt=sgt, put=put, j=j, t=t: e.tensor_tensor(out=actT[:, j, t * 512:(t + 1) * 512], in0=sgt[:, :], in1=put[:, :],
                                                                                   op=ALU.mult),
                               waits=[ts, tu] + a_free, sig=True)
                    pu.release(ku, ta)
                    sg.release(ks, ta)
                    t_act_last = ta
                wgu.release(k, t_pe_last)
                if j == FC - 1:
                    h_free = [t_pe_last]
            a_free = []
            for c in range(DC):
                k, wt, fr = wo.next()
                t_w = P.dma("gpsimd", wt[:, :, :], w_out_v[:, :, c * 128:(c + 1) * 128], osem[k], waits=fr)
                t_pe_last = None
                for t in range(ntile):
                    ky, pyt, fry = py.next()
                    tp = None
                    for j in range(FC):
                        tp = P.pe(lambda e, j=j, wt=wt, pyt=pyt, t=t: e.matmul(pyt[:, :], wt[:, j, :], actT[:, j, t * 512:(t + 1) * 512],
                                                                               start=(j == 0), stop=(j == FC - 1)),
                                  waits=[t_w, t_act_last] + fry, sig=(j == FC - 1))
                    t_pe_last = tp
                    tx = P.dve(lambda e, pyt=pyt, c=c, t=t, g0=g0: e.scalar_tensor_tensor(out=xT[:, c, g0 + t * 512:g0 + (t + 1) * 512], in0=pyt[:, :], scalar=0.5,
                                                                                  in1=xT[:, c, g0 + t * 512:g0 + (t + 1) * 512], op0=ALU.mult, op1=ALU.add),
                               waits=[tp], sig=True)
                    py.release(ky, tx)
                wo.release(k, t_pe_last)
                if c == DC - 1:
                    a_free = [t_pe_last]
        P.emit(blk)


def block_outproj(nc, cx, name, mix_dram, w_out):
    xT = cx.xT
    with contextlib.ExitStack() as st:
        P = Prog(nc, st, name)
        mT = sb(st, nc, name + "_mT", [128, DC, NT], BF16)
        wr = Ring([sb(st, nc, f"{name}_w{k}", [128, DC, 128], BF16) for k in range(2)])
        py = Ring([psb(st, nc, f"{name}_py{k}") for k in range(2)])
        msem = P.dma_sems("m", 1)[0]
        wsem = P.dma_sems("w", 2)
        blk = st.enter_context(nc.Block(name))
        t_m = None
        mv = mix_dram.rearrange("(c p) t -> p c t", p=128)
        for c in range(DC):
            t_m = P.dma("sync", mT[:, c, :], mv[:, c, :], msem)
        w_v = w_out.rearrange("(c p) d -> p c d", p=128)
        for oc in range(DC):
            k, wt, fr = wr.next()
            t_w = P.dma("gpsimd", wt[:, :, :], w_v[:, :, oc * 128:(oc + 1) * 128], wsem[k], waits=fr)
            tl = None
            for t in range(NT // 512):
                ky, pyt, fry = py.next()
                tp = None
                for c in range(DC):
                    tp = P.pe(lambda e, c=c, wt=wt, pyt=pyt, t=t: e.matmul(pyt[:, :], wt[:, c, :], mT[:, c, t * 512:(t + 1) * 512],
                                                                           start=(c == 0), stop=(c == DC - 1)),
                              waits=[t_w, t_m] + fry, sig=(c == DC - 1))
                tl = tp
                tx = P.dve(lambda e, pyt=pyt, oc=oc, t=t: e.tensor_tensor(out=xT[:, oc, t * 512:(t + 1) * 512], in0=pyt[:, :],
                                                                         in1=xT[:, oc, t * 512:(t + 1) * 512], op=ALU.add),
                           waits=[tp], sig=True)
                py.release(ky, tx)
            wr.release(k, tl)
        P.emit(blk)


def block_proj(nc, cx, name, gcol, fm_specs, tm_specs, tabs=None, store_x=None):
    xT = cx.xT
    with contextlib.ExitStack() as st:
        P = Prog(nc, st, name)
        nu = NormUnit(P, st, nc, name, cx.gT, cx.ones)
        hT = sb(st, nc, name + "_hT", [128, DC, NT], BF16)
        wr = Ring([sb(st, nc, f"{name}_w{k}", [128, 2, DC, 128], BF16) for k in range(2)])
        wsem = P.dma_sems("w", 2)
        p1 = Ring([psb(st, nc, f"{name}_p1{k}") for k in range(2)])
        p2 = Ring([psb(st, nc, f"{name}_p2{k}") for k in range(2)])
        tabt = []
        tsem = P.dma_sems("t", 1)[0]
        blk = st.enter_context(nc.Block(name))
        t_xst = None
        if store_x is not None:
            xsem = P.dma_sems("xs", 1)[0]
            xv = store_x.rearrange("(c p) t -> p c t", p=128)
            for c in range(DC):
                t_xst = P.dma("sync", xv[:, c, :], xT[:, c, :], xsem)
        t_tab = None
        if tabs:
            for i, (cd, sd) in enumerate(tabs):
                ct = sb(st, nc, f"{name}_tc{i}", [128, NT], F32)
                stt = sb(st, nc, f"{name}_ts{i}", [128, NT], F32)
                P.dma("sync", ct[:, :], cd, tsem)
                t_tab = P.dma("sync", stt[:, :], sd, tsem)
                tabt.append((ct, stt))
        t_h = []
        for t in range(NT // 512):
            t_h.append(nu.run(xT, t * 512, gcol, lambda c, t=t: hT[:, c, t * 512:(t + 1) * 512]))
        f1 = Ring([sb(st, nc, f"{name}_f1{k}", [128, 512], F32) for k in range(2)])
        f2 = Ring([sb(st, nc, f"{name}_f2{k}", [128, 512], F32) for k in range(2)])
        og = Ring([sb(st, nc, f"{name}_og{k}", [128, 512], BF16) for k in range(3)])
        ssem = P.dma_sems("s", 3)
        last_store = []
        for (w_ap, wsw_ap, tab_idx, scale, out_dram) in fm_specs:
            k, wt, fr = wr.next()
            wv = w_ap.rearrange("(c p) f -> p c f", p=128)
            t_w = P.dma("gpsimd", wt[:, 0, :, :], wv, wsem[k], waits=fr)
            if wsw_ap is not None:
                t_w = P.dma("gpsimd", wt[:, 1, :, :], wsw_ap.rearrange("(c p) f -> p c f", p=128), wsem[k])
            tl = None
            for t in range(NT // 512):
                k1, p1t, fr1 = p1.next()
                ta = None
                for c in range(DC):
                    ta = P.pe(lambda e, c=c, wt=wt, p1t=p1t, t=t: e.matmul(p1t[:, :], wt[:, 0, c, :], hT[:, c, t * 512:(t + 1) * 512],
                                                                           start=(c == 0), stop=(c == DC - 1)),
                              waits=[t_w, t_h[t]] + fr1, sig=(c == DC - 1))
                tl = ta
                ko, ogt, fro = og.next()
                if wsw_ap is not None:
                    k2, p2t, fr2 = p2.next()
                    tb = None
                    for c in range(DC):
                        tb = P.pe(lambda e, c=c, wt=wt, p2t=p2t, t=t: e.matmul(p2t[:, :], wt[:, 1, c, :], hT[:, c, t * 512:(t + 1) * 512],
                                                                               start=(c == 0), stop=(c == DC - 1)),
                                  waits=fr2, sig=(c == DC - 1))
                    tl = tb
                    ct, stt = tabt[tab_idx]
                    kf1, f1t, frf1 = f1.next()
                    kf2, f2t, frf2 = f2.next()
                    td1 = P.dve(lambda e, f1t=f1t, p1t=p1t, ct=ct, t=t: e.tensor_tensor(out=f1t[:, :], in0=p1t[:, :], in1=ct[:, t * 512:(t + 1) * 512], op=ALU.mult),
                                waits=[ta, t_tab] + frf1, sig=True)
                    p1.release(k1, td1)
                    td2 = P.dve(lambda e, f2t=f2t, p2t=p2t, stt=stt, t=t: e.tensor_tensor(out=f2t[:, :], in0=p2t[:, :], in1=stt[:, t * 512:(t + 1) * 512], op=ALU.mult),
                                waits=[tb] + frf2, sig=True)
                    p2.release(k2, td2)
                    to = P.pool(lambda e, ogt=ogt, f1t=f1t, f2t=f2t: e.tensor_tensor(out=ogt[:, :], in0=f1t[:, :], in1=f2t[:, :], op=ALU.add),
                                waits=[td1, td2] + fro, sig=True)
                    f1.release(kf1, to)
                    f2.release(kf2, to)
                else:
                    to = P.act(lambda e, ogt=ogt, p1t=p1t, scale=scale: e.activation(out=ogt[:, :], in_=p1t[:, :], func=AF.Copy, scale=float(scale)),
                               waits=[ta] + fro, sig=True)
                    p1.release(k1, to)
                tst = P.dma("sync", out_dram[:, t * 512:(t + 1) * 512], ogt[:, :], ssem[ko], waits=[to])
                og.release(ko, tst)
                last_store.append(tst)
            wr.release(k, tl)
        if tm_specs:
            wtm = Ring([sb(st, nc, f"{name}_wtm{k}", [128, DC, 512], BF16) for k in range(2)])
            wtsem = P.dma_sems("wt", 2)
            otb = Ring([sb(st, nc, f"{name}_otb{k}", [128, 512], BF16) for k in range(2)])
            otf = Ring([sb(st, nc, f"{name}_otf{k}", [128, 512], F32) for k in range(2)])
            sbsem = P.dma_sems("sb", 2)
            sfsem = P.dma_sems("sf", 2)
            for (w_ap, func, out_dram, odt) in tm_specs:
                k, wt, fr = wtm.next()
                t_w = P.dma("gpsimd", wt[:, :, :], w_ap.rearrange("(c p) f -> p c f", p=128), wtsem[k], waits=fr)
                tl = None
                for tb_ in range(NT // 128):
                    k1, p1t, fr1 = p1.next()
                    ta = None
                    for c in range(DC):
                        ta = P.pe(lambda e, c=c, wt=wt, p1t=p1t, tb_=tb_: e.matmul(p1t[:, :], hT[:, c, tb_ * 128:(tb_ + 1) * 128], wt[:, c, :],
                                                                                   start=(c == 0), stop=(c == DC - 1)),
                                  waits=[t_w, t_h[tb_ // 4]] + fr1, sig=(c == DC - 1))
                    tl = ta
                    if odt == "bf16":
                        ko, ot, fro = otb.next()
                        sem = sbsem[ko]
                    else:
                        ko, ot, fro = otf.next()
                        sem = sfsem[ko]
                    to = P.act(lambda e, ot=ot, p1t=p1t, func=func: e.activation(out=ot[:, :], in_=p1t[:, :], func=func),
                               waits=[ta] + fro, sig=True)
                    p1.release(k1, to)
                    tst = P.dma("sync", out_dram[tb_ * 128:(tb_ + 1) * 128, :], ot[:, :], sem, waits=[to])
                    (otb if odt == "bf16" else otf).release(ko, tst)
                    last_store.append(tst)
                wtm.release(k, tl)
        P.op("sync", lambda e: e.nop(), waits=last_store[-8:] + [t_xst])
        P.emit(blk)


def block_store_x(nc, cx, name, x_out):
    with contextlib.ExitStack() as st:
        P = Prog(nc, st, name)
        sem = P.dma_sems("st", 1)[0]
        blk = st.enter_context(nc.Block(name))
        xv = x_out.rearrange("(c p) t -> p c t", p=128)
        tok = None
        for c in range(DC):
            tok = P.dma("sync", xv[:, c, :], cx.xT[:, c, :], sem)
        P.op("sync", lambda e: e.nop(), waits=[tok])
        P.emit(blk)


def block_load_x(nc, cx, name, x_in, gains):
    xv = x_in.rearrange("(c p) t -> p c t", p=128)
    pairs = [(cx.xT[:, c, :], xv[:, c, :]) for c in range(DC)]
    pairs.append((cx.gT[:, :], gains))
    block_load(nc, name, pairs)


def block_final(nc, cx, name, gcol, y_out):
    with contextlib.ExitStack() as st:
        P = Prog(nc, st, name)
        nu = NormUnit(P, st, nc, name, cx.gT, cx.ones)
        yt = Ring([sb(st, nc, f"{name}_y{k}", [128, DC, 512], F32) for k in range(2)])
        ssem = P.dma_sems("s", 2)
        blk = st.enter_context(nc.Block(name))
        yv = y_out.rearrange("(c p) t -> p c t", p=128)
        last = []
        for t in range(NT // 512):
            k, y, fr = yt.next()
            th = nu.run(cx.xT, t * 512, gcol, lambda c, y=y: y[:, c, :], out_free=fr)
            tok = None
            for c in range(DC):
                tok = P.dma("sync", yv[:, c, t * 512:(t + 1) * 512], y[:, c, :], ssem[k], waits=[th])
            yt.release(k, tok)
            last.append(tok)
        P.op("sync", lambda e: e.nop(), waits=last[-2:])
        P.emit(blk)


def new_nc():
    return bass.Bass("TRN2", target_bir_lowering=False)


def setup_ctx(nc, st, name):
    cx = Ctx()
    cx.xT = sb(st, nc, name + "_xT", [128, DC, NT], F32)
    cx.gT = sb(st, nc, name + "_gT", [128, 56], F32)
    cx.ones, cx.eps = block_consts(nc, st, name)
    return cx


G_FFN1 = (0, 8)
G_MIX = (16, 24)
G_FFN2 = (32, 40)
G_FINAL = 48


def build_A():
    nc = new_nc()
    x_in = nc.dram_tensor("xT_in", [D, NT], F32, kind="ExternalInput").ap()
    gains = nc.dram_tensor("gains", [128, 56], F32, kind="ExternalInput").ap()
    w_in = nc.dram_tensor("w_in", [D, 2 * FF], F32, kind="ExternalInput").ap()
    w_out = nc.dram_tensor("w_out", [FF, D], F32, kind="ExternalInput").ap()
    w_hyb = nc.dram_tensor("w_hyb", [D, HYB_IN], F32, kind="ExternalInput").ap()
    w_hsw = nc.dram_tensor("w_hsw", [D, 2048], F32, kind="ExternalInput").ap()
    tabs_d = [nc.dram_tensor(n, [128, NT], F32, kind="ExternalInput").ap() for n in ("rc", "rs", "dc", "ds")]
    x_out = nc.dram_tensor("xT_out", [D, NT], F32, kind="ExternalOutput").ap()
    qk_fm = nc.dram_tensor("qk_fm", [2048, NT], BF16, kind="ExternalOutput").ap()
    v_tm = nc.dram_tensor("v_tm", [NT, 1024], BF16, kind="ExternalOutput").ap()
    g_tm = nc.dram_tensor("g_tm", [NT, 512], F32, kind="ExternalOutput").ap()
    with contextlib.ExitStack() as st:
        cx = setup_ctx(nc, st, "A")
        block_load_x(nc, cx, "A_ld", x_in, gains)
        block_ffn(nc, cx, "A_ffn", G_FFN1[0], w_in, w_out)
        fm = []
        for i in range(4):
            fm.append((w_hyb[:, i * 128:(i + 1) * 128], w_hsw[:, i * 128:(i + 1) * 128], 0, 1.0, qk_fm[i * 128:(i + 1) * 128, :]))
        for i in range(4):
            fm.append((w_hyb[:, 512 + i * 128:512 + (i + 1) * 128], w_hsw[:, 512 + i * 128:512 + (i + 1) * 128], 0, 1.0,
                       qk_fm[512 + i * 128:512 + (i + 1) * 128, :]))
        for i in range(4):
            fm.append((w_hyb[:, 2048 + i * 128:2048 + (i + 1) * 128], w_hsw[:, 1024 + i * 128:1024 + (i + 1) * 128], 1, 1.0,
                       qk_fm[1024 + i * 128:1024 + (i + 1) * 128, :]))
        for i in range(4):
            fm.append((w_hyb[:, 2560 + i * 128:2560 + (i + 1) * 128], w_hsw[:, 1536 + i * 128:1536 + (i + 1) * 128], 1, 1.0,
                       qk_fm[1536 + i * 128:1536 + (i + 1) * 128, :]))
        tm = [
            (w_hyb[:, 1024:1536], AF.Copy, v_tm[:, 0:512], "bf16"),
            (w_hyb[:, 3072:3584], AF.Copy, v_tm[:, 512:1024], "bf16"),
            (w_hyb[:, 1536:2048], AF.Silu, g_tm[:, :], "f32"),
        ]
        block_proj(nc, cx, "A_pj", G_MIX[0], fm, tm, tabs=[(tabs_d[0], tabs_d[1]), (tabs_d[2], tabs_d[3])], store_x=x_out)
    return nc


def gains_table(inp):
    vecs = [inp["ffn1_norm"][0], inp["ffn1_norm"][1], inp["mix_norm"][0], inp["mix_norm"][1],
            inp["ffn2_norm"][0], inp["ffn2_norm"][1], inp["final_norm"]]
    cols = [np.asarray(v, np.float32).reshape(DC, 128).T for v in vecs]
    return np.ascontiguousarray(np.concatenate(cols, axis=1))


def rope_tables(rot_dim, theta, pos):
    half = rot_dim // 2
    inv_freq = (1.0 / (np.float32(theta) ** (np.arange(half, dtype=np.float32) / np.float32(half)))).astype(np.float32)
    ang = pos.astype(np.float32)[:, None] * inv_freq[None, :]
    cos, sin = np.cos(ang).astype(np.float32), np.sin(ang).astype(np.float32)
    C = np.ones((64, len(pos)), np.float32)
    Sg = np.zeros((64, len(pos)), np.float32)
    C[:half] = cos.T
    C[half:rot_dim] = cos.T
    Sg[:half] = -sin.T
    Sg[half:rot_dim] = sin.T
    return np.ascontiguousarray(np.concatenate([C, C], 0)), np.ascontiguousarray(np.concatenate([Sg, Sg], 0))


def swap_cols(w, rot_dim):
    half = rot_dim // 2
    n = w.shape[1] // 64
    idx = []
    for h in range(n):
        base = h * 64
        perm = list(range(64))
        for i in range(half):
            perm[i] = half + i
            perm[half + i] = i
        idx.extend(base + p for p in perm)
    return np.ascontiguousarray(w[:, idx])


def run_A(inp, xT_shards):
    nc = build_A()
    g = gains_table(inp)
    w_hyb = np.asarray(inp["hyb_w_in"][0], np.float32)
    w_hsw = np.concatenate([swap_cols(w_hyb[:, 0:512], 64), swap_cols(w_hyb[:, 512:1024], 64),
                            swap_cols(w_hyb[:, 2048:2560], 16), swap_cols(w_hyb[:, 2560:3072], 16)], axis=1)
    w_hsw = np.ascontiguousarray(w_hsw)
    maps = []
    for c in range(NCORE):
        pos = np.arange(NT) + (c % 4) * NT
        rc, rs = rope_tables(64, 10000.0, pos)
        dc, ds = rope_tables(16, 500000.0, pos)
        maps.append({"xT_in": xT_shards[c], "gains": g, "w_in": np.asarray(inp["ffn1_w_in"][0], np.float32),
                     "w_out": np.asarray(inp["ffn1_w_out"][0], np.float32), "w_hyb": w_hyb, "w_hsw": w_hsw,
                     "rc": rc, "rs": rs, "dc": dc, "ds": ds})
    res = run_bass_kernel_spmd(nc, maps, core_ids=list(range(NCORE)))
    return res.results


def shard_x(x):
    xf = np.asarray(x, np.float32).reshape(B * S, D)
    return [np.ascontiguousarray(xf[c * NT:(c + 1) * NT].T) for c in range(NCORE)]


def block_ret(nc, name, d):
    with contextlib.ExitStack() as st:
        P = Prog(nc, st, name)
        qT = sb(st, nc, name + "_qT", [128, S], BF16)
        kT = sb(st, nc, name + "_kT", [128, S], BF16)
        qd = sb(st, nc, name + "_qd", [128, S], BF16)
        v = sb(st, nc, name + "_v", [128, 64, 128], BF16)
        gate = sb(st, nc, name + "_gate", [128, 64, 128], F32)
        outT = sb(st, nc, name + "_outT", [128, S], BF16)
        decT = sb(st, nc, name + "_decT", [128, 256], F32)
        qdec = sb(st, nc, name + "_qdec", [128, 512], F32)
        kdec = sb(st, nc, name + "_kdec", [128, 2], F32)
        cdec = sb(st, nc, name + "_cdec", [128, 1], F32)
        gn = sb(st, nc, name + "_gn", [128, 128], F32)
        ident = sb(st, nc, name + "_ident", [128, 128], BF16)
        epsg = sb(st, nc, name + "_epsg", [128, 1], F32)
        S_f = sb(st, nc, name + "_Sf", [128, 64], F32)
        S_b = Ring([sb(st, nc, f"{name}_Sb{k}", [128, 64], BF16) for k in range(2)])
        PT = Ring([sb(st, nc, f"{name}_PT{k}", [128, 256], BF16) for k in range(2)])
        kd = Ring([sb(st, nc, f"{name}_kd{k}", [128, 128], BF16) for k in range(2)])
        stats = Ring([sb(st, nc, f"{name}_stats{k}", [128, 2, 6], F32) for k in range(2)])
        mv = Ring([sb(st, nc, f"{name}_mv{k}", [128, 2, 2], F32) for k in range(2)])
        sd = Ring([sb(st, nc, f"{name}_sd{k}", [128, 2], F32) for k in range(2)])
        rstd = Ring([sb(st, nc, f"{name}_rstd{k}", [128, 2], F32) for k in range(2)])
        y = Ring([sb(st, nc, f"{name}_y{k}", [128, 128], F32) for k in range(2)])
        m_tm = Ring([sb(st, nc, f"{name}_mtm{k}", [128, 128], BF16) for k in range(2)])
        scA = Ring([psb(st, nc, f"{name}_scA")])
        scB = Ring([psb(st, nc, f"{name}_scB")])
        tp = Ring([psb(st, nc, f"{name}_tp{k}", (128, 1024), BF16) for k in range(2)])
        obA = Ring([psb(st, nc, f"{name}_oA")])
        obB = Ring([psb(st, nc, f"{name}_oB")])
        kv = Ring([psb(st, nc, f"{name}_kv{k}") for k in range(2)])
        lsem = P.dma_sems("l", 1)[0]
        osem = P.dma_sems("o", 1)[0]
        blk = st.enter_context(nc.Block(name))
        P.dma("sync", qT[:, :], d["rq"], lsem)
        P.dma("sync", kT[:, :], d["rk"], lsem)
        P.dma("sync", v[:, :, :], d["rv"].rearrange("(n j) e -> j n e", j=128), lsem)
        P.dma("sync", gate[:, :, :], d["rg"].rearrange("(n j) e -> j n e", j=128), lsem)
        for t_, nm in ((decT, "decT"), (qdec, "qdec"), (kdec, "kdec"), (cdec, "cdec"), (gn, "gn"), (ident, "ident")):
            t_ld = P.dma("sync", t_[:, :], d[nm], lsem)
        P.dve(lambda e: e.memset(epsg[:, :], GN_EPS))
        if DBG.get('ret_chunks', 64) < 64:
            P.dve(lambda e: e.memset(outT[:, :], 0.0))
        t_qd = None
        for ch in range(16):
            t_qd = P.dve(lambda e, ch=ch: e.tensor_tensor(out=qd[:, ch * 512:(ch + 1) * 512], in0=qT[:, ch * 512:(ch + 1) * 512], in1=qdec[:, :], op=ALU.mult),
                         waits=[t_ld], sig=(ch == 15))
        NCH = DBG.get('ret_chunks', 64)
        cst = [dict() for _ in range(NCH)]
        sb_cur = [None]
        out_last = [None]

        def stage1(n):
            c_ = cst[n]
            cs = slice(n * 128, (n + 1) * 128)
            _, sctA, frscA = scA.next()
            _, sctB, frscB = scB.next()
            P.pe(lambda e, sctA=sctA, cs=cs: e.matmul(sctA[:, 0:128], kT[0:64, cs], qT[0:64, cs], start=True, stop=True), waits=[t_ld] + frscA + frscB)
            t_sc = P.pe(lambda e, sctB=sctB, cs=cs: e.matmul(sctB[:, 0:128], kT[64:128, cs], qT[64:128, cs], start=True, stop=True), sig=True)
            kpt, ptt, frpt = PT.next()
            P.dve(lambda e, ptt=ptt, sctA=sctA: e.tensor_tensor(out=ptt[:, 0:128], in0=sctA[:, 0:128], in1=decT[:, 0:128], op=ALU.mult),
                  waits=[t_sc] + frpt, free=True)
            t_pt = P.dve(lambda e, ptt=ptt, sctB=sctB: e.tensor_tensor(out=ptt[:, 128:256], in0=sctB[:, 0:128], in1=decT[:, 128:256], op=ALU.mult),
                         sig=True, free=True)
            scA.release(0, t_pt)
            scB.release(0, t_pt)
            ktp, tpt, frtp = tp.next()
            t_tp = P.pe(lambda e, tpt=tpt, cs=cs: e.transpose(tpt[:, 0:128], kT[:, cs], ident[:, :]), waits=frtp, sig=True)
            kkd, kdt, frkd = kd.next()
            P.dve(lambda e, kdt=kdt, tpt=tpt: e.tensor_scalar(out=kdt[:, 0:64], in0=tpt[:, 0:64], scalar1=kdec[:, 0:1], scalar2=None, op0=ALU.mult),
                  waits=[t_tp] + frkd, free=True)
            t_kd = P.dve(lambda e, kdt=kdt, tpt=tpt: e.tensor_scalar(out=kdt[:, 64:128], in0=tpt[:, 64:128], scalar1=kdec[:, 1:2], scalar2=None, op0=ALU.mult),
                         sig=True, free=True)
            tp.release(ktp, t_kd)
            c_.update(cs=cs, kpt=kpt, ptt=ptt, t_pt=t_pt, kkd=kkd, kdt=kdt, t_kd=t_kd)

        def stage2(n):
            c_ = cst[n]
            cs, kpt, ptt, t_pt, kkd, kdt, t_kd = c_["cs"], c_["kpt"], c_["ptt"], c_["t_pt"], c_["kkd"], c_["kdt"], c_["t_kd"]
            _, otA, froA = obA.next()
            _, otB, froB = obB.next()
            ots = (otA, otB)
            t_o = None
            for h in range(2):
                hs = slice(h * 64, (h + 1) * 64)
                ot = ots[h]
                t_o = P.pe(lambda e, ot=ot, ptt=ptt, n=n, h=h, hs=hs: e.matmul(ot[:, 0:64], ptt[:, h * 128:(h + 1) * 128], v[:, n, hs], start=True, stop=(n == 0)),
                           waits=[t_pt] + froA + froB, sig=(n == 0 and h == 1))
                if n > 0:
                    t_o = P.pe(lambda e, ot=ot, cs=cs, hs=hs, sbt=sb_cur[0][1]: e.matmul(ot[:, 0:64], qd[hs, cs], sbt[hs, :], start=False, stop=True),
                               waits=[t_qd, sb_cur[0][2]], sig=(h == 1))
            PT.release(kpt, t_o)
            if sb_cur[0] is not None:
                S_b.release(sb_cur[0][0], t_o)
            kkv, kvt, frkv = kv.next()
            P.pe(lambda e, kvt=kvt, kdt=kdt, n=n: e.matmul(kvt[0:64, 0:64], kdt[:, 0:64], v[:, n, 0:64], start=True, stop=True), waits=[t_kd] + frkv)
            t_kv = P.pe(lambda e, kvt=kvt, kdt=kdt, n=n: e.matmul(kvt[64:128, 0:64], kdt[:, 64:128], v[:, n, 64:128], start=True, stop=True), sig=True)
            kd.release(kkd, t_kv)
            if n < NCH - 1:
                if n == 0:
                    P.dve(lambda e, kvt=kvt: e.tensor_copy(out=S_f[:, :], in_=kvt[:, 0:64]), waits=[t_kv])
                else:
                    P.dve(lambda e, kvt=kvt: e.scalar_tensor_tensor(out=S_f[:, :], in0=S_f[:, :], scalar=cdec[:, 0:1], in1=kvt[:, 0:64], op0=ALU.mult, op1=ALU.add),
                          waits=[t_kv])
                ksb, sbt, frsb = S_b.next()
                t_sbn = P.dve(lambda e, sbt=sbt: e.tensor_copy(out=sbt[:, :], in_=S_f[:, :]), waits=frsb, sig=True)
                kv.release(kkv, t_sbn)
                sb_cur[0] = (ksb, sbt, t_sbn)
            kst, stt, _ = stats.next()
            kmv, mvt, frmv = mv.next()
            for h in range(2):
                P.dve(lambda e, stt=stt, ot=ots[h], h=h: e.bn_stats(out=stt[:, h, :], in_=ot[:, 0:64]), waits=[t_o])
            t_mv = None
            for h in range(2):
                t_mv = P.dve(lambda e, stt=stt, mvt=mvt, h=h: e.bn_aggr(out=mvt[:, h, :], in_=stt[:, h, :]), waits=frmv, sig=(h == 1))
            ksd, sdt, frsd = sd.next()
            t_sd = P.act(lambda e, sdt=sdt, mvt=mvt: e.activation(out=sdt[:, :], in_=mvt[:, :, 1], func=AF.Sqrt, bias=epsg[:, 0:1], scale=1.0),
                         waits=[t_mv] + frsd, sig=True, free=True)
            krs, rst, _ = rstd.next()
            P.dve(lambda e, rst=rst, sdt=sdt: e.reciprocal(out=rst[:, :], in_=sdt[:, :]), waits=[t_sd])
            ky, yt, fry = y.next()
            t_y = None
            for h in range(2):
                t_y = P.dve(lambda e, yt=yt, ot=ots[h], mvt=mvt, rst=rst, h=h: e.tensor_scalar(out=yt[:, h * 64:(h + 1) * 64], in0=ot[:, 0:64],
                                                                                              scalar1=mvt[:, h, 0:1], scalar2=rst[:, h:h + 1],
                                                                                              op0=ALU.subtract, op1=ALU.mult),
                            waits=fry, sig=(h == 1))
            obA.release(0, t_y)
            obB.release(0, t_y)
            sd.release(ksd, t_y)
            mv.release(kmv, t_y)
            t_y2 = P.dve(lambda e, yt=yt: e.tensor_tensor(out=yt[:, :], in0=yt[:, :], in1=gn[:, :], op=ALU.mult), sig=True)
            km, mt, frm = m_tm.next()
            t_m = P.pool(lambda e, mt=mt, yt=yt, n=n: e.tensor_tensor(out=mt[:, :], in0=yt[:, :], in1=gate[:, n, :], op=ALU.mult),
                         waits=[t_y2, t_ld] + frm, sig=True, free=True)
            y.release(ky, t_m)
            c_.update(km=km, mt=mt, t_m=t_m)

        def stage3(n):
            c_ = cst[n]
            cs, km, mt, t_m = c_["cs"], c_["km"], c_["mt"], c_["t_m"]
            ktp2, tpt2, frtp2 = tp.next()
            t_tp2 = P.pe(lambda e, tpt2=tpt2, mt=mt: e.transpose(tpt2[:, 0:128], mt[:, :], ident[:, :]), waits=[t_m] + frtp2, sig=True)
            m_tm.release(km, t_tp2)
            out_last[0] = P.act(lambda e, tpt2=tpt2, cs=cs: e.activation(out=outT[:, cs], in_=tpt2[:, 0:128], func=AF.Copy), waits=[t_tp2], sig=True, free=True)
            tp.release(ktp2, out_last[0])

        for i in range(-1, NCH + 1):
            if 0 <= i + 1 < NCH:
                stage1(i + 1)
            if 0 <= i < NCH:
                stage2(i)
            if 0 <= i - 1 < NCH:
                stage3(i - 1)
        t_out_last = out_last[0]
        t_st = P.dma("sync", d["ret_out"], outT[:, :], osem, waits=[t_out_last])
        P.op("sync", lambda e: e.nop(), waits=[t_st])
        P.emit(blk)


DIL = (1, 4, 16)


def block_dil(nc, name, d, ones):
    with contextlib.ExitStack() as st:
        P = Prog(nc, st, name)
        qT = sb(st, nc, name + "_qT", [128, S], BF16)
        kT = sb(st, nc, name + "_kT", [128, S], BF16)
        vp = [sb(st, nc, f"{name}_vp{p}", [128, 64, 128], BF16) for p in range(3)]
        acc = sb(st, nc, name + "_acc", [128, 2, S], F32)
        qDI = sb(st, nc, name + "_qDI", [128, S], BF16)
        kDI = sb(st, nc, name + "_kDI", [128, S], BF16)
        outT = qT
        dmask = sb(st, nc, name + "_dmask", [128, 512], BF16)
        sq = Ring([sb(st, nc, f"{name}_sq{k}", [128, 512], BF16) for k in range(2)])
        mx = sb(st, nc, name + "_mx", [128, 4, 16], F32)
        mx2 = sb(st, nc, name + "_mx2", [128, 4], F32)
        prod = sb(st, nc, name + "_prod", [128, 2], F32)
        bias = sb(st, nc, name + "_bias", [128, 2], F32)
        Pt = Ring([sb(st, nc, f"{name}_P{k}", [128, 512], BF16) for k in range(3)])
        spA = Ring([psb(st, nc, f"{name}_spA{k}") for k in range(2)])
        spB = Ring([psb(st, nc, f"{name}_spB{k}") for k in range(2)])
        nb_ = Ring([psb(st, nc, f"{name}_n{k}", (128, 4, 128)) for k in range(2)])
        nq = Ring([psb(st, nc, f"{name}_nq{k}") for k in range(2)])
        lsem = P.dma_sems("l", 1)[0]
        vsem = P.dma_sems("v", 1)[0]
        osem = P.dma_sems("o", 1)[0]
        blk = st.enter_context(nc.Block(name))
        P.dma("sync", qT[:, :], d["dq"], lsem)
        P.dma("sync", kT[:, :], d["dk"], lsem)
        t_ld = P.dma("sync", dmask[:, :], d["dmask"], lsem)
        t_v = None
        for p, dl in enumerate(DIL):
            nbc = 64 // dl
            src = d["dv"].rearrange("(nb i r) e -> r i nb e", i=128, r=dl)
            for r in range(dl):
                t_v = P.dma("sync", vp[p][:, r * nbc:(r + 1) * nbc, :], src[r], vsem)
        for ti, src_t in enumerate((qT, kT)):
            for ch in range(16):
                ks, sqt, frs = sq.next()
                t_sq = P.dve(lambda e, sqt=sqt, src_t=src_t, ch=ch: e.tensor_tensor(out=sqt[:, :], in0=src_t[:, ch * 512:(ch + 1) * 512],
                                                                                   in1=src_t[:, ch * 512:(ch + 1) * 512], op=ALU.mult),
                             waits=[t_ld] + frs, sig=True)
                t_last = None
                for h in range(2):
                    kn, nqt, frn = nq.next()
                    t_n = P.pe(lambda e, nqt=nqt, sqt=sqt, h=h: e.matmul(nqt[:, :], ones[h * 64:(h + 1) * 64, :], sqt[h * 64:(h + 1) * 64, :], start=True, stop=True),
                               waits=[t_sq] + frn, sig=True)
                    t_r = P.dve(lambda e, nqt=nqt, ti=ti, h=h, ch=ch: e.reduce_max(out=mx[:, ti * 2 + h, ch:ch + 1], in_=nqt[:, :], axis=AX.X),
                                waits=[t_n], sig=True)
                    nq.release(kn, t_r)
                    t_last = t_n
                sq.release(ks, t_last)
        P.dve(lambda e: e.reduce_max(out=mx2[:, :], in_=mx[:, :, :], axis=AX.X))
        t_p = P.dve(lambda e: e.tensor_tensor(out=prod[:, :], in0=mx2[:, 0:2], in1=mx2[:, 2:4], op=ALU.mult), sig=True)
        t_s = P.act(lambda e: e.activation(out=prod[:, :], in_=prod[:, :], func=AF.Sqrt), waits=[t_p], sig=True)
        t_bias = P.dve(lambda e: e.tensor_scalar(out=bias[:, :], in0=prod[:, :], scalar1=-1.02 / 8.0, scalar2=None, op0=ALU.mult), waits=[t_s], sig=True)
        units = []
        for p, dl in enumerate(DIL[:DBG.get('dil_patterns', 3)]):
            nbc = 64 // dl
            for r in range(dl):
                for nb in range(nbc):
                    units.append((p, dl, nbc, r, nb))
        ust = [dict() for _ in units]
        pe_last = [None]
        de_tok = {0: None}
        acc_last = [None]

        def stageA(i):
            p, dl, nbc, r, nb = units[i]
            L = S // dl
            u_ = ust[i]
            if p == 0:
                qs, ks_ = qT, kT
            else:
                qs, ks_ = qDI, kDI
                if p not in de_tok:
                    t_de = None
                    for rr in range(dl):
                        P.dve(lambda e, rr=rr, dl=dl, L=L: e.tensor_copy(out=qDI[:, rr * L:(rr + 1) * L], in_=qT[:, rr:rr + (L - 1) * dl + 1:dl]),
                              waits=[t_ld, pe_last[0]])
                        t_de = P.dve(lambda e, rr=rr, dl=dl, L=L: e.tensor_copy(out=kDI[:, rr * L:(rr + 1) * L], in_=kT[:, rr:rr + (L - 1) * dl + 1:dl]), sig=True)
                    de_tok[p] = t_de
            t_de = de_tok[p]
            ctoks = slice(r * L + nb * 128, r * L + (nb + 1) * 128)
            ptoks = slice(r * L + (nb - 1) * 128, r * L + nb * 128)
            kspA, sptA, frspA = spA.next()
            kspB, sptB, frspB = spB.next()
            spts = (sptA, sptB)
            t_s_ = None
            first = True
            for h in range(2):
                hs = slice(h * 64, (h + 1) * 64)
                spt = spts[h]
                t_s_ = P.pe(lambda e, spt=spt, hs=hs, ctoks=ctoks, qs=qs, ks_=ks_: e.matmul(spt[:, 0:128], ks_[hs, ctoks], qs[hs, ctoks], start=True, stop=True),
                            waits=([t_ld, t_de] + frspA + frspB) if first else (), sig=(nb == 0 and h == 1))
                first = False
                if nb > 0:
                    t_s_ = P.pe(lambda e, spt=spt, hs=hs, ctoks=ctoks, ptoks=ptoks, qs=qs, ks_=ks_: e.matmul(spt[:, 128:256], ks_[hs, ptoks], qs[hs, ctoks],
                                                                                                           start=True, stop=True), sig=(h == 1))
            pe_last[0] = t_s_
            kP, Ptt, frP = Pt.next()
            w = 256 if nb > 0 else 128
            t_e = None
            for h in range(2):
                t_e = P.act(lambda e, Ptt=Ptt, spt=spts[h], h=h, w=w: e.activation(out=Ptt[:, h * 256:h * 256 + w], in_=spt[:, 0:w], func=AF.Exp,
                                                                                  bias=bias[:, h:h + 1], scale=0.125),
                            waits=[t_s_, t_bias] + frP, sig=(h == 1), free=True)
            spA.release(kspA, t_e)
            spB.release(kspB, t_e)
            if nb > 0:
                t_m = P.pool(lambda e, Ptt=Ptt: e.tensor_tensor(out=Ptt[:, :], in0=Ptt[:, :], in1=dmask[:, :], op=ALU.mult), waits=[t_e, t_ld], sig=True, free=True)
            else:
                P.pool(lambda e, Ptt=Ptt: e.tensor_tensor(out=Ptt[:, 0:128], in0=Ptt[:, 0:128], in1=dmask[:, 0:128], op=ALU.mult), waits=[t_e, t_ld], free=True)
                t_m = P.pool(lambda e, Ptt=Ptt: e.tensor_tensor(out=Ptt[:, 256:384], in0=Ptt[:, 256:384], in1=dmask[:, 256:384], op=ALU.mult), sig=True, free=True)
            u_["kP"], u_["Ptt"], u_["t_m"] = kP, Ptt, t_m

        def stageB(i):
            p, dl, nbc, r, nb = units[i]
            u_ = ust[i]
            kP, Ptt, t_m = u_["kP"], u_["Ptt"], u_["t_m"]
            u = r * nbc + nb
            start = nb * 128 * dl + r
            toks = slice(start, start + 127 * dl + 1, dl)
            kn, nt, frn = nb_.next()
            t_n = None
            first = True
            for which in range(2):
                for h in range(2):
                    hs = slice(h * 64, (h + 1) * 64)
                    lhs_c = vp[p][:, u, hs] if which == 0 else ones[:, 0:64]
                    t_n = P.pe(lambda e, nt=nt, hs=hs, lhs_c=lhs_c, Ptt=Ptt, h=h, which=which, nb=nb: e.matmul(nt[hs, which, :], lhs_c, Ptt[:, h * 256:h * 256 + 128],
                                                                                                             start=True, stop=(nb == 0)),
                               waits=([t_m, t_v] + frn) if first else (), sig=(nb == 0 and which == 1 and h == 1))
                    first = False
                    if nb > 0:
                        lhs_p = vp[p][:, u - 1, hs] if which == 0 else ones[:, 0:64]
                        t_n = P.pe(lambda e, nt=nt, hs=hs, lhs_p=lhs_p, Ptt=Ptt, h=h, which=which: e.matmul(nt[hs, which, :], lhs_p, Ptt[:, h * 256 + 128:h * 256 + 256],
                                                                                                        start=False, stop=True),
                                   sig=(which == 1 and h == 1))
            Pt.release(kP, t_n)
            if p == 0:
                t_acc = P.dve(lambda e, nt=nt, toks=toks: e.tensor_copy(out=acc[:, :, toks], in_=nt[:, 0:2, :]), waits=[t_n], sig=True, free=True)
            else:
                t_acc = P.dve(lambda e, nt=nt, toks=toks: e.tensor_tensor(out=acc[:, :, toks], in0=acc[:, :, toks], in1=nt[:, 0:2, :], op=ALU.add),
                              waits=[t_n], sig=True, free=True, hard=[acc_last[0]])
            acc_last[0] = t_acc
            nb_.release(kn, t_acc)

        NU = len(units)
        for i in range(-2, NU):
            if 0 <= i + 2 < NU:
                stageA(i + 2)
            if i >= 0:
                stageB(i)
        t_pe_last = pe_last[0]
        t_o = None
        for ch in range(4):
            cs = slice(ch * 2048, (ch + 1) * 2048)
            P.dve(lambda e, cs=cs: e.reciprocal(out=acc[:, 1, cs], in_=acc[:, 1, cs]), hard=[acc_last[0]])
            t_o = P.dve(lambda e, cs=cs: e.tensor_tensor(out=outT[:, cs], in0=acc[:, 0, cs], in1=acc[:, 1, cs], op=ALU.mult), waits=[t_pe_last], sig=True)
        t_st = P.dma("sync", d["dil_out"], outT[:, :], osem, waits=[t_o])
        P.op("sync", lambda e: e.nop(), waits=[t_st])
        P.emit(blk)


def ret_consts(g):
    i = np.arange(128, dtype=np.float64)
    decT = np.zeros((128, 256), np.float32)
    qdec = np.zeros((128, 512), np.float32)
    kdec = np.zeros((128, 2), np.float32)
    cdec = np.zeros((128, 1), np.float32)
    for hh in range(2):
        h = 2 * g + hh
        lg = np.log(1.0 - 2.0 ** (-5.0 - h))
        diff = i[None, :] - i[:, None]
        dm = np.where(diff >= 0, np.exp(np.maximum(diff, 0) * lg), 0.0) / 8.0
        decT[:, hh * 128:(hh + 1) * 128] = dm
        qd = np.exp((i + 1) * lg) / 8.0
        qdec[hh * 64:(hh + 1) * 64, :] = np.tile(qd, 4)[None, :]
        kdec[:, hh] = np.exp((127 - i) * lg)
        cdec[hh * 64:(hh + 1) * 64, 0] = np.exp(128 * lg)
    return decT, qdec, kdec, cdec


def dil_mask():
    k = np.arange(128)[:, None]
    q = np.arange(128)[None, :]
    cur = (k <= q).astype(np.float32)
    prev = (k >= q).astype(np.float32)
    m = np.concatenate([cur, prev, cur, prev], axis=1)
    return m.astype(NPBF)


def build_B(which="both"):
    nc = new_nc()
    d = {}
    d["rqk"] = nc.dram_tensor("rqk", [2, 128, S], BF16, kind="ExternalInput").ap()
    d["dqk"] = nc.dram_tensor("dqk", [2, 128, S], BF16, kind="ExternalInput").ap()
    d["rv"] = nc.dram_tensor("rv", [S, 128], BF16, kind="ExternalInput").ap()
    d["dv"] = nc.dram_tensor("dv", [S, 128], BF16, kind="ExternalInput").ap()
    d["rg"] = nc.dram_tensor("rg", [S, 128], F32, kind="ExternalInput").ap()
    d["decT"] = nc.dram_tensor("decT", [128, 256], F32, kind="ExternalInput").ap()
    d["qdec"] = nc.dram_tensor("qdec", [128, 512], F32, kind="ExternalInput").ap()
    d["kdec"] = nc.dram_tensor("kdec", [128, 2], F32, kind="ExternalInput").ap()
    d["cdec"] = nc.dram_tensor("cdec", [128, 1], F32, kind="ExternalInput").ap()
    d["gn"] = nc.dram_tensor("gn", [128, 128], F32, kind="ExternalInput").ap()
    d["ident"] = nc.dram_tensor("ident", [128, 128], BF16, kind="ExternalInput").ap()
    d["dmask"] = nc.dram_tensor("dmask", [128, 512], BF16, kind="ExternalInput").ap()
    d["mixT"] = nc.dram_tensor("mixT", [256, S], BF16, kind="ExternalOutput").ap()
    d["rq"], d["rk"], d["dq"], d["dk"] = d["rqk"][0], d["rqk"][1], d["dqk"][0], d["dqk"][1]
    d["ret_out"], d["dil_out"] = d["mixT"][0:128, :], d["mixT"][128:256, :]
    with contextlib.ExitStack() as st:
        ones, eps = block_consts(nc, st, "B")
        if which in ("both", "ret"):
            block_ret(nc, "B_ret", d)
        if which in ("both", "dil"):
            block_dil(nc, "B_dil", d, ones)
    return nc


def run_B(inp, resA, which="both"):
    nc = build_B(which)
    ident = np.eye(128, dtype=np.float32).astype(NPBF)
    dm = dil_mask()
    gnv = np.asarray(inp["ret_gn"][0], np.float32)
    maps = []
    for c in range(NCORE):
        b, g = c // 4, c % 4
        ra = [resA[b * 4 + i] for i in range(4)]
        qk = np.concatenate([np.asarray(r["qk_fm"]) for r in ra], axis=1)
        vt = np.concatenate([np.asarray(r["v_tm"]) for r in ra], axis=0)
        gt = np.concatenate([np.asarray(r["g_tm"]) for r in ra], axis=0)
        sl = slice(g * 128, (g + 1) * 128)
        rqk = np.ascontiguousarray(np.stack([qk[0:512][sl], qk[512:1024][sl]]))
        dqk = np.ascontiguousarray(np.stack([qk[1024:1536][sl], qk[1536:2048][sl]]))
        decT, qdec, kdec, cdec = ret_consts(g)
        maps.append({"rqk": rqk, "dqk": dqk,
                     "rv": np.ascontiguousarray(vt[:, g * 128:(g + 1) * 128]),
                     "dv": np.ascontiguousarray(vt[:, 512 + g * 128:512 + (g + 1) * 128]),
                     "rg": np.ascontiguousarray(gt[:, g * 128:(g + 1) * 128]),
                     "decT": decT, "qdec": qdec, "kdec": kdec, "cdec": cdec,
                     "gn": np.ascontiguousarray(np.broadcast_to(gnv[g * 128:(g + 1) * 128][None, :], (128, 128))),
                     "ident": ident, "dmask": dm})
    res = run_bass_kernel_spmd(nc, maps, core_ids=list(range(NCORE)))
    return res.results


def mix_from_B(resB):
    out = []
    for c in range(NCORE):
        b, ts = c // 4, c % 4
        tsl = slice(ts * NT, (ts + 1) * NT)
        ret = np.concatenate([np.asarray(resB[b * 4 + g]["mixT"])[0:128, tsl] for g in range(4)], axis=0)
        dil = np.concatenate([np.asarray(resB[b * 4 + g]["mixT"])[128:256, tsl] for g in range(4)], axis=0)
        out.append(np.ascontiguousarray(np.concatenate([ret, dil], axis=0)))
    return out


def block_sb(nc, name, d):
    with contextlib.ExitStack() as st:
        P = Prog(nc, st, name)
        qT = [sb(st, nc, f"{name}_qT{p}", [128, S], BF16) for p in range(2)]
        kT = [sb(st, nc, f"{name}_kT{p}", [128, S], BF16) for p in range(2)]
        v = sb(st, nc, name + "_v", [128, 64, 256], BF16)
        outT = [sb(st, nc, f"{name}_oT{p}", [128, S], BF16) for p in range(2)]
        masks = sb(st, nc, name + "_masks", [128, 4, 512], BF16)
        negU = sb(st, nc, name + "_negU", [128, 128], BF16)
        negones = sb(st, nc, name + "_negones", [128, 128], BF16)
        Eb = Ring([sb(st, nc, f"{name}_E{k}", [128, 1024], F32) for k in range(2)])
        Lb = Ring([sb(st, nc, f"{name}_L{k}", [128, 1024], BF16) for k in range(3)])
        Ab = Ring([sb(st, nc, f"{name}_A{k}", [128, 1024], BF16) for k in range(2)])
        Accb = Ring([sb(st, nc, f"{name}_Acc{k}", [128, 1024], BF16) for k in range(3)])
        Zb = Ring([psb(st, nc, f"{name}_Z{k}", (128, 1024)) for k in range(3)])
        Ob = Ring([psb(st, nc, f"{name}_O{k}") for k in range(2)])
        lsem = P.dma_sems("l", 1)[0]
        osem = P.dma_sems("o", 1)[0]
        blk = st.enter_context(nc.Block(name))
        for p in range(2):
            P.dma("sync", qT[p][:, :], d["q"][p], lsem)
            P.dma("sync", kT[p][:, :], d["k"][p], lsem)
        P.dma("sync", v[:, :, :], d["v"].rearrange("(n j) e -> j n e", j=128), lsem)
        P.dma("sync", masks[:, :, :], d["masks"], lsem)
        t_ld = P.dma("sync", negU[:, :], d["negU"], lsem)
        t_c = P.dve(lambda e: e.memset(negones[:, :], -1.0), sig=True)

        nqt = DBG.get("sb_qt", 16)
        npair = DBG.get("sb_pairs", 2)
        if nqt < 16 or npair < 2:
            for p in range(2):
                P.dve(lambda e, p=p: e.memset(outT[p][:, :], 0.0))
        blocks = []
        for p in range(npair):
            for qt in range(nqt):
                kbs = list(range(4 * qt + 3, -1, -1))
                for kb in kbs:
                    a = kb - 4 * qt
                    blocks.append(dict(p=p, qt=qt, kb=kb, a=(a if a >= 0 else None), first=(kb == kbs[0]), last=(kb == 0)))
        N = len(blocks)
        stt = [dict() for _ in range(N)]
        acc_cur = [None]
        o_cur = [None]
        last_evacs = []
        H2 = (slice(0, 512), slice(512, 1024))

        def st1(i):
            b = blocks[i]
            s_ = stt[i]
            kz, zt, frz = Zb.next()
            p = b["p"]
            qs = slice(b["qt"] * 512, (b["qt"] + 1) * 512)
            ks = slice(b["kb"] * 128, (b["kb"] + 1) * 128)
            P.pe(lambda e, zt=zt, p=p, qs=qs, ks=ks: e.matmul(zt[:, 0:512], kT[p][0:64, ks], qT[p][0:64, qs], start=True, stop=True),
                 waits=[t_ld] + frz)
            s_["t_qk"] = P.pe(lambda e, zt=zt, p=p, qs=qs, ks=ks: e.matmul(zt[:, 512:1024], kT[p][64:128, ks], qT[p][64:128, qs], start=True, stop=True),
                              sig=True)
            s_["kz"], s_["zt"] = kz, zt

        def st2(i):
            b = blocks[i]
            s_ = stt[i]
            zt = s_["zt"]
            ke, et, fre = Eb.next()
            t_e = P.act(lambda e, et=et, zt=zt: e.activation(out=et[:, :], in_=zt[:, :], func=AF.Exp), waits=[s_["t_qk"]] + fre, sig=True, free=True)
            s_["ke"], s_["et"], s_["t_e"] = ke, et, t_e

        def st2b(i):
            b = blocks[i]
            s_ = stt[i]
            ke, et, t_e = s_["ke"], s_["et"], s_["t_e"]
            kl, lt, frl = Lb.next()
            t_l = P.act(lambda e, lt=lt, et=et: e.activation(out=lt[:, :], in_=et[:, :], func=AF.Ln, bias=1.0, scale=1.0), waits=frl, sig=True,
                        free=True, hard=[t_e])
            Eb.release(ke, t_l)
            if b["a"] is not None:
                a = b["a"]
                P.dve(lambda e, lt=lt, a=a: e.tensor_tensor(out=lt[:, 0:512], in0=lt[:, 0:512], in1=masks[:, a, :], op=ALU.mult), waits=[t_l, t_ld], free=True)
                t_l = P.dve(lambda e, lt=lt, a=a: e.tensor_tensor(out=lt[:, 512:1024], in0=lt[:, 512:1024], in1=masks[:, a, :], op=ALU.mult), sig=True, free=True)
            s_["kl"], s_["lt"], s_["t_l"] = kl, lt, t_l
            s_["acc"] = None if b["first"] else acc_cur[0]
            if not b["last"]:
                ka, at, fra = Accb.next()
                if b["first"]:
                    t_a = P.pool(lambda e, at=at, lt=lt: e.tensor_copy(out=at[:, :], in_=lt[:, :]), waits=[t_l] + fra, sig=True, free=True)
                else:
                    prev = acc_cur[0]
                    t_a = P.pool(lambda e, at=at, lt=lt, pt=prev[1]: e.tensor_tensor(out=at[:, :], in0=pt[:, :], in1=lt[:, :], op=ALU.add),
                                 waits=[t_l, prev[2]] + fra, sig=True)
                acc_cur[0] = (ka, at, t_a)
                s_["t_accupd"] = t_a
            else:
                s_["t_accupd"] = None

        def st3(i):
            b = blocks[i]
            s_ = stt[i]
            zt, lt = s_["zt"], s_["lt"]
            t_u = None
            for hh in range(2):
                t_u = P.pe(lambda e, zt=zt, lt=lt, hh=hh: e.matmul(zt[:, H2[hh]], negU[:, :], lt[:, H2[hh]], start=False, stop=True, skip_group_check=True),
                           waits=[s_["t_l"], t_c], sig=(hh == 1))
            if s_["acc"] is not None:
                ka, at, t_a = s_["acc"]
                for hh in range(2):
                    t_u = P.pe(lambda e, zt=zt, at=at, hh=hh: e.matmul(zt[:, H2[hh]], negones[:, :], at[:, H2[hh]], start=False, stop=True, skip_group_check=True),
                               waits=[t_a], sig=(hh == 1))
                Accb.release(ka, t_u)
                if s_["t_accupd"] is not None:
                    Accb.release(ka, s_["t_accupd"])
            s_["t_u"] = t_u
            rel = [t_u]
            if s_["t_accupd"] is not None:
                rel.append(s_["t_accupd"])
            Lb.release(s_["kl"], *rel)

        def st4(i):
            b = blocks[i]
            s_ = stt[i]
            zt = s_["zt"]
            kA, At, frA = Ab.next()
            t_A = P.act(lambda e, At=At, zt=zt: e.activation(out=At[:, :], in_=zt[:, :], func=AF.Exp), waits=[s_["t_u"]] + frA, sig=True, free=True)
            Zb.release(s_["kz"], t_A)
            if b["a"] is not None:
                a = b["a"]
                P.dve(lambda e, At=At, a=a: e.tensor_tensor(out=At[:, 0:512], in0=At[:, 0:512], in1=masks[:, a, :], op=ALU.mult), waits=[t_A], free=True)
                t_A = P.dve(lambda e, At=At, a=a: e.tensor_tensor(out=At[:, 512:1024], in0=At[:, 512:1024], in1=masks[:, a, :], op=ALU.mult), sig=True, free=True)
            s_["kA"], s_["At"], s_["t_A"] = kA, At, t_A

        def st5(i):
            b = blocks[i]
            s_ = stt[i]
            At = s_["At"]
            p = b["p"]
            if b["first"]:
                ko, ot, fro = Ob.next()
                o_cur[0] = (ko, ot, fro)
            ko, ot, fro = o_cur[0]
            t_av = None
            for hh in range(2):
                hs = slice(hh * 64, (hh + 1) * 64)
                hcol = slice((2 * p + hh) * 64, (2 * p + hh + 1) * 64)
                t_av = P.pe(lambda e, ot=ot, hs=hs, At=At, kb=b["kb"], hcol=hcol, hh=hh, first=b["first"], last=b["last"]:
                            e.matmul(ot[hs, :], v[:, kb, hcol], At[:, H2[hh]], start=first, stop=last, skip_group_check=True),
                            waits=[s_["t_A"]] + (fro if b["first"] else []), sig=(hh == 1))
            Ab.release(s_["kA"], t_av)
            if b["last"]:
                qs = slice(b["qt"] * 512, (b["qt"] + 1) * 512)
                t_ev = P.dve(lambda e, ot=ot, p=p, qs=qs: e.tensor_copy(out=outT[p][:, qs], in_=ot[:, :]), waits=[t_av], sig=True, free=True)
                Ob.release(ko, t_ev)
                last_evacs.append(t_ev)

        for s_i in range(-3, N):
            if 0 <= s_i + 2 < N:
                st2(s_i + 2)
            if 0 <= s_i < N:
                st4(s_i)
            if 0 <= s_i + 2 < N:
                st2b(s_i + 2)
            if 0 <= s_i < N:
                st5(s_i)
            if 0 <= s_i + 3 < N:
                st1(s_i + 3)
            if 0 <= s_i + 2 < N:
                st3(s_i + 2)
        toks = []
        for p in range(2):
            toks.append(P.dma("sync", d["o"][p], outT[p][:, :], osem, waits=last_evacs[-2:]))
        P.op("sync", lambda e: e.nop(), waits=toks)
        P.emit(blk)


def sb_consts():
    m = np.zeros((128, 4, 512), np.float32)
    i = np.arange(128)[:, None]
    j = np.arange(512)[None, :]
    for a in range(4):
        m[:, a, :] = ((a * 128 + i) < j).astype(np.float32)
    jj = np.arange(128)[:, None]
    ss = np.arange(128)[None, :]
    negU = -(jj >= ss).astype(np.float32)
    return m.astype(NPBF), negU.astype(NPBF)


def build_D():
    nc = new_nc()
    d = {}
    d["qk"] = nc.dram_tensor("qk", [2, 2, 128, S], BF16, kind="ExternalInput").ap()
    d["v"] = nc.dram_tensor("v", [S, 256], BF16, kind="ExternalInput").ap()
    d["masks"] = nc.dram_tensor("masks", [128, 4, 512], BF16, kind="ExternalInput").ap()
    d["negU"] = nc.dram_tensor("negU", [128, 128], BF16, kind="ExternalInput").ap()
    d["oT"] = nc.dram_tensor("oT", [256, S], BF16, kind="ExternalOutput").ap()
    d["q"] = [d["qk"][0, p] for p in range(2)]
    d["k"] = [d["qk"][1, p] for p in range(2)]
    d["o"] = [d["oT"][p * 128:(p + 1) * 128, :] for p in range(2)]
    block_sb(nc, "D_sb", d)
    return nc


def run_D(resC):
    nc = build_D()
    masks, negU = sb_consts()
    maps = []
    for c in range(NCORE):
        b, g = c // 4, c % 4
        rc = [resC[b * 4 + i] for i in range(4)]
        qk = np.concatenate([np.asarray(r["qk_fm"]) for r in rc], axis=1)
        vt = np.concatenate([np.asarray(r["v_tm"]) for r in rc], axis=0)
        q = qk[0:1024][g * 256:(g + 1) * 256].reshape(2, 128, S)
        k = qk[1024:2048][g * 256:(g + 1) * 256].reshape(2, 128, S)
        maps.append({"qk": np.ascontiguousarray(np.stack([q, k])), "v": np.ascontiguousarray(vt[:, g * 256:(g + 1) * 256]),
                     "masks": masks, "negU": negU})
    res = run_bass_kernel_spmd(nc, maps, core_ids=list(range(NCORE)))
    return res.results


def mix_from_D(resD):
    out = []
    for c in range(NCORE):
        b, ts = c // 4, c % 4
        tsl = slice(ts * NT, (ts + 1) * NT)
        out.append(np.ascontiguousarray(np.concatenate([np.asarray(resD[b * 4 + g]["oT"])[:, tsl] for g in range(4)], axis=0)))
    return out


def build_C():
    nc = new_nc()
    x_in = nc.dram_tensor("xT_in", [D, NT], F32, kind="ExternalInput").ap()
    gains = nc.dram_tensor("gains", [128, 56], F32, kind="ExternalInput").ap()
    mix = nc.dram_tensor("mix", [D, NT], BF16, kind="ExternalInput").ap()
    w_ho = nc.dram_tensor("w_ho", [D, D], F32, kind="ExternalInput").ap()
    w_in2 = nc.dram_tensor("w_in2", [D, 2 * FF], F32, kind="ExternalInput").ap()
    w_out2 = nc.dram_tensor("w_out2", [FF, D], F32, kind="ExternalInput").ap()
    w_in1 = nc.dram_tensor("w_in1", [D, 2 * FF], F32, kind="ExternalInput").ap()
    w_out1 = nc.dram_tensor("w_out1", [FF, D], F32, kind="ExternalInput").ap()
    w_sb = nc.dram_tensor("w_sb", [D, 3 * D], F32, kind="ExternalInput").ap()
    x_out = nc.dram_tensor("xT_out", [D, NT], F32, kind="ExternalOutput").ap()
    qk_fm = nc.dram_tensor("qk_fm", [2048, NT], BF16, kind="ExternalOutput").ap()
    v_tm = nc.dram_tensor("v_tm", [NT, 1024], BF16, kind="ExternalOutput").ap()
    with contextlib.ExitStack() as st:
        cx = setup_ctx(nc, st, "C")
        block_load_x(nc, cx, "C_ld", x_in, gains)
        block_outproj(nc, cx, "C_op", mix, w_ho)
        block_ffn(nc, cx, "C_f2", G_FFN2[0], w_in2, w_out2)
        block_ffn(nc, cx, "C_f1", G_FFN1[1], w_in1, w_out1)
        fm = []
        for i in range(8):
            fm.append((w_sb[:, i * 128:(i + 1) * 128], None, None, 0.125, qk_fm[i * 128:(i + 1) * 128, :]))
        for i in range(8):
            fm.append((w_sb[:, 1024 + i * 128:1024 + (i + 1) * 128], None, None, 1.0, qk_fm[1024 + i * 128:1024 + (i + 1) * 128, :]))
        tm = [(w_sb[:, 2048:2560], AF.Copy, v_tm[:, 0:512], "bf16"), (w_sb[:, 2560:3072], AF.Copy, v_tm[:, 512:1024], "bf16")]
        block_proj(nc, cx, "C_pj", G_MIX[1], fm, tm, store_x=x_out)
    return nc


def run_C(inp, xT_shards, mix_shards):
    nc = build_C()
    g = gains_table(inp)
    maps = []
    for c in range(NCORE):
        maps.append({"xT_in": xT_shards[c], "gains": g, "mix": mix_shards[c],
                     "w_ho": np.asarray(inp["hyb_w_out"][0], np.float32),
                     "w_in2": np.asarray(inp["ffn2_w_in"][0], np.float32), "w_out2": np.asarray(inp["ffn2_w_out"][0], np.float32),
                     "w_in1": np.asarray(inp["ffn1_w_in"][1], np.float32), "w_out1": np.asarray(inp["ffn1_w_out"][1], np.float32),
                     "w_sb": np.asarray(inp["sb_w_in"][0], np.float32)})
    res = run_bass_kernel_spmd(nc, maps, core_ids=list(range(NCORE)))
    return res.results


def build_E():
    nc = new_nc()
    x_in = nc.dram_tensor("xT_in", [D, NT], F32, kind="ExternalInput").ap()
    gains = nc.dram_tensor("gains", [128, 56], F32, kind="ExternalInput").ap()
    mix = nc.dram_tensor("mix", [D, NT], BF16, kind="ExternalInput").ap()
    w_so = nc.dram_tensor("w_so", [D, D], F32, kind="ExternalInput").ap()
    w_in2 = nc.dram_tensor("w_in2", [D, 2 * FF], F32, kind="ExternalInput").ap()
    w_out2 = nc.dram_tensor("w_out2", [FF, D], F32, kind="ExternalInput").ap()
    y_out = nc.dram_tensor("yT_out", [D, NT], F32, kind="ExternalOutput").ap()
    with contextlib.ExitStack() as st:
        cx = setup_ctx(nc, st, "E")
        block_load_x(nc, cx, "E_ld", x_in, gains)
        block_outproj(nc, cx, "E_op", mix, w_so)
        block_ffn(nc, cx, "E_f2", G_FFN2[1], w_in2, w_out2)
        block_final(nc, cx, "E_fin", G_FINAL, y_out)
    return nc


def run_E(inp, xT_shards, mix_shards):
    nc = build_E()
    g = gains_table(inp)
    maps = []
    for c in range(NCORE):
        maps.append({"xT_in": xT_shards[c], "gains": g, "mix": mix_shards[c],
                     "w_so": np.asarray(inp["sb_w_out"][0], np.float32),
                     "w_in2": np.asarray(inp["ffn2_w_in"][1], np.float32), "w_out2": np.asarray(inp["ffn2_w_out"][1], np.float32)})
    res = run_bass_kernel_spmd(nc, maps, core_ids=list(range(NCORE)))
    return res.results


def kernel(**inp):
    inp = {k: np.asarray(v) for k, v in inp.items()}
    xs = shard_x(inp["x"])
    resA = run_A(inp, xs)
    resB = run_B(inp, resA)
    xs1 = [np.asarray(r["xT_out"]) for r in resA]
    resC = run_C(inp, xs1, mix_from_B(resB))
    resD = run_D(resC)
    xs2 = [np.asarray(r["xT_out"]) for r in resC]
    resE = run_E(inp, xs2, mix_from_D(resD))
    y = np.concatenate([np.asarray(r["yT_out"], np.float32).T for r in resE], axis=0)
    return np.ascontiguousarray(y.reshape(B, S, D).astype(np.float32))


def build_fused():
    nc = new_nc()
    dt_ = nc.dram_tensor
    x_in = dt_("xT_in", [D, S], F32, kind="ExternalInput").ap()
    gains = dt_("gains", [128, 56], F32, kind="ExternalInput").ap()
    w_in1 = [dt_(f"w_in1_{l}", [D, 2 * FF], F32, kind="ExternalInput").ap() for l in range(2)]
    w_out1 = [dt_(f"w_out1_{l}", [FF, D], F32, kind="ExternalInput").ap() for l in range(2)]
    w_in2 = [dt_(f"w_in2_{l}", [D, 2 * FF], F32, kind="ExternalInput").ap() for l in range(2)]
    w_out2 = [dt_(f"w_out2_{l}", [FF, D], F32, kind="ExternalInput").ap() for l in range(2)]
    w_hyb = dt_("w_hyb", [D, HYB_IN], F32, kind="ExternalInput").ap()
    w_hsw = dt_("w_hsw", [D, 2048], F32, kind="ExternalInput").ap()
    w_ho = dt_("w_ho", [D, D], F32, kind="ExternalInput").ap()
    w_sb = dt_("w_sb", [D, 3 * D], F32, kind="ExternalInput").ap()
    w_so = dt_("w_so", [D, D], F32, kind="ExternalInput").ap()
    tabs_d = [dt_(n, [4, 128, NT], F32, kind="ExternalInput").ap() for n in ("rc", "rs", "dc", "ds")]
    decT = dt_("decT", [4, 128, 256], F32, kind="ExternalInput").ap()
    qdec = dt_("qdec", [4, 128, 512], F32, kind="ExternalInput").ap()
    kdec = dt_("kdec", [4, 128, 2], F32, kind="ExternalInput").ap()
    cdec = dt_("cdec", [4, 128, 1], F32, kind="ExternalInput").ap()
    gn = dt_("gn", [4, 128, 128], F32, kind="ExternalInput").ap()
    ident = dt_("ident", [128, 128], BF16, kind="ExternalInput").ap()
    dmask = dt_("dmask", [128, 512], BF16, kind="ExternalInput").ap()
    masks = dt_("masks", [128, 4, 512], BF16, kind="ExternalInput").ap()
    negU = dt_("negU", [128, 128], BF16, kind="ExternalInput").ap()
    y_out = dt_("yT_out", [D, S], F32, kind="ExternalOutput").ap()
    x_scr = dt_("x_scr", [D, S], F32, kind="Internal").ap()
    qk_scr = dt_("qk_scr", [2048, S], BF16, kind="Internal").ap()
    v_scr = dt_("v_scr", [S, 1024], BF16, kind="Internal").ap()
    g_scr = dt_("g_scr", [S, 512], F32, kind="Internal").ap()
    mix_scr = dt_("mix_scr", [D, S], BF16, kind="Internal").ap()
    NQ = DBG.get("fused_quarters", 4)

    def qs(q):
        return slice(q * NT, (q + 1) * NT)

    PH = DBG.get('fused_phases', 'ABCDE')
    NG = DBG.get('fused_groups', 4)
    for q in range(NQ if 'A' in PH else 0):
        with contextlib.ExitStack() as st:
            cx = setup_ctx(nc, st, f"A{q}")
            block_load_x(nc, cx, f"A{q}_ld", x_in[:, qs(q)], gains)
            block_ffn(nc, cx, f"A{q}_ffn", G_FFN1[0], w_in1[0], w_out1[0])
            block_store_x(nc, cx, f"A{q}_st", x_scr[:, qs(q)])
            fm = []
            for i in range(4):
                fm.append((w_hyb[:, i * 128:(i + 1) * 128], w_hsw[:, i * 128:(i + 1) * 128], 0, 1.0, qk_scr[i * 128:(i + 1) * 128, qs(q)]))
            for i in range(4):
                fm.append((w_hyb[:, 512 + i * 128:512 + (i + 1) * 128], w_hsw[:, 512 + i * 128:512 + (i + 1) * 128], 0, 1.0,
                           qk_scr[512 + i * 128:512 + (i + 1) * 128, qs(q)]))
            for i in range(4):
                fm.append((w_hyb[:, 2048 + i * 128:2048 + (i + 1) * 128], w_hsw[:, 1024 + i * 128:1024 + (i + 1) * 128], 1, 1.0,
                           qk_scr[1024 + i * 128:1024 + (i + 1) * 128, qs(q)]))
            for i in range(4):
                fm.append((w_hyb[:, 2560 + i * 128:2560 + (i + 1) * 128], w_hsw[:, 1536 + i * 128:1536 + (i + 1) * 128], 1, 1.0,
                           qk_scr[1536 + i * 128:1536 + (i + 1) * 128, qs(q)]))
            tm = [
                (w_hyb[:, 1024:1536], AF.Copy, v_scr[qs(q), 0:512], "bf16"),
                (w_hyb[:, 3072:3584], AF.Copy, v_scr[qs(q), 512:1024], "bf16"),
                (w_hyb[:, 1536:2048], AF.Silu, g_scr[qs(q), :], "f32"),
            ]
            block_proj(nc, cx, f"A{q}_pj", G_MIX[0], fm, tm, tabs=[(tabs_d[0][q], tabs_d[1][q]), (tabs_d[2][q], tabs_d[3][q])])
    for g in range(NG if 'B' in PH else 0):
        with contextlib.ExitStack() as st:
            ones, eps = block_consts(nc, st, f"B{g}")
            d = {"rq": qk_scr[g * 128:(g + 1) * 128, :], "rk": qk_scr[512 + g * 128:512 + (g + 1) * 128, :],
                 "dq": qk_scr[1024 + g * 128:1024 + (g + 1) * 128, :], "dk": qk_scr[1536 + g * 128:1536 + (g + 1) * 128, :],
                 "rv": v_scr[:, g * 128:(g + 1) * 128], "dv": v_scr[:, 512 + g * 128:512 + (g + 1) * 128], "rg": g_scr[:, g * 128:(g + 1) * 128],
                 "decT": decT[g], "qdec": qdec[g], "kdec": kdec[g], "cdec": cdec[g], "gn": gn[g], "ident": ident, "dmask": dmask,
                 "ret_out": mix_scr[g * 128:(g + 1) * 128, :], "dil_out": mix_scr[512 + g * 128:512 + (g + 1) * 128, :]}
            block_ret(nc, f"B{g}_ret", d)
            block_dil(nc, f"B{g}_dil", d, ones)
    for q in range(NQ if 'C' in PH else 0):
        with contextlib.ExitStack() as st:
            cx = setup_ctx(nc, st, f"C{q}")
            block_load_x(nc, cx, f"C{q}_ld", x_scr[:, qs(q)], gains)
            block_outproj(nc, cx, f"C{q}_op", mix_scr[:, qs(q)], w_ho)
            block_ffn(nc, cx, f"C{q}_f2", G_FFN2[0], w_in2[0], w_out2[0])
            block_ffn(nc, cx, f"C{q}_f1", G_FFN1[1], w_in1[1], w_out1[1])
            block_store_x(nc, cx, f"C{q}_st", x_scr[:, qs(q)])
            fm = []
            for i in range(8):
                fm.append((w_sb[:, i * 128:(i + 1) * 128], None, None, 0.125, qk_scr[i * 128:(i + 1) * 128, qs(q)]))
            for i in range(8):
                fm.append((w_sb[:, 1024 + i * 128:1024 + (i + 1) * 128], None, None, 1.0, qk_scr[1024 + i * 128:1024 + (i + 1) * 128, qs(q)]))
            tm = [(w_sb[:, 2048:2560], AF.Copy, v_scr[qs(q), 0:512], "bf16"), (w_sb[:, 2560:3072], AF.Copy, v_scr[qs(q), 512:1024], "bf16")]
            block_proj(nc, cx, f"C{q}_pj", G_MIX[1], fm, tm)
    for g in range(NG if 'D' in PH else 0):
        d = {"q": [qk_scr[g * 256 + p * 128:g * 256 + (p + 1) * 128, :] for p in range(2)],
             "k": [qk_scr[1024 + g * 256 + p * 128:1024 + g * 256 + (p + 1) * 128, :] for p in range(2)],
             "v": v_scr[:, g * 256:(g + 1) * 256], "masks": masks, "negU": negU,
             "o": [mix_scr[g * 256 + p * 128:g * 256 + (p + 1) * 128, :] for p in range(2)]}
        block_sb(nc, f"D{g}_sb", d)
    for q in range(NQ if 'E' in PH else 0):
        with contextlib.ExitStack() as st:
            cx = setup_ctx(nc, st, f"E{q}")
            block_load_x(nc, cx, f"E{q}_ld", x_scr[:, qs(q)], gains)
            block_outproj(nc, cx, f"E{q}_op", mix_scr[:, qs(q)], w_so)
            block_ffn(nc, cx, f"E{q}_f2", G_FFN2[1], w_in2[1], w_out2[1])
            block_final(nc, cx, f"E{q}_fin", G_FINAL, y_out[:, qs(q)])
    return nc


def fused_inputs(inp):
    g = gains_table(inp)
    w_hyb = np.asarray(inp["hyb_w_in"][0], np.float32)
    w_hsw = np.ascontiguousarray(np.concatenate([swap_cols(w_hyb[:, 0:512], 64), swap_cols(w_hyb[:, 512:1024], 64),
                                                 swap_cols(w_hyb[:, 2048:2560], 16), swap_cols(w_hyb[:, 2560:3072], 16)], axis=1))
    tabs = {k: [] for k in ("rc", "rs", "dc", "ds")}
    for q in range(4):
        pos = np.arange(NT) + q * NT
        rc, rs = rope_tables(64, 10000.0, pos)
        dc, ds = rope_tables(16, 500000.0, pos)
        for k, v_ in zip(("rc", "rs", "dc", "ds"), (rc, rs, dc, ds)):
            tabs[k].append(v_)
    tabs = {k: np.ascontiguousarray(np.stack(v_)) for k, v_ in tabs.items()}
    rcs = [ret_consts(gg) for gg in range(4)]
    gnv = np.asarray(inp["ret_gn"][0], np.float32)
    masks, negU = sb_consts()
    common = {"gains": g, "w_hyb": w_hyb, "w_hsw": w_hsw,
              "w_ho": np.asarray(inp["hyb_w_out"][0], np.float32), "w_sb": np.asarray(inp["sb_w_in"][0], np.float32),
              "w_so": np.asarray(inp["sb_w_out"][0], np.float32),
              "decT": np.ascontiguousarray(np.stack([r[0] for r in rcs])), "qdec": np.ascontiguousarray(np.stack([r[1] for r in rcs])),
              "kdec": np.ascontiguousarray(np.stack([r[2] for r in rcs])), "cdec": np.ascontiguousarray(np.stack([r[3] for r in rcs])),
              "gn": np.ascontiguousarray(np.stack([np.broadcast_to(gnv[gg * 128:(gg + 1) * 128][None, :], (128, 128)) for gg in range(4)])),
              "ident": np.eye(128, dtype=np.float32).astype(NPBF), "dmask": dil_mask(), "masks": masks, "negU": negU}
    common.update(tabs)
    for l in range(2):
        common[f"w_in1_{l}"] = np.asarray(inp["ffn1_w_in"][l], np.float32)
        common[f"w_out1_{l}"] = np.asarray(inp["ffn1_w_out"][l], np.float32)
        common[f"w_in2_{l}"] = np.asarray(inp["ffn2_w_in"][l], np.float32)
        common[f"w_out2_{l}"] = np.asarray(inp["ffn2_w_out"][l], np.float32)
    x = np.asarray(inp["x"], np.float32)
    maps = []
    for c in range(NCORE):
        m = dict(common)
        m["xT_in"] = np.ascontiguousarray(x[c // 4].T)
        maps.append(m)
    return maps


def kernel_fused_replicated(**inp):
    inp = {k: np.asarray(v) for k, v in inp.items()}
    nc = build_fused()
    maps = fused_inputs(inp)
    res = run_bass_kernel_spmd(nc, maps, core_ids=list(range(NCORE))).results
    y = np.stack([np.asarray(res[0]["yT_out"], np.float32).T, np.asarray(res[4]["yT_out"], np.float32).T], axis=0)
    return np.ascontiguousarray(y.astype(np.float32))
```
